# Optimizing a Trainium2 kernel written in Bass

```python
import math
import jax, jax.numpy as jnp
from jax import lax
import numpy as np

D_MODEL = 1024
BATCH = 8
SEQ = 2048
DEPTH = 4
DEC_BATCH = 128
DEC_SEQ = 4
PAST_LEN = 8192
PAGE_SIZE = 128

N_META = 16
HEAD_DIM = 64
N_Q_HEADS = 8
N_KV_HEADS = 2
Q_PER_KV = N_Q_HEADS // N_KV_HEADS
ATTN_WIDTH = N_Q_HEADS * HEAD_DIM
KV_WIDTH = N_KV_HEADS * HEAD_DIM
WINDOW = 128
ATTN_BLOCK = 128
ROPE_THETA = 10000.0
D_RNN = D_MODEL
N_LRU_BLOCKS = 8
LRU_BLOCK = D_RNN // N_LRU_BLOCKS
CONV_W = 4
LRU_C = 8.0
D_FF = -(-8 * D_MODEL // (3 * 256)) * 256
LN_EPS = 1e-5
NEG_INF = -1e30
DEEPNORM_ALPHA = (2 * DEPTH) ** 0.25
DEEPNORM_BETA = (8 * DEPTH) ** -0.25

Q_END = ATTN_WIDTH
K_END = Q_END + KV_WIDTH
V_END = K_END + KV_WIDTH
XR_END = V_END + D_RNN
GATE_END = XR_END + D_RNN
GA_END = GATE_END + D_MODEL
D_IN = GA_END + D_MODEL

kernel_name = 'griffin_swa_sink_rglru_deepnorm_meta_step'


def layer_norm(x, g, b):
    x32 = x.astype(jnp.float32)
    mu = x32.mean(-1, keepdims=True)
    var = jnp.square(x32 - mu).mean(-1, keepdims=True)
    y = (x32 - mu) * lax.rsqrt(var + LN_EPS) * g.astype(jnp.float32) + b.astype(jnp.float32)
    return y.astype(x.dtype)


def rope(x, pos):
    half = HEAD_DIM // 2
    inv = ROPE_THETA ** (-jnp.arange(half, dtype=jnp.float32) / half)
    ang = pos.astype(jnp.float32)[:, None] * inv[None, :]
    cos = jnp.cos(ang)[None, :, None, :]
    sin = jnp.sin(ang)[None, :, None, :]
    x32 = x.astype(jnp.float32)
    x1, x2 = x32[..., :half], x32[..., half:]
    return jnp.concatenate([x1 * cos - x2 * sin, x2 * cos + x1 * sin], axis=-1).astype(x.dtype)


def sink_attention(q, k, v, valid, sinks):
    s = jnp.einsum('bnqkgd,bnskd->bnkgqs', q, k, preferred_element_type=jnp.float32) * (HEAD_DIM ** -0.5)
    s = jnp.where(valid[None, :, None, None], s, NEG_INF)
    sink = sinks.astype(jnp.float32).reshape(N_KV_HEADS, Q_PER_KV)[None, None, :, :, None, None]
    m = jnp.maximum(s.max(-1, keepdims=True), sink)
    p = jnp.exp(s - m)
    denom = p.sum(-1, keepdims=True) + jnp.exp(sink - m)
    return jnp.einsum('bnkgqs,bnskd->bnqkgd', (p / denom).astype(v.dtype), v)


def window_mask(qpos, kpos):
    qp = qpos[..., :, None]
    kp = kpos[..., None, :]
    return (kp <= qp) & (kp > qp - WINDOW) & (kp >= 0)


def attn_prompt(q, k, v, sinks):
    B, L = q.shape[:2]
    pad = (-L) % ATTN_BLOCK
    Lp = L + pad
    nb = Lp // ATTN_BLOCK

    def padf(t):
        return jnp.pad(t, ((0, 0), (pad, 0)) + ((0, 0),) * (t.ndim - 2))

    def with_prev(t):
        prev = jnp.concatenate([jnp.zeros_like(t[:, :1]), t[:, :-1]], axis=1)
        return jnp.concatenate([prev, t], axis=2)

    qb = padf(q).reshape(B, nb, ATTN_BLOCK, N_KV_HEADS, Q_PER_KV, HEAD_DIM)
    kb = with_prev(padf(k).reshape(B, nb, ATTN_BLOCK, N_KV_HEADS, HEAD_DIM))
    vb = with_prev(padf(v).reshape(B, nb, ATTN_BLOCK, N_KV_HEADS, HEAD_DIM))
    qpos = (jnp.arange(Lp, dtype=jnp.int32) - pad).reshape(nb, ATTN_BLOCK)
    kpos = jnp.concatenate([qpos - ATTN_BLOCK, qpos], axis=1)
    o = sink_attention(qb, kb, vb, window_mask(qpos, kpos), sinks)
    return o.reshape(B, Lp, ATTN_WIDTH)[:, pad:]


def attn_sample(q, k_new, v_new, k_buf, v_buf, sinks):
    DB, S = q.shape[:2]
    W = k_buf.shape[1]
    kk = jnp.concatenate([k_buf.astype(k_new.dtype), k_new], axis=1)
    vv = jnp.concatenate([v_buf.astype(v_new.dtype), v_new], axis=1)
    qpos = PAST_LEN + jnp.arange(S, dtype=jnp.int32)
    kpos = jnp.concatenate([PAST_LEN - W + jnp.arange(W, dtype=jnp.int32), qpos])
    valid = window_mask(qpos, kpos)[None]
    o = sink_attention(q.reshape(DB, 1, S, N_KV_HEADS, Q_PER_KV, HEAD_DIM), kk[:, None], vv[:, None], valid, sinks)
    return o.reshape(DB, S, ATTN_WIDTH), kk[:, S:], vv[:, S:]


def causal_conv(xr, buf, w, b):
    T = xr.shape[1]
    xe = jnp.concatenate([buf.astype(xr.dtype), xr], axis=1)
    y = sum(xe[:, j:j + T] * w[j] for j in range(CONV_W)) + b
    return y, xe[:, -(CONV_W - 1):]


def _lin_combine(c1, c2):
    a1, b1 = c1
    a2, b2 = c2
    return a1 * a2, a2 * b1 + b2


def rg_lru(xc, h0, wa, ba, wx, bx, lam):
    B, T, _ = xc.shape
    x32 = xc.astype(jnp.float32)
    xb = x32.reshape(B, T, N_LRU_BLOCKS, LRU_BLOCK)
    r = jax.nn.sigmoid(jnp.einsum('btnc,ncd->btnd', xb, wa.astype(jnp.float32)).reshape(B, T, D_RNN) + ba.astype(jnp.float32))
    i = jax.nn.sigmoid(jnp.einsum('btnc,ncd->btnd', xb, wx.astype(jnp.float32)).reshape(B, T, D_RNN) + bx.astype(jnp.float32))
    log_a = -LRU_C * r * jax.nn.softplus(-lam.astype(jnp.float32))
    a = jnp.exp(log_a)
    b = jnp.sqrt(-jnp.expm1(2.0 * log_a)) * (i * x32)
    b = b.at[:, 0].add(a[:, 0] * h0.astype(jnp.float32))
    _, h = lax.associative_scan(_lin_combine, (a, b), axis=1)
    return h, h[:, -1]


def mixer_block(x, pos, l, p, k_buf, v_buf, conv_buf, h0):
    B, T, _ = x.shape
    proj = x @ p['w_in'][l]
    q = rope(proj[..., :Q_END].reshape(B, T, N_Q_HEADS, HEAD_DIM), pos)
    k = rope(proj[..., Q_END:K_END].reshape(B, T, N_KV_HEADS, HEAD_DIM), pos)
    v = proj[..., K_END:V_END].reshape(B, T, N_KV_HEADS, HEAD_DIM)
    xr = proj[..., V_END:XR_END]
    gate = proj[..., XR_END:GATE_END]
    g_attn = proj[..., GATE_END:GA_END]
    g_lru = proj[..., GA_END:]
    sinks = p['attn_sinks'][l]
    if k_buf is None:
        attn = attn_prompt(q, k, v, sinks)
        new_k, new_v = k[:, -WINDOW:], v[:, -WINDOW:]
    else:
        attn, new_k, new_v = attn_sample(q, k, v, k_buf, v_buf, sinks)
    xc, new_conv = causal_conv(xr, conv_buf, p['conv_w'][l], p['conv_b'][l])
    h, h_last = rg_lru(xc, h0, p['lru_wa'][l], p['lru_ba'][l], p['lru_wx'][l], p['lru_bx'][l], p['lru_lambda'][l])
    rec = h.astype(x.dtype) * jax.nn.gelu(gate)
    merged = (jax.nn.sigmoid(g_attn) * (attn @ p['w_attn_proj'][l])
              + jax.nn.sigmoid(g_lru) * (rec @ p['w_lru_proj'][l]))
    return merged @ p['w_out'][l], (new_k, new_v, new_conv, h_last.astype(x.dtype))


def swiglu(x, w_in, w_out):
    u = x @ w_in
    return (jax.nn.silu(u[..., :D_FF]) * u[..., D_FF:]) @ w_out


def trunk(x, pos, p, cache_k, cache_v, conv_state, lru_state):
    B = x.shape[0]
    outs = ([], [], [], [])
    for l in range(DEPTH):
        if cache_k is None:
            kb = vb = None
            cb = jnp.zeros((B, CONV_W - 1, D_RNN), x.dtype)
            h0 = jnp.zeros((B, D_RNN), x.dtype)
        else:
            kb, vb, cb, h0 = cache_k[l], cache_v[l], conv_state[l], lru_state[l]
        mix, st = mixer_block(x, pos, l, p, kb, vb, cb, h0)
        x = layer_norm(DEEPNORM_ALPHA * x + mix, p['ln1_g'][l], p['ln1_b'][l])
        x = layer_norm(DEEPNORM_ALPHA * x + swiglu(x, p['w_ffn_in'][l], p['w_ffn_out'][l]), p['ln2_g'][l], p['ln2_b'][l])
        for o, s in zip(outs, st):
            o.append(s)
    return x, [jnp.stack(o) for o in outs]


def setup_inputs(seed: int = 0) -> dict:
    key = jax.random.key(seed)
    ks = jax.random.split(key, 32)
    f32 = jnp.float32

    def nrm(k, shape, scale=1.0):
        return jax.random.normal(k, shape, f32) * scale

    a_c = jax.random.uniform(ks[14], (DEPTH, D_RNN), f32, 0.9, 0.999)
    a_base = a_c ** (1.0 / LRU_C)
    lru_lambda = jnp.log(a_base) - jnp.log1p(-a_base)
    return {
        'x_prompt': nrm(ks[0], (BATCH, SEQ, D_MODEL)),
        'x_sample': nrm(ks[1], (DEC_BATCH, DEC_SEQ, D_MODEL)),
        'cache_win_k': nrm(ks[2], (DEPTH, DEC_BATCH, WINDOW, N_KV_HEADS, HEAD_DIM)),
        'cache_win_v': nrm(ks[3], (DEPTH, DEC_BATCH, WINDOW, N_KV_HEADS, HEAD_DIM)),
        'state_conv': nrm(ks[4], (DEPTH, DEC_BATCH, CONV_W - 1, D_RNN)),
        'state_lru': nrm(ks[5], (DEPTH, DEC_BATCH, D_RNN), 0.5),
        'meta_tokens': nrm(ks[6], (N_META, D_MODEL)),
        'w_in': nrm(ks[7], (DEPTH, D_MODEL, D_IN), D_MODEL ** -0.5),
        'w_attn_proj': nrm(ks[8], (DEPTH, ATTN_WIDTH, D_MODEL), ATTN_WIDTH ** -0.5),
        'w_lru_proj': nrm(ks[9], (DEPTH, D_RNN, D_MODEL), D_RNN ** -0.5),
        'w_out': nrm(ks[10], (DEPTH, D_MODEL, D_MODEL), DEEPNORM_BETA * D_MODEL ** -0.5),
        'attn_sinks': nrm(ks[11], (DEPTH, N_Q_HEADS), 0.5),
        'conv_w': nrm(ks[12], (DEPTH, CONV_W, D_RNN), CONV_W ** -0.5),
        'conv_b': nrm(ks[13], (DEPTH, D_RNN), 0.01),
        'lru_wa': nrm(ks[15], (DEPTH, N_LRU_BLOCKS, LRU_BLOCK, LRU_BLOCK), LRU_BLOCK ** -0.5),
        'lru_ba': nrm(ks[16], (DEPTH, D_RNN), 0.01),
        'lru_wx': nrm(ks[17], (DEPTH, N_LRU_BLOCKS, LRU_BLOCK, LRU_BLOCK), LRU_BLOCK ** -0.5),
        'lru_bx': nrm(ks[18], (DEPTH, D_RNN), 0.01),
        'lru_lambda': lru_lambda,
        'ln1_g': 1.0 + nrm(ks[19], (DEPTH, D_MODEL), 0.01),
        'ln1_b': nrm(ks[20], (DEPTH, D_MODEL), 0.01),
        'w_ffn_in': nrm(ks[21], (DEPTH, D_MODEL, 2 * D_FF), D_MODEL ** -0.5),
        'w_ffn_out': nrm(ks[22], (DEPTH, D_FF, D_MODEL), DEEPNORM_BETA * D_FF ** -0.5),
        'ln2_g': 1.0 + nrm(ks[23], (DEPTH, D_MODEL), 0.01),
        'ln2_b': nrm(ks[24], (DEPTH, D_MODEL), 0.01),
    }


def reference(x_prompt, x_sample, cache_win_k, cache_win_v, state_conv, state_lru,
              meta_tokens, w_in, w_attn_proj, w_lru_proj, w_out, attn_sinks,
              conv_w, conv_b, lru_wa, lru_ba, lru_wx, lru_bx, lru_lambda,
              ln1_g, ln1_b, w_ffn_in, w_ffn_out, ln2_g, ln2_b):
    p = dict(w_in=w_in, w_attn_proj=w_attn_proj, w_lru_proj=w_lru_proj, w_out=w_out,
             attn_sinks=attn_sinks, conv_w=conv_w, conv_b=conv_b, lru_wa=lru_wa, lru_ba=lru_ba,
             lru_wx=lru_wx, lru_bx=lru_bx, lru_lambda=lru_lambda, ln1_g=ln1_g, ln1_b=ln1_b,
             w_ffn_in=w_ffn_in, w_ffn_out=w_ffn_out, ln2_g=ln2_g, ln2_b=ln2_b)
    B, T, D = x_prompt.shape
    meta = jnp.broadcast_to(meta_tokens.astype(x_prompt.dtype)[None], (B, N_META, D))
    xp = jnp.concatenate([meta, x_prompt], axis=1)
    pos_p = jnp.arange(T + N_META, dtype=jnp.int32)
    yp, st_p = trunk(xp, pos_p, p, None, None, None, None)
    pos_s = PAST_LEN + jnp.arange(x_sample.shape[1], dtype=jnp.int32)
    ys, st_s = trunk(x_sample, pos_s, p, cache_win_k, cache_win_v, state_conv, state_lru)
    return (yp[:, N_META:], ys, st_p[0], st_p[1], st_p[2], st_p[3], st_s[0], st_s[1], st_s[2], st_s[3])
```

```python
import numpy as np
import concourse.bass as bass
import concourse.mybir as mybir
from concourse.bass_utils import run_bass_kernel_spmd

F32 = mybir.dt.float32
BF16 = mybir.dt.bfloat16
AF = mybir.ActivationFunctionType
ALU = mybir.AluOpType

NL = 4
D = 1024
NSLOT = 5
NU = 35
ALPHA = float((2 * NL) ** 0.25)
LN_EPS = 1e-5
PAST = 8192

U_Q, U_QR, U_KV, U_LRU = 0, 1, 2, 3
U_XG = [4, 5, 6, 7]
U_GA = [8, 12]
U_AP = [9, 13]
U_GL = [10, 14]
U_LP = [11, 15]
U_OUT = [16, 17]
U_F1 = list(range(18, 29))
U_F2 = [[29, 30, 31], [32, 33, 34]]

HALVES = [
    dict(NC=1104, tts=[(0, 80), (80, 592), (592, 1104)], blocks=[80 + 128 * j for j in range(8)], samp=True, seq0=64),
    dict(NC=1024, tts=[(0, 512), (512, 1024)], blocks=[128 * j for j in range(8)], samp=False, seq0=0),
]
NCMAX = 1104


class Sched:
    def __init__(self, nc):
        self.nc = nc
        self.E = {'pe': nc.tensor, 'act': nc.scalar, 'dve': nc.vector, 'pool': nc.gpsimd, 'sp': nc.sync}
        self.comp = ('pe', 'act', 'dve', 'pool')
        self.csem = {e: nc.alloc_semaphore('c_' + e) for e in self.comp}
        self.ccnt = {e: 0 for e in self.comp}
        self.dsem = {}
        self.dcnt = {}
        self.waited = {}
        self.lastw = {}
        self.readers = {}

    def _wait(self, eng, tok):
        name, sem, val, src = tok
        if src == eng and (eng == 'pe' or SELFWAIT[0] == 0 or (SELFWAIT[0] == 2 and eng == 'act')
                           or (SELFWAIT[0] == 3 and eng == 'dve')):
            return
        key = (eng, name)
        if self.waited.get(key, 0) >= val:
            return
        self.waited[key] = val
        self.E[eng].wait_ge(sem, val)

    def _deps(self, eng, reads, writes):
        for k in list(reads) + list(writes):
            t = self.lastw.get(k)
            if t is not None:
                if isinstance(t, list):
                    for t_ in t:
                        self._wait(eng, t_)
                else:
                    self._wait(eng, t)
        for k in writes:
            for t in self.readers.get(k, {}).values():
                self._wait(eng, t)

    def _toks(self, k):
        out = []
        t = self.lastw.get(k)
        if t is not None:
            out.extend(t if isinstance(t, list) else [t])
        out.extend(self.readers.get(k, {}).values())
        return out

    def transfer(self, old_keys, new_keys):
        best = {}
        for k in list(old_keys) + list(new_keys):
            for t in self._toks(k):
                if t[0] not in best or best[t[0]][2] < t[2]:
                    best[t[0]] = t
        for nk in new_keys:
            self.lastw[nk] = list(best.values())
            self.readers[nk] = {}

    def _commit(self, tok, reads, writes):
        for k in writes:
            self.lastw[k] = tok
            self.readers[k] = {}
        for k in reads:
            self.readers.setdefault(k, {})[tok[0]] = tok

    @staticmethod
    def _excl(reads, writes):
        r = [k for k in reads if not (isinstance(k, tuple) and k[0] == 'ps')]
        w = list(writes) + [k for k in reads if isinstance(k, tuple) and k[0] == 'ps']
        return r, w

    def op(self, eng, fn, reads=(), writes=()):
        reads, writes = self._excl(reads, writes)
        self._deps(eng, reads, writes)
        ins = fn(self.E[eng])
        self.ccnt[eng] += 1
        ins.then_inc(self.csem[eng], 1)
        tok = ('c_' + eng, self.csem[eng], self.ccnt[eng], eng)
        self._commit(tok, reads, writes)

    def dma(self, q, out, in_, semkey, reads=(), writes=()):
        self._deps(q, reads, writes)
        if semkey not in self.dsem:
            self.dsem[semkey] = self.nc.alloc_semaphore('d_' + semkey)
            self.dcnt[semkey] = 0
        ins = self.E[q].dma_start(out=out, in_=in_)
        self.dcnt[semkey] += 16
        ins.then_inc(self.dsem[semkey], 16)
        tok = ('d_' + semkey, self.dsem[semkey], self.dcnt[semkey], 'dma')
        self._commit(tok, reads, writes)

    def barrier(self, engines=('pe', 'act', 'dve', 'pool', 'sp')):
        for e in engines:
            for f in self.comp:
                if f != e and self.ccnt[f] > 0:
                    self._wait(e, ('c_' + f, self.csem[f], self.ccnt[f], f))

    def finish(self):
        for k, sem in self.dsem.items():
            self._wait('sp', ('d_' + k, sem, self.dcnt[k], 'dma'))
        for f in self.comp:
            if self.ccnt[f] > 0:
                self._wait('sp', ('c_' + f, self.csem[f], self.ccnt[f], f))


class _StopBuild(Exception):
    pass


STOP = [None]
SELFWAIT = [1]


def _stage(name):
    if STOP[0] is not None and STOP[0] == name:
        raise _StopBuild()


def build_program():
    nc = bass.Bass("TRN2", target_bir_lowering=False)
    S = Sched(nc)
    try:
        _build_body(nc, S)
    except _StopBuild:
        pass
    S.finish()
    return nc


def _build_body(nc, S):

    def din(name, shape):
        return nc.dram_tensor(name, shape, F32, kind="ExternalInput").ap()

    def dout(name, shape):
        return nc.dram_tensor(name, shape, F32, kind="ExternalOutput").ap()

    xp = din("xp", [2048, D]); xs = din("xs", [64, D]); meta = din("meta", [16, D])
    ck = din("ck", [NL, 16, 128, 128]); cv = din("cv", [NL, 16, 128, 128])
    sconv = din("sconv", [NL, 48, D]); slru = din("slru", [NL, 16, D])
    wu = din("wu", [NL, NU, 128, 8, 512])
    prm = din("prm", [NL, 128, 104])
    rtab = din("rtab", [2, 2, 128, NCMAX])
    cst = din("cst", [128, 2688])
    yp = dout("yp", [2048, D]); ys = dout("ys", [64, D])
    wkp = dout("wkp", [NL, 128, 128]); wvp = dout("wvp", [NL, 128, 128])
    cvp = dout("cvp", [NL, 3, D]); lrp = dout("lrp", [NL, 8, 128])
    wks = dout("wks", [NL, 16, 128, 128]); wvs = dout("wvs", [NL, 16, 128, 128])
    cvs = dout("cvs", [NL, 48, D]); lrs = dout("lrs", [NL, 16, D])

    def sb(name, shape, dt):
        return nc.alloc_sbuf_tensor(name, shape, dt)

    xres = sb("xres", [128, 8, NCMAX], F32)
    xT = sb("xT", [128, 8, NCMAX], BF16)
    ring = [sb(f"ring{i}", [128, 8, 512], BF16) for i in range(NSLOT)]
    cosT = sb("cosT", [128, NCMAX], F32)
    sinT = sb("sinT", [128, NCMAX], F32)
    identF = sb("identF", [128, 128], F32)
    mdiag = sb("mdiag", [128, 4, 128], BF16)
    mprev = sb("mprev", [128, 4, 128], BF16)
    mpm = sb("mpm", [128, 4, 128], BF16)
    mc = sb("mc", [128, 512], BF16)
    mn = sb("mn", [128, 512], BF16)
    onesB = sb("onesB", [128, 128], BF16)
    halfF = sb("halfF", [128, 512], F32)
    lruw = sb("lruw", [128, 8, 256], BF16)
    pl = sb("pl", [128, 104], F32)
    es8 = sb("es8", [128, 8], F32)
    nsp8 = sb("nsp8", [128, 8], F32)
    nsp16 = sb("nsp16", [128, 8], F32)
    sptmp = sb("sptmp", [128, 8], F32)
    es_tile = sb("es_tile", [128, 4, 128], F32)
    kcar = sb("kcar", [128, NL, 128], BF16)
    vcar = sb("vcar", [128, NL, 2, 128], BF16)
    ccar = sb("ccar", [128, NL, 8, 3], F32)
    hcar = sb("hcar", [128, NL, 8], F32)
    cs_s = sb("cs_s", [128, 8, 48], F32)
    hs_s = sb("hs_s", [128, 8, 16], F32)
    stg = [sb(f"stg{i}", [128, 1024], F32) for i in range(2)]
    hb16 = sb("hb16", [128, 16], F32)
    hnsp8 = sb("hnsp8", [128, 8], F32)
    last = sb("lastperm", [128, 8], F32)
    A0 = (nc.lookup_mloc(last).addr + 32 + 63) // 64 * 64
    assert A0 + 80512 <= nc.SBUF_PARTITION_SIZE_BYTES, (A0,)

    def at(name, shape, dt, off):
        return nc.alloc_sbuf_tensor_at(name, shape, dt, offset=A0 + off)

    attnT = at("attnT", [128, 4, NCMAX], BF16, 0)
    rec = at("rec", [128, 8, NCMAX], BF16, 8832)
    merged = at("merged", [128, 8, NCMAX], BF16, 26496)
    qT = at("qT", [128, 4, NCMAX], BF16, 8832)
    kT = at("kT", [128, 128 + NCMAX], BF16, 17664)
    vaug = at("vaug", [128, 11, 2, 128], BF16, 20160)
    Pa = [at("Pa0", [128, 512], BF16, 25792), at("Pa1", [128, 512], BF16, 68032)]
    Pb = [at("Pb0", [128, 512], BF16, 26816), at("Pb1", [128, 512], BF16, 69056)]
    dsb = [at("dsb0", [128, 512], F32, 27840), at("dsb1", [128, 512], F32, 70080)]
    rden = [at("rden0", [128, 512], F32, 29888), at("rden1", [128, 512], F32, 72128)]
    kc_raw = at("kc_raw", [128, 16, 128], F32, 31936)
    kcT = at("kcT", [128, 16, 128], BF16, 40128)
    vc_aug = at("vc_aug", [128, 16, 2, 128], BF16, 44224)
    Pc = at("Pc", [128, 512], BF16, 52416)
    Pn = at("Pn", [128, 512], BF16, 53440)
    kf32 = at("kf32", [128, 192], F32, 54464)
    vtmp = at("vtmp", [128, 128], F32, 55232)
    ropeA = [at("ropeA0", [128, 512], F32, 55744), at("ropeA1", [128, 512], F32, 61888)]
    ropeB = [at("ropeB0", [128, 512], F32, 57792), at("ropeB1", [128, 512], F32, 63936)]
    ropeC = [at("ropeC0", [128, 512], F32, 59840), at("ropeC1", [128, 512], F32, 65984)]
    cstF = at("cstF", [128, 2688], F32, 61888)
    LB = 44160
    xrow = [at("xrow0", [128, 3 + 1040], F32, 26496), at("xrow1", [128, 3 + 1040], F32, 30688)]
    hrow = [at("hrow0", [128, 1 + 1040], F32, 34880), at("hrow1", [128, 1 + 1040], F32, 39072)]
    hs = at("hs", [128, 16, 4], F32, 43264)
    h0_s = at("h0_s", [128, 8, 16], F32, 43520)
    asets = []
    for i_ in range(4):
        o_ = LB + 5120 * i_
        asets.append(dict(xc=at(f"xc{i_}", [128, 512], F32, o_), xcb=at(f"xcb{i_}", [128, 512], BF16, o_ + 2048),
                          gg=at(f"gg{i_}", [128, 512], F32, o_ + 3072)))
    bsets = []
    for i_ in range(2):
        o_ = LB + 20480 + 6144 * i_
        bsets.append(dict(ii=at(f"ii{i_}", [128, 512], F32, o_),
                          aa=at(f"aa{i_}", [128, 512], F32, o_ + 2048), a2=at(f"a2{i_}", [128, 512], F32, o_ + 4096)))
    xe_s = at("xe_s", [128, 8, 16, 7], F32, 76928)
    cset = []
    for i_ in range(2):
        o_ = LB + 8192 * i_
        cset.append(dict(sga=at(f"sga{i_}", [128, 512], F32, o_), sgl=at(f"sgl{i_}", [128, 512], F32, o_ + 2048),
                         m1=at(f"m1{i_}", [128, 512], F32, o_ + 4096), m2=at(f"m2{i_}", [128, 512], F32, o_ + 6144)))
    zb = at("zb", [128, 8, 512], BF16, 0)
    sq = at("sq", [128, 8, 512], BF16, 8192)
    mean = at("mean", [128, 512], F32, 16384)
    msq = at("msq", [128, 512], F32, 18432)
    rstd = at("rstd", [128, 512], F32, 20480)
    t1 = [at("t1a", [128, 512], F32, 22528), at("t1b", [128, 512], F32, 24576 - 128)]
    hT = at("hT", [128, 22, NCMAX], BF16, 26496)
    silu_t = [at("silu0", [128, 512], F32, 75072), at("silu1", [128, 512], F32, 77120)]

    ps = [nc.alloc_psum_tensor(f"ps{i}", [128, 512], F32) for i in range(8)]
    mmctr = [0]
    mmpool = [[0, 1, 2, 3]]

    def set_banks(lst):
        mmpool[0] = list(lst)

    def mmbank():
        b = mmpool[0][mmctr[0] % len(mmpool[0])]
        mmctr[0] += 1
        return b

    class Rot:
        def __init__(self, items):
            self.items = items
            self.i = -1

        def nxt(self):
            self.i = (self.i + 1) % len(self.items)
            return self.i, self.items[self.i]

    def mm(out_ap, pairs, reads, writes):
        def fn(pe):
            n = len(pairs)
            ins = None
            for i, (l, r) in enumerate(pairs):
                ins = pe.matmul(out_ap, lhsT=l, rhs=r, start=(i == 0), stop=(i == n - 1))
            return ins
        S.op('pe', fn, reads, writes)

    def tp(out_ap, in_ap, n, reads, writes):
        S.op('pe', lambda pe: pe.transpose(out_ap, in_ap, identF[0:n, 0:n]), reads, writes)

    def act(out, in_, func, reads, writes, bias=None, scale=None):
        kw = {}
        if func == AF.Copy and (bias is not None or scale is not None):
            func = AF.Identity
        if bias is not None:
            kw['bias'] = bias
        if scale is not None:
            kw['scale'] = scale
        S.op('act', lambda e: e.activation(out=out, in_=in_, func=func, **kw), reads, writes)

    def tt(out, in0, in1, op, reads, writes, eng='dve'):
        S.op(eng, lambda e: e.tensor_tensor(out=out, in0=in0, in1=in1, op=op), reads, writes)

    def ts(out, in0, s1, s2, op0, op1, reads, writes, eng='dve'):
        if s2 is None:
            S.op(eng, lambda e: e.tensor_scalar(out=out, in0=in0, scalar1=s1, scalar2=None, op0=op0), reads, writes)
        else:
            S.op(eng, lambda e: e.tensor_scalar(out=out, in0=in0, scalar1=s1, scalar2=s2, op0=op0, op1=op1), reads, writes)

    def stt(out, in0, sc, in1, op0, op1, reads, writes, eng='dve'):
        S.op(eng, lambda e: e.scalar_tensor_tensor(out=out, in0=in0, scalar=sc, in1=in1, op0=op0, op1=op1), reads, writes)

    def cp(out, in_, reads, writes, eng='dve'):
        S.op(eng, lambda e: e.tensor_copy(out=out, in_=in_), reads, writes)

    def ms(ap, val, writes, eng='dve'):
        S.op(eng, lambda e: e.memset(ap, val), (), writes)

    rs = dict(issued=0, released=-1)
    total_units = 2 * NL * NU

    def ring_prefetch():
        while rs['issued'] < total_units and rs['issued'] - NSLOT <= rs['released']:
            g = rs['issued']
            l = (g // NU) % NL
            u = g % NU
            s = g % NSLOT
            for hf in range(2):
                S.dma('pool', ring[s][:, 4 * hf:4 * hf + 4, :], wu[l, u, :, 4 * hf:4 * hf + 4, :], f"ring{s}_{hf}", reads=(), writes=[('slot', s, hf)])
            rs['issued'] += 1

    def ring_release(g):
        rs['released'] = max(rs['released'], g)
        ring_prefetch()

    S.dma('sp', cstF[:, :], cst[:, :], 'cst', (), ['cstF'])
    cp(mdiag[:, :, :], cstF[:, 0:512].rearrange("p (r q) -> p r q", r=4), ['cstF'], ['masks'])
    cp(mprev[:, :, :], cstF[:, 512:1024].rearrange("p (r q) -> p r q", r=4), ['cstF'], ['masks'])
    cp(mpm[:, :, :], cstF[:, 1024:1536].rearrange("p (r q) -> p r q", r=4), ['cstF'], ['masks'])
    cp(mc[:, :], cstF[:, 1536:2048], ['cstF'], ['masks'])
    cp(mn[:, :], cstF[:, 2048:2560], ['cstF'], ['masks'])
    cp(identF[:, :], cstF[:, 2560:2688], ['cstF'], ['ident'])
    ms(onesB[:, :], 1.0, ['ones'])
    ms(halfF[:, :], 0.5, ['ones'])
    ring_prefetch()
    _stage('setup')

    stg_i = [0]

    def next_stg():
        i = stg_i[0] % 2
        stg_i[0] += 1
        return i

    def load_x_tile(src_ap, nrows, col0, ti):
        si = next_stg()
        S.dma('sp', stg[si][0:nrows, :], src_ap, f"stg{si}", (), [('stg', si)])
        for hb in range(2):
            bank = 6 + hb
            for kk in range(4):
                k = 4 * hb + kk
                tp(ps[bank][:, 128 * kk:128 * kk + nrows], stg[si][0:nrows, 128 * k:128 * k + 128], nrows,
                   [('stg', si), 'ident'], [('ps', bank)])
            src = ps[bank][:, :].rearrange("p (a b) -> p a b", a=4)[:, :, 0:nrows]
            act(xres[:, 4 * hb:4 * hb + 4, col0:col0 + nrows], src, AF.Copy, [('ps', bank)], [('xres', ti)])
            cp(xT[:, 4 * hb:4 * hb + 4, col0:col0 + nrows], src, [('ps', bank)], [('xT', ti)])

    def store_tok_tile(dst_ap, nrows, col0, ti):
        si = next_stg()
        for hb in range(2):
            bank = 6 + hb
            for kk in range(4):
                k = 4 * hb + kk
                tp(ps[bank][0:nrows, 128 * kk:128 * kk + 128], xres[:, k, col0:col0 + nrows], 128,
                   [('xres', ti), 'ident'], [('ps', bank)])
            act(stg[si][0:nrows, 512 * hb:512 * hb + 512], ps[bank][0:nrows, :], AF.Copy, [('ps', bank)], [('stg', si)])
        S.dma('sp', dst_ap, stg[si][0:nrows, :], f"stg{si}", [('stg', si)], ())

    def tile_of(c, tts):
        for i, (a, b) in enumerate(tts):
            if a <= c < b:
                return i
        raise ValueError

    t1bufs = [at(f"t1z{i}", [128, 512], F32, 2048 * i) for i in range(4)]
    t1keys = [('t1', i) for i in range(4)]
    t1r = Rot(list(zip(t1bufs, t1keys)))

    ZBK = [('zb', r) for r in range(8)]
    SQK = [('sq', r) for r in range(8)]

    def ln_row_stage(ti, c0, c1, r):
        w = c1 - c0
        cp(zb[:, r, 0:w], xres[:, r, c0:c1], [('xres', ti)], [('zb', r), ('t1', r // 2)])
        act(sq[:, r, 0:w], xres[:, r, c0:c1], AF.Square, [('xres', ti)], [('sq', r)])

    def layer_norm_tile(ti, c0, c1, gcol, bcol):
        w = c1 - c0
        mm(ps[4][:, 0:w], [(onesB[:, :], zb[:, r, 0:w]) for r in range(8)], ZBK + ['ones'], [('ps', 4)])
        mm(ps[5][:, 0:w], [(onesB[:, :], sq[:, r, 0:w]) for r in range(8)], SQK + ['ones'], [('ps', 5)])
        act(mean[:, 0:w], ps[4][:, 0:w], AF.Copy, [('ps', 4)], ['mean'], scale=1.0 / D)
        tt(msq[:, 0:w], mean[:, 0:w], mean[:, 0:w], ALU.mult, ['mean'], ['msq'])
        stt(rstd[:, 0:w], ps[5][:, 0:w], 1.0 / D, msq[:, 0:w], ALU.mult, ALU.subtract, [('ps', 5), 'msq'], ['rstd'])
        ts(rstd[:, 0:w], rstd[:, 0:w], LN_EPS, None, ALU.add, ALU.bypass, ['rstd'], ['rstd'])
        act(rstd[:, 0:w], rstd[:, 0:w], AF.Ln, ['rstd'], ['rstd'])
        act(rstd[:, 0:w], rstd[:, 0:w], AF.Exp, ['rstd'], ['rstd'], scale=-0.5)
        for r in range(8 + 2):
            if r < 8:
                _, (tb, tk) = t1r.nxt()
                tt(tb[:, 0:w], xres[:, r, c0:c1], mean[:, 0:w], ALU.subtract, [('xres', ti), 'mean'], [tk])
                tt(tb[:, 0:w], tb[:, 0:w], rstd[:, 0:w], ALU.mult, [tk, 'rstd'], [tk], eng='pool')
                act(xres[:, r, c0:c1], tb[:, 0:w], AF.Identity, [tk, 'pl'], [('xres', ti, r)],
                    bias=pl[:, bcol + r:bcol + r + 1], scale=pl[:, gcol + r:gcol + r + 1])
            if r - 2 >= 0:
                rr_ = r - 2
                cp(xT[:, rr_, c0:c1], xres[:, rr_, c0:c1], [('xres', ti, rr_)], [('xT', ti)])
        S.transfer([('xres', ti, r) for r in range(8)], [('xres', ti)])

    for hh, H in enumerate(HALVES):
        NC_ = H['NC']; tts = H['tts']; blocks = H['blocks']; samp = H['samp']; seq0 = H['seq0']
        nt = len(tts)
        xT_all = [('xT', i) for i in range(nt)]
        S.barrier()
        S.dma('sp', cosT[:, 0:NC_], rtab[hh, 0, :, 0:NC_], 'rtab', (), ['rtab'])
        S.dma('sp', sinT[:, 0:NC_], rtab[hh, 1, :, 0:NC_], 'rtab', (), ['rtab'])
        _stage('rtab')
        if samp:
            load_x_tile(xs[:, :], 64, 0, 0)
            _stage('x0')
            load_x_tile(meta[:, :], 16, 64, 0)
            _stage('x1')
        for j, c0 in enumerate(blocks):
            gj = j + (0 if hh == 0 else 8)
            load_x_tile(xp[128 * gj:128 * gj + 128, :], 128, c0, tile_of(c0, tts))

        _stage('xload')
        for l in range(NL):
            gbase = (hh * NL + l) * NU
            if l == 0:
                S.barrier()
            else:
                oldk = ([('hT', i) for i in range(nt)] + [('silu', 0), ('silu', 1)] + [('zb', r_) for r_ in range(8)] + [('sq', r_) for r_ in range(8)] + ['mean', 'msq', 'rstd', 't1a'] + [('t1', i) for i in range(4)]
                        + [('merged', i) for i in range(nt)])
                newk = ([('q', i) for i in range(nt)] + [('k', i) for i in range(-1, nt)] + ['vaug', 'kc_raw', 'kcT', 'vc_aug', 'Pc', 'Pn', 'kf32']
                        + [(nm, i) for nm in ('ropeA', 'ropeB', 'ropeC', 'Pa', 'Pb', 'dsb', 'rden') for i in range(2)]
                        + [('attnT', 0), ('attnT', 1)])
                S.transfer(oldk, newk)
            S.dma('sp', pl[:, :], prm[l], 'prm', (), ['pl'])
            act(es8[:, :], pl[:, 96:104], AF.Exp, ['pl'], ['es8'])
            for r in range(4):
                ts(es_tile[0:64, r, :], halfF[0:64, 0:128], es8[0:64, 4 + r:5 + r], 2.0, ALU.mult, ALU.mult, ['es8', 'ones'], ['es'])
                ts(es_tile[64:128, r, :], halfF[64:128, 0:128], es8[64:128, r:r + 1], 2.0, ALU.mult, ALU.mult, ['es8', 'ones'], ['es'])
            act(sptmp[:, :], pl[:, 56:64], AF.Exp, ['pl'], ['sp'], scale=-1.0)
            act(sptmp[:, :], sptmp[:, :], AF.Ln, ['sp'], ['sp'], bias=1.0)
            ts(nsp8[:, :], sptmp[:, :], -8.0, None, ALU.mult, ALU.bypass, ['sp'], ['nsp8'])
            ts(nsp16[:, :], sptmp[:, :], -16.0, None, ALU.mult, ALU.bypass, ['sp'], ['nsp8'])
            ts(hnsp8[:, :], sptmp[:, :], -4.0, None, ALU.mult, ALU.bypass, ['sp'], ['nsp8'])
            ts(hb16[:, :], pl[:, 40:56], 0.5, None, ALU.mult, ALU.bypass, ['pl'], ['nsp8'])

            _stage('params')
            if samp:
                S.dma('sp', wks[l, :, 0:124, :], ck[l, :, 4:128, :], 'cpy', (), ())
                S.dma('sp', wvs[l, :, 0:124, :], cv[l, :, 4:128, :], 'cpy', (), ())
                S.dma('sp', kc_raw[:, :, :], ck[l].rearrange("b s c -> s b c"), 'kc', (), ['kc_raw'])
            ms(vaug[:, :, 0, 64:128], 1.0, ['vaug'])
            ms(vaug[:, :, 1, 0:64], 1.0, ['vaug'])
            if hh == 1:
                cp(kT[:, 0:128], kcar[:, l, :], ['kcar'], [('k', -1)])
                cp(vaug[:, 0, 0, 0:64], vcar[:, l, 0, 0:64], ['vcar'], ['vaug'])
                cp(vaug[:, 0, 1, 64:128], vcar[:, l, 1, 64:128], ['vcar'], ['vaug'])

            gq = gbase + U_Q; gqr = gbase + U_QR; gkv = gbase + U_KV
            sq_ = ring[gq % NSLOT]; sqr_ = ring[gqr % NSLOT]; skv_ = ring[gkv % NSLOT]
            def slotkey(g):
                return (('slot', g % NSLOT, 0), ('slot', g % NSLOT, 1))

            set_banks([0, 1, 2, 3, 4, 5])
            rrot = Rot([0, 1])
            for ti, (c0, c1) in enumerate(tts):
                for r in range(4):
                    w = c1 - c0
                    b1 = mmbank(); b2 = mmbank()
                    ri, _ = rrot.nxt()
                    rA, rB = ropeA[ri], ropeB[ri]
                    mm(ps[b1][:, 0:w], [(sq_[:, k, 128 * r:128 * r + 128], xT[:, k, c0:c1]) for k in range(8)],
                       [*slotkey(gq), ('xT', ti)], [('ps', b1)])
                    mm(ps[b2][:, 0:w], [(sqr_[:, k, 128 * r:128 * r + 128], xT[:, k, c0:c1]) for k in range(8)],
                       [*slotkey(gqr), ('xT', ti)], [('ps', b2)])
                    tt(rA[:, 0:w], ps[b1][:, 0:w], cosT[:, c0:c1], ALU.mult, [('ps', b1), 'rtab'], [('ropeA', ri)])
                    tt(rB[:, 0:w], ps[b2][:, 0:w], sinT[:, c0:c1], ALU.mult, [('ps', b2), 'rtab'], [('ropeB', ri)])
                    tt(qT[:, r, c0:c1], rA[:, 0:w], rB[:, 0:w], ALU.add, [('ropeA', ri), ('ropeB', ri)], [('q', ti)], eng='pool')
            ring_release(gq); ring_release(gqr)
            for ti, (c0, c1) in enumerate(tts):
                w = c1 - c0
                b1 = mmbank(); b2 = mmbank()
                ri, _ = rrot.nxt()
                rA, rB, rC = ropeA[ri], ropeB[ri], ropeC[ri]
                mm(ps[b1][:, 0:w], [(skv_[:, k, 0:128], xT[:, k, c0:c1]) for k in range(8)], [*slotkey(gkv), ('xT', ti)], [('ps', b1)])
                mm(ps[b2][:, 0:w], [(skv_[:, k, 128:256], xT[:, k, c0:c1]) for k in range(8)], [*slotkey(gkv), ('xT', ti)], [('ps', b2)])
                tt(rA[:, 0:w], ps[b1][:, 0:w], cosT[:, c0:c1], ALU.mult, [('ps', b1), 'rtab'], [('ropeA', ri)])
                tt(rB[:, 0:w], ps[b2][:, 0:w], sinT[:, c0:c1], ALU.mult, [('ps', b2), 'rtab'], [('ropeB', ri)])
                tt(rC[:, 0:w], rA[:, 0:w], rB[:, 0:w], ALU.add, [('ropeA', ri), ('ropeB', ri)], [('ropeC', ri)], eng='pool')
                act(kT[:, 128 + c0:128 + c1], rC[:, 0:w], AF.Copy, [('ropeC', ri)], [('k', ti)])
                if samp and ti == 0:
                    act(kf32[:, 0:64], rC[:, 0:64], AF.Copy, [('ropeC', ri)], ['kf32'])
                if hh == 1 and ti == nt - 1:
                    act(kf32[:, 64:192], rC[:, w - 128:w], AF.Copy, [('ropeC', ri)], ['kf32'])
            vtiles = []
            if samp:
                vtiles.append((2, 0, 64, 0)); vtiles.append((1, 64, 16, 0))
            for j, c0 in enumerate(blocks):
                vtiles.append((3 + j, c0, 128, tile_of(c0, tts)))
            for (vi, c0, nrow, ti) in vtiles:
                mm(ps[7][0:nrow, 0:128], [(xT[:, k, c0:c0 + nrow], skv_[:, k, 256:384]) for k in range(8)],
                   [*slotkey(gkv), ('xT', ti)], [('ps', 7)])
                act(vaug[0:nrow, vi, 0, 0:64], ps[7][0:nrow, 0:64], AF.Copy, [('ps', 7)], ['vaug'])
                act(vaug[0:nrow, vi, 1, 64:128], ps[7][0:nrow, 64:128], AF.Copy, [('ps', 7)], ['vaug'])
                if vi == 2:
                    cp(stg[0][0:64, 0:128], ps[7][0:64, 0:128], [('ps', 7)], [('stg', 0)])
                    for b_ in range(16):
                        S.dma('sp', wvs[l, b_, 124:128, :], stg[0][4 * b_:4 * b_ + 4, 0:128], 'stg0', [('stg', 0)], ())
                if hh == 1 and vi == 10:
                    cp(stg[0][:, 0:128], ps[7][:, 0:128], [('ps', 7)], [('stg', 0)])
                    S.dma('sp', wvp[l], stg[0][:, 0:128], 'stg0', [('stg', 0)], ())
            ring_release(gkv)
            if samp:
                tp(ps[7][0:64, 0:128], kf32[:, 0:64], 128, ['kf32', 'ident'], [('ps', 7)])
                cp(stg[1][0:64, 0:128], ps[7][0:64, 0:128], [('ps', 7)], [('stg', 1)])
                for b_ in range(16):
                    S.dma('sp', wks[l, b_, 124:128, :], stg[1][4 * b_:4 * b_ + 4, 0:128], 'stg1', [('stg', 1)], ())
            if hh == 1:
                tp(ps[7][:, 0:128], kf32[:, 64:192], 128, ['kf32', 'ident'], [('ps', 7)])
                cp(stg[1][:, 0:128], ps[7][:, 0:128], [('ps', 7)], [('stg', 1)])
                S.dma('sp', wkp[l], stg[1][:, 0:128], 'stg1', [('stg', 1)], ())

            _stage('qkv')
            kall = [('k', i) for i in range(-1, nt)]
            qall = [('q', i) for i in range(nt)]

            items = []
            if samp:
                items.append((64, 16, 0, 0, None, vaug[:, 1], None))
            for j, c0 in enumerate(blocks):
                if hh == 0 and j == 0:
                    items.append((c0, 128, 16, 128 + 64, vaug[:, 1], vaug[:, 3], mpm))
                else:
                    items.append((c0, 128, 128, 128 + c0 - 128, vaug[:, 3 + j - 1] if j > 0 else vaug[:, 0], vaug[:, 3 + j], mprev))
            work = [(it, g) for it in items for g in range(2)]

            def v3(ap2):
                return ap2.rearrange("p (r q) -> p r q", r=4)

            def attn_s(wi):
                (cq0, Nq, Kp, kp0, vprev, vcur, mask_prev), g = work[wi]
                si_ = wi % 2
                bA, bB = (0, 1) if si_ == 0 else (3, 4)
                Pa_, Pb_ = Pa[si_], Pb[si_]
                kPa, kPb = ('Pa', si_), ('Pb', si_)
                pr = slice(64 * g, 64 * g + 64)
                rhs_q = qT[pr, :, cq0:cq0 + Nq]
                NN = 4 * Nq
                if Kp:
                    mm(v3(ps[bA][0:Kp, 0:NN]), [(kT[pr, kp0:kp0 + Kp], rhs_q)], kall + qall, [('ps', bA)])
                    act(Pa_[0:Kp, 0:NN], ps[bA][0:Kp, 0:NN], AF.Exp, [('ps', bA)], [kPa], scale=0.125)
                    tt(v3(Pa_[0:Kp, 0:NN]), v3(Pa_[0:Kp, 0:NN]), mask_prev[0:Kp, :, 0:Nq], ALU.mult, [kPa, 'masks'], [kPa], eng='pool')
                mm(v3(ps[bB][0:Nq, 0:NN]), [(kT[pr, 128 + cq0:128 + cq0 + Nq], rhs_q)], kall + qall, [('ps', bB)])
                act(Pb_[0:Nq, 0:NN], ps[bB][0:Nq, 0:NN], AF.Exp, [('ps', bB)], [kPb], scale=0.125)
                tt(v3(Pb_[0:Nq, 0:NN]), v3(Pb_[0:Nq, 0:NN]), mdiag[0:Nq, :, 0:Nq], ALU.mult, [kPb, 'masks'], [kPb], eng='pool')

            def attn_pv(wi):
                (cq0, Nq, Kp, kp0, vprev, vcur, mask_prev), g = work[wi]
                si_ = wi % 2
                bO = 2 if si_ == 0 else 5
                Pa_, Pb_, dsb_, rden_ = Pa[si_], Pb[si_], dsb[si_], rden[si_]
                kPa, kPb, kds, krd = ('Pa', si_), ('Pb', si_), ('dsb', si_), ('rden', si_)
                pr = slice(64 * g, 64 * g + 64)
                dn = slice(64 - 64 * g, 128 - 64 * g)
                NN = 4 * Nq
                pairs = []
                rd_ = [kPb, 'vaug']
                if Kp:
                    pairs.append((vprev[0:Kp, g, :], Pa_[0:Kp, 0:NN]))
                    rd_.append(kPa)
                pairs.append((vcur[0:Nq, g, :], Pb_[0:Nq, 0:NN]))
                mm(ps[bO][:, 0:NN], pairs, rd_, [('ps', bO)])
                tt(v3(dsb_[dn, 0:NN]), v3(ps[bO][dn, 0:NN]), es_tile[dn, :, 0:Nq], ALU.add, [('ps', bO), 'es'], [kds])
                act(rden_[dn, 0:NN], dsb_[dn, 0:NN], AF.Ln, [kds], [krd])
                act(rden_[dn, 0:NN], rden_[dn, 0:NN], AF.Exp, [krd], [krd], scale=-1.0)
                tt(attnT[pr, :, cq0:cq0 + Nq], v3(ps[bO][pr, 0:NN]), v3(rden_[dn, 0:NN]), ALU.mult,
                   [('ps', bO), krd], [('attnT', g)])

            for wi in range(len(work) + 1):
                if wi < len(work):
                    attn_s(wi)
                if wi - 1 >= 0:
                    attn_pv(wi - 1)

            _stage('attnp')
            if samp:
                for b in range(16):
                    tp(ps[7][:, 0:128], kc_raw[:, b, :], 128, ['kc_raw', 'ident'], [('ps', 7)])
                    act(kcT[:, b, :], ps[7][:, 0:128], AF.Copy, [('ps', 7)], ['kcT'])
                S.dma('sp', kc_raw[:, :, :], cv[l].rearrange("b s c -> s b c"), 'kc', ['kc_raw'], ['kc_raw'])
                ms(vc_aug[:, :, 0, 64:128], 1.0, ['vc_aug'])
                ms(vc_aug[:, :, 1, 0:64], 1.0, ['vc_aug'])
                act(vc_aug[:, :, 0, 0:64], kc_raw[:, :, 0:64], AF.Copy, ['kc_raw'], ['vc_aug'])
                act(vc_aug[:, :, 1, 64:128], kc_raw[:, :, 64:128], AF.Copy, ['kc_raw'], ['vc_aug'])

                def cols(psap, g, b):
                    return psap[:, 256 * g:256 * g + 256].rearrange("p (r b t) -> p r b t", r=4, b=16)[:, :, b, :]
                for g in range(2):
                    pr = slice(64 * g, 64 * g + 64)
                    for b in range(16):
                        S.op('pe', lambda pe, g=g, b=b, pr=pr: pe.matmul(cols(ps[4], g, b), lhsT=kcT[pr, b, :], rhs=qT[pr, :, 4 * b:4 * b + 4], start=True, stop=True),
                             ['kcT'] + qall, [('ps', 4)])
                    mm(ps[5][0:64, 256 * g:256 * g + 256].rearrange("p (r q) -> p r q", r=4),
                       [(kT[pr, 128:192], qT[pr, :, 0:64])], kall + qall, [('ps', 5)])
                act(Pc[:, :], ps[4][:, :], AF.Exp, [('ps', 4)], ['Pc'], scale=0.125)
                tt(Pc[:, :], Pc[:, :], mc[:, :], ALU.mult, ['Pc', 'masks'], ['Pc'])
                act(Pn[0:64, :], ps[5][0:64, :], AF.Exp, [('ps', 5)], ['Pn'], scale=0.125)
                tt(Pn[0:64, :], Pn[0:64, :], mn[0:64, :], ALU.mult, ['Pn', 'masks'], ['Pn'])
                for g in range(2):
                    for b in range(16):
                        def fn(pe, g=g, b=b):
                            pe.matmul(cols(ps[6], g, b), lhsT=vc_aug[:, b, g, :], rhs=cols(Pc, g, b), start=True, stop=False)
                            return pe.matmul(cols(ps[6], g, b), lhsT=vaug[0:64, 2, g, :], rhs=cols(Pn, g, b)[0:64], start=False, stop=True)
                        S.op('pe', fn, ['Pc', 'Pn', 'vc_aug', 'vaug'], [('ps', 6)])
                for g in range(2):
                    pr = slice(64 * g, 64 * g + 64)
                    dn = slice(64 - 64 * g, 128 - 64 * g)
                    cs = slice(256 * g, 256 * g + 256)
                    tt(dsb[0][dn, cs].rearrange("p (r q) -> p r q", r=4), ps[6][dn, cs].rearrange("p (r q) -> p r q", r=4),
                       es_tile[dn, :, 0:64], ALU.add, [('ps', 6), 'es'], [('dsb', 0)])
                    act(rden[0][dn, cs], dsb[0][dn, cs], AF.Ln, [('dsb', 0)], [('rden', 0)])
                    act(rden[0][dn, cs], rden[0][dn, cs], AF.Exp, [('rden', 0)], [('rden', 0)], scale=-1.0)
                    tt(attnT[pr, :, 0:64], ps[6][pr, cs].rearrange("p (r q) -> p r q", r=4),
                       rden[0][dn, cs].rearrange("p (r q) -> p r q", r=4), ALU.mult, [('ps', 6), ('rden', 0)], [('attnT', g)])
            if hh == 0:
                cp(kcar[:, l, :], kT[:, 128 + NC_ - 128:128 + NC_], kall, ['kcar'])
                cp(vcar[:, l, :, :], vaug[:, 10, :, :], ['vaug'], ['vcar'])

            _stage('attns')
            S.barrier()
            glru = gbase + U_LRU
            for hf in range(2):
                S.dma('pool', lruw[:, 4 * hf:4 * hf + 4, :], wu[l, U_LRU, :, 4 * hf:4 * hf + 4, 0:256], f'lruw{hf}', (), [('lruw', hf)])
            ring_release(glru)
            L = NC_ - seq0
            set_banks([0, 1, 2, 3, 4, 5, 6])
            if samp:
                S.dma('sp', stg[0][0:48, :], sconv[l], 'stg0', (), [('stg', 0)])
                S.dma('sp', stg[1][0:16, :], slru[l], 'stg1', (), [('stg', 1)])
                for n in range(8):
                    tp(ps[7][:, 0:48], stg[0][0:48, 128 * n:128 * n + 128], 48, [('stg', 0), 'ident'], [('ps', 7)])
                    act(xe_s[:, n, :, 0:3], ps[7][:, 0:48].rearrange("p (b j) -> p b j", b=16), AF.Copy, [('ps', 7)], [('xe_s', n)])
                    tp(ps[7][:, 64:80], stg[1][0:16, 128 * n:128 * n + 128], 16, [('stg', 1), 'ident'], [('ps', 7)])
                    act(h0_s[:, n, :], ps[7][:, 64:80], AF.Copy, [('ps', 7)], [('h0_s', n)])
            steps = [(n, ti) for n in range(8) for ti in range(nt)]
            set_banks([0, 1, 2, 3, 4, 5, 6, 7])

            gelu_pending = {}

            def emit_gelu(si):
                if si in gelu_pending:
                    gg_, bgk_, w_, kgg_ = gelu_pending.pop(si)
                    act(gg_[:, 0:w_], ps[bgk_][:, 0:w_], AF.Gelu_apprx_tanh, [('ps', bgk_)], [kgg_])

            def part_a(si):
                n, ti = steps[si]
                c0, c1 = tts[ti]
                w = c1 - c0
                AS = asets[si % 4]
                xc, xcb, gg = AS['xc'], AS['xcb'], AS['gg']
                kxc, kxcb, kgg = ('xc', si % 4), ('xcb', si % 4), ('gg', si % 4)
                xr_, kxr = xrow[n % 2], ('xrow', n % 2)
                hr_, khr = hrow[n % 2], ('hrow', n % 2)
                gx = gbase + U_XG[n // 2]
                sx_ = ring[gx % NSLOT]
                cxr = (n % 2) * 256
                cgt = cxr + 128
                if ti == 0:
                    if samp:
                        ms(xr_[:, 0:3], 0.0, [kxr])
                        ms(hr_[:, 0:1], 0.0, [khr])
                    else:
                        cp(xr_[:, 0:3], ccar[:, l, n, :], [('ccar', n)], [kxr])
                        cp(hr_[:, 0:1], hcar[:, l, n:n + 1], [('hcar', n)], [khr])
                bxk = mmbank(); bgk = mmbank()
                mm(ps[bxk][:, 0:w], [(sx_[:, k, cxr:cxr + 128], xT[:, k, c0:c1]) for k in range(8)], [*slotkey(gx), ('xT', ti)], [('ps', bxk)])
                mm(ps[bgk][:, 0:w], [(sx_[:, k, cgt:cgt + 128], xT[:, k, c0:c1]) for k in range(8)], [*slotkey(gx), ('xT', ti)], [('ps', bgk)])
                sa = max(c0, seq0)
                a = sa - seq0
                Ls = c1 - sa
                o = sa - c0
                if samp and ti == 0:
                    act(xe_s[:, n, :, 3:7], ps[bxk][:, 0:64].rearrange("p (b t) -> p b t", b=16), AF.Copy, [('ps', bxk)], [('xe_s', n)])
                cp(xr_[:, 3 + a:3 + a + Ls], ps[bxk][:, o:w], [('ps', bxk)], [kxr])
                gelu_pending[si] = (gg, bgk, w, kgg)
                if samp and ti == 0:
                    xcs = xc[:, 0:64].rearrange("p (b t) -> p b t", b=16)
                    ts(xcs, xe_s[:, n, :, 0:4], pl[:, 64 + 4 * n:65 + 4 * n], pl[:, 32 + n:33 + n], ALU.mult, ALU.add, [('xe_s', n), 'pl'], [kxc])
                    for jj in range(1, 4):
                        stt(xcs, xe_s[:, n, :, jj:jj + 4], pl[:, 64 + 4 * n + jj:65 + 4 * n + jj], xcs, ALU.mult, ALU.add, [('xe_s', n), 'pl', kxc], [kxc])
                    cp(cs_s[:, n, :].rearrange("p (b j) -> p b j", b=16), xe_s[:, n, :, 4:7], [('xe_s', n)], ['cs_s'], eng='pool')
                ts(xc[:, o:w], xr_[:, a:a + Ls], pl[:, 64 + 4 * n:65 + 4 * n], pl[:, 32 + n:33 + n], ALU.mult, ALU.add, [kxr, 'pl'], [kxc])
                for jj in range(1, 4):
                    stt(xc[:, o:w], xr_[:, a + jj:a + jj + Ls], pl[:, 64 + 4 * n + jj:65 + 4 * n + jj], xc[:, o:w], ALU.mult, ALU.add, [kxr, 'pl', kxc], [kxc])
                cp(xcb[:, 0:w], xc[:, 0:w], [kxc], [kxcb])
                if ti == nt - 1:
                    cp(ccar[:, l, n, :], xr_[:, L:L + 3], [kxr], [('ccar', n)], eng='pool')

            def bcommon(si):
                n, ti = steps[si]
                c0, c1 = tts[ti]
                w = c1 - c0
                AS = asets[si % 4]
                BS = bsets[si % 2]
                sa = max(c0, seq0)
                return dict(n=n, ti=ti, c0=c0, c1=c1, w=w, xc=AS['xc'], xcb=AS['xcb'], gg=AS['gg'],
                            kxc=('xc', si % 4), kxcb=('xcb', si % 4), kgg=('gg', si % 4),
                            ii=BS['ii'], aa=BS['aa'], a2=BS['a2'], kii=('ii', si % 2), kaa=('aa', si % 2), ka2=('a2', si % 2),
                            hr_=hrow[n % 2], khr=('hrow', n % 2), sa=sa, a=sa - seq0, Ls=c1 - sa, o=sa - c0)

            def part_b1(si):
                d = bcommon(si)
                n, w, xc, xcb, ii, aa, a2 = d['n'], d['w'], d['xc'], d['xcb'], d['ii'], d['aa'], d['a2']
                kxc, kxcb, kii, kaa, ka2 = d['kxc'], d['kxcb'], d['kii'], d['kaa'], d['ka2']
                brk = mmbank(); bik = mmbank()
                mm(ps[brk][:, 0:w], [(lruw[:, n, 0:128], xcb[:, 0:w])], [('lruw', n // 4), kxcb], [('ps', brk)])
                mm(ps[bik][:, 0:w], [(lruw[:, n, 128:256], xcb[:, 0:w])], [('lruw', n // 4), kxcb], [('ps', bik)])
                act(aa[:, 0:w], ps[brk][:, 0:w], AF.Tanh, [('ps', brk), 'nsp8'], [kaa], bias=hb16[:, n:n + 1], scale=0.5)
                act(ii[:, 0:w], ps[bik][:, 0:w], AF.Tanh, [('ps', bik), 'nsp8'], [kii], bias=hb16[:, 8 + n:9 + n], scale=0.5)
                emit_gelu(si + LOOK)

            def part_b1b(si):
                d = bcommon(si)
                n, w, xc, xcb, ii, aa, a2 = d['n'], d['w'], d['xc'], d['xcb'], d['ii'], d['aa'], d['a2']
                kxc, kxcb, kii, kaa, ka2 = d['kxc'], d['kxcb'], d['kii'], d['kaa'], d['ka2']
                act(a2[:, 0:w], aa[:, 0:w], AF.Exp, [kaa, 'nsp8'], [ka2], scale=nsp8[:, n:n + 1], bias=nsp8[:, n:n + 1])
                act(aa[:, 0:w], aa[:, 0:w], AF.Exp, [kaa, 'nsp8'], [kaa], scale=hnsp8[:, n:n + 1], bias=hnsp8[:, n:n + 1])
                act(a2[:, 0:w], a2[:, 0:w], AF.Relu, [ka2], [ka2], scale=-0.25, bias=0.25)
                act(a2[:, 0:w], a2[:, 0:w], AF.Ln, [ka2], [ka2], bias=1e-30)
                act(a2[:, 0:w], a2[:, 0:w], AF.Exp, [ka2], [ka2], scale=0.5)
                stt(ii[:, 0:w], ii[:, 0:w], 1.0, xc[:, 0:w], ALU.add, ALU.mult, [kii, kxc], [kii])
                tt(ii[:, 0:w], ii[:, 0:w], a2[:, 0:w], ALU.mult, [kii, ka2], [kii], eng='pool')

            def part_b2(si):
                d = bcommon(si)
                n, ti, c1, w, gg, ii, aa = d['n'], d['ti'], d['c1'], d['w'], d['gg'], d['ii'], d['aa']
                kgg, kii, kaa, hr_, khr = d['kgg'], d['kii'], d['kaa'], d['hr_'], d['khr']
                sa, a, Ls, o = d['sa'], d['a'], d['Ls'], d['o']
                bb = ii
                kbb = kii
                if samp and ti == 0:
                    a3 = aa[:, 0:64].rearrange("p (b t) -> p b t", b=16)
                    b3 = bb[:, 0:64].rearrange("p (b t) -> p b t", b=16)
                    tt(hs[:, :, 0], a3[:, :, 0], h0_s[:, n, :], ALU.mult, [kaa, ('h0_s', n)], ['hs'])
                    tt(hs[:, :, 0], hs[:, :, 0], b3[:, :, 0], ALU.add, ['hs', kbb], ['hs'])
                    for t_ in range(1, 4):
                        tt(hs[:, :, t_], a3[:, :, t_], hs[:, :, t_ - 1], ALU.mult, [kaa, 'hs'], ['hs'])
                        tt(hs[:, :, t_], hs[:, :, t_], b3[:, :, t_], ALU.add, ['hs', kbb], ['hs'])
                    cp(hs_s[:, n, :], hs[:, :, 3], ['hs'], ['hs_s'], eng='pool')
                    tt(rec[:, n, 0:64], hs[:, :, :].rearrange("p b t -> p (b t)"), gg[:, 0:64], ALU.mult, ['hs', kgg], [('rec', ti)])
                S.op('dve', lambda e: e.tensor_tensor_scan(out=hr_[:, 1 + a:1 + a + Ls], data0=aa[:, o:w], data1=bb[:, o:w],
                                                           initial=hr_[:, a:a + 1], op0=ALU.mult, op1=ALU.add),
                     [kaa, kbb, khr], [khr])
                tt(rec[:, n, sa:c1], hr_[:, 1 + a:1 + a + Ls], gg[:, o:w], ALU.mult, [khr, kgg], [('rec', ti)], eng='pool')
                if ti == nt - 1:
                    cp(hcar[:, l, n:n + 1], hr_[:, L:L + 1], [khr], [('hcar', n)], eng='pool')
                    if n % 2 == 1:
                        ring_release(gbase + U_XG[n // 2])

            LOOK = 2
            NS = len(steps)
            assert NS % 2 == 0 and LOOK == 2
            for it in range(NS + 4):
                if it % 2 == 0:
                    for s_ in (it - 4, it - 3):
                        if 0 <= s_ < NS:
                            part_b2(s_)
                if it < NS:
                    part_a(it)
                    if it < LOOK:
                        emit_gelu(it)
                if 0 <= it - 2 < NS:
                    part_b1(it - 2)
                if it % 2 == 1:
                    for s_ in (it - 3, it - 2):
                        if 0 <= s_ < NS:
                            part_b1b(s_)
            assert not gelu_pending
            def emit_b_outputs():
                ccar_all = [('ccar', n) for n in range(8)]
                hcar_all = [('hcar', n) for n in range(8)]
                if hh == 1:
                    for hb in range(2):
                        for kk in range(4):
                            n = 4 * hb + kk
                            tp(ps[6 + hb][0:3, 128 * kk:128 * kk + 128], ccar[:, l, n, :], 128, ccar_all + ['ident'], [('ps', 6 + hb)])
                        cp(stg[0][0:3, 512 * hb:512 * hb + 512], ps[6 + hb][0:3, :], [('ps', 6 + hb)], [('stg', 0)])
                    S.dma('sp', cvp[l], stg[0][0:3, :], 'stg0', [('stg', 0)], ())
                    tp(ps[7][0:8, 0:128], hcar[:, l, :], 128, hcar_all + ['ident'], [('ps', 7)])
                    cp(stg[1][0:8, 0:128], ps[7][0:8, 0:128], [('ps', 7)], [('stg', 1)])
                    S.dma('sp', lrp[l], stg[1][0:8, 0:128], 'stg1', [('stg', 1)], ())
                else:
                    for hb in range(2):
                        for kk in range(4):
                            n = 4 * hb + kk
                            tp(ps[6 + hb][0:48, 128 * kk:128 * kk + 128], cs_s[:, n, :], 128, ['cs_s', 'ident'], [('ps', 6 + hb)])
                        cp(stg[0][0:48, 512 * hb:512 * hb + 512], ps[6 + hb][0:48, :], [('ps', 6 + hb)], [('stg', 0)])
                    S.dma('sp', cvs[l], stg[0][0:48, :], 'stg0', [('stg', 0)], ())
                    for hb in range(2):
                        for kk in range(4):
                            n = 4 * hb + kk
                            tp(ps[6 + hb][0:16, 128 * kk:128 * kk + 128], hs_s[:, n, :], 128, ['hs_s', 'ident'], [('ps', 6 + hb)])
                        cp(stg[1][0:16, 512 * hb:512 * hb + 512], ps[6 + hb][0:16, :], [('ps', 6 + hb)], [('stg', 1)])
                    S.dma('sp', lrs[l], stg[1][0:16, :], 'stg1', [('stg', 1)], ())

            _stage('lru')
            keysB = ([('xrow', i) for i in range(2)] + [('hrow', i) for i in range(2)] + ['hs'] + [('h0_s', n) for n in range(8)]
                     + [(nm, i) for nm in ('xc', 'xcb', 'gg') for i in range(4)] + [(nm, i) for nm in ('ii', 'aa', 'a2') for i in range(2)]
                     + [('xe_s', n) for n in range(8)])
            keysC = [(nm, i) for nm in ('sga', 'sgl', 'm1', 'm2') for i in range(2)]
            keysM = [('merged', i) for i in range(nt)]
            S.transfer(keysB, keysC + keysM)
            set_banks([0, 1, 2, 3, 4, 5, 6, 7])
            crot = Rot(cset)
            for cg in range(2):
                gga = gbase + U_GA[cg]; gap = gbase + U_AP[cg]; ggl = gbase + U_GL[cg]; glp = gbase + U_LP[cg]
                s_ga = ring[gga % NSLOT]; s_ap = ring[gap % NSLOT]; s_gl = ring[ggl % NSLOT]; s_lp = ring[glp % NSLOT]
                for cc in range(4):
                    c = 4 * cg + cc
                    wc = slice(128 * cc, 128 * cc + 128)
                    for ti, (c0, c1) in enumerate(tts):
                        w = c1 - c0
                        ci, CS = crot.nxt()
                        sga, sgl, m1, m2 = CS['sga'], CS['sgl'], CS['m1'], CS['m2']
                        b1 = mmbank()
                        mm(ps[b1][:, 0:w], [(s_ga[:, k, wc], xT[:, k, c0:c1]) for k in range(8)], [*slotkey(gga), ('xT', ti)], [('ps', b1)])
                        act(sga[:, 0:w], ps[b1][:, 0:w], AF.Sigmoid, [('ps', b1)], [('sga', ci)])
                        b2 = mmbank()
                        mm(ps[b2][:, 0:w], [(s_ap[:, r, wc], attnT[:, r, c0:c1]) for r in range(4)], [*slotkey(gap), ('attnT', 0), ('attnT', 1)], [('ps', b2)])
                        tt(m1[:, 0:w], ps[b2][:, 0:w], sga[:, 0:w], ALU.mult, [('ps', b2), ('sga', ci)], [('m1', ci)])
                        b3 = mmbank()
                        mm(ps[b3][:, 0:w], [(s_gl[:, k, wc], xT[:, k, c0:c1]) for k in range(8)], [*slotkey(ggl), ('xT', ti)], [('ps', b3)])
                        act(sgl[:, 0:w], ps[b3][:, 0:w], AF.Sigmoid, [('ps', b3)], [('sgl', ci)])
                        b4 = mmbank()
                        mm(ps[b4][:, 0:w], [(s_lp[:, k, wc], rec[:, k, c0:c1]) for k in range(8)], [*slotkey(glp), ('rec', ti)], [('ps', b4)])
                        tt(m2[:, 0:w], ps[b4][:, 0:w], sgl[:, 0:w], ALU.mult, [('ps', b4), ('sgl', ci)], [('m2', ci)])
                        tt(merged[:, c, c0:c1], m1[:, 0:w], m2[:, 0:w], ALU.add, [('m1', ci), ('m2', ci)], [('merged', ti)], eng='pool')
                ring_release(gga); ring_release(gap); ring_release(ggl); ring_release(glp)
                if cg == 0:
                    emit_b_outputs()

            _stage('merge')
            keysLN = [('zb', r_) for r_ in range(8)] + [('sq', r_) for r_ in range(8)] + ['mean', 'msq', 'rstd', 't1a'] + [('t1', i) for i in range(4)]
            S.transfer([('attnT', 0), ('attnT', 1)] + [('rec', i) for i in range(nt)], keysLN)
            set_banks([0, 1, 2, 3])
            gos = [gbase + U_OUT[0], gbase + U_OUT[1]]
            for ti, (c0, c1) in enumerate(tts):
                w = c1 - c0
                for o_ in range(8):
                    go = gos[o_ // 4]
                    s_o = ring[go % NSLOT]
                    wc = slice(128 * (o_ % 4), 128 * (o_ % 4) + 128)
                    b1 = mmbank()
                    mm(ps[b1][:, 0:w], [(s_o[:, k, wc], merged[:, k, c0:c1]) for k in range(8)], [*slotkey(go), ('merged', ti)], [('ps', b1)])
                    stt(xres[:, o_, c0:c1], xres[:, o_, c0:c1], ALPHA, ps[b1][:, 0:w], ALU.mult, ALU.add, [('ps', b1), ('xres', ti)], [('xres', ti)])
                    ln_row_stage(ti, c0, c1, o_)
                layer_norm_tile(ti, c0, c1, 0, 8)
            ring_release(gos[0]); ring_release(gos[1])

            _stage('ln1')
            S.transfer(keysM + keysC + keysB, [('hT', i) for i in range(nt)] + [('silu', 0), ('silu', 1)])
            set_banks([0, 1, 2, 3, 4, 5, 6, 7])
            srot = Rot(silu_t)
            for i_ in range(11):
                gf = gbase + U_F1[i_]
                s_f = ring[gf % NSLOT]
                for ti, (c0, c1) in enumerate(tts):
                    for e_ in range(2):
                        n = 2 * i_ + e_
                        w = c1 - c0
                        b1 = mmbank(); b2 = mmbank()
                        mm(ps[b1][:, 0:w], [(s_f[:, k, 256 * e_:256 * e_ + 128], xT[:, k, c0:c1]) for k in range(8)], [*slotkey(gf), ('xT', ti)], [('ps', b1)])
                        mm(ps[b2][:, 0:w], [(s_f[:, k, 256 * e_ + 128:256 * e_ + 256], xT[:, k, c0:c1]) for k in range(8)], [*slotkey(gf), ('xT', ti)], [('ps', b2)])
                        sli, sl_ = srot.nxt()
                        act(sl_[:, 0:w], ps[b1][:, 0:w], AF.Silu, [('ps', b1)], [('silu', sli)])
                        tt(hT[:, n, c0:c1], ps[b2][:, 0:w], sl_[:, 0:w], ALU.mult, [('ps', b2), ('silu', sli)], [('hT', ti)])
                ring_release(gf)
            set_banks([0, 1, 2, 3])
            for og in range(2):
                gs = [gbase + u for u in U_F2[og]]
                ss = [ring[g % NSLOT] for g in gs]
                for ti, (c0, c1) in enumerate(tts):
                    w = c1 - c0
                    if og == 1:
                        for r_ in range(4):
                            ln_row_stage(ti, c0, c1, r_)
                    for oo in range(4):
                        o_ = 4 * og + oo
                        wc = slice(128 * oo, 128 * oo + 128)
                        b1 = mmbank()
                        pairs = [(ss[k // 8][:, k % 8, wc], hT[:, k, c0:c1]) for k in range(22)]
                        mm(ps[b1][:, 0:w], pairs, [k_ for g in gs for k_ in slotkey(g)] + [('hT', ti)], [('ps', b1)])
                        stt(xres[:, o_, c0:c1], xres[:, o_, c0:c1], ALPHA, ps[b1][:, 0:w], ALU.mult, ALU.add, [('ps', b1), ('xres', ti)], [('xres', ti)])
                        if og == 1:
                            ln_row_stage(ti, c0, c1, o_)
                    if og == 1:
                        layer_norm_tile(ti, c0, c1, 16, 24)
                for g in gs:
                    ring_release(g)

            _stage('layer')
        if samp:
            store_tok_tile(ys[:, :], 64, 0, 0)
        for j, c0 in enumerate(blocks):
            gj = j + (0 if hh == 0 else 8)
            store_tok_tile(yp[128 * gj:128 * gj + 128, :], 128, c0, tile_of(c0, tts))


def _unit(wcols):
    return np.ascontiguousarray(wcols.reshape(8, 128, 512).transpose(1, 0, 2))


def _rot_cols(w64):
    return np.concatenate([w64[:, 32:64], w64[:, 0:32]], axis=1)


def _build_units(inp):
    w_in = inp['w_in']; w_ap = inp['w_attn_proj']; w_lp = inp['w_lru_proj']; w_out = inp['w_out']
    wa = inp['lru_wa']; wx = inp['lru_wx']; wf1 = inp['w_ffn_in']; wf2 = inp['w_ffn_out']
    wu = np.zeros((NL, NU, 128, 8, 512), np.float32)
    z128 = np.zeros((1024, 128), np.float32)
    for l in range(NL):
        W = w_in[l]
        q = W[:, 0:512]; k = W[:, 512:640]; v = W[:, 640:768]
        xr = W[:, 768:1792]; gt = W[:, 1792:2816]; ga = W[:, 2816:3840]; gl = W[:, 3840:4864]
        qb = [np.concatenate([q[:, 64 * r:64 * r + 64], q[:, 64 * (4 + r):64 * (4 + r) + 64]], axis=1) for r in range(4)]
        qrb = [np.concatenate([_rot_cols(q[:, 64 * r:64 * r + 64]), _rot_cols(q[:, 64 * (4 + r):64 * (4 + r) + 64])], axis=1) for r in range(4)]
        wu[l, U_Q] = _unit(np.concatenate(qb, axis=1))
        wu[l, U_QR] = _unit(np.concatenate(qrb, axis=1))
        kr = np.concatenate([_rot_cols(k[:, 0:64]), _rot_cols(k[:, 64:128])], axis=1)
        wu[l, U_KV] = _unit(np.concatenate([k, kr, v, z128], axis=1))
        for n in range(8):
            wu[l, U_LRU][:, n, 0:128] = wa[l, n]
            wu[l, U_LRU][:, n, 128:256] = wx[l, n]
        for i in range(4):
            wu[l, U_XG[i]] = _unit(np.concatenate([xr[:, 256 * i:256 * i + 128], gt[:, 256 * i:256 * i + 128],
                                                   xr[:, 256 * i + 128:256 * i + 256], gt[:, 256 * i + 128:256 * i + 256]], axis=1))
        for cg in range(2):
            wu[l, U_GA[cg]] = _unit(ga[:, 512 * cg:512 * cg + 512])
            wu[l, U_GL[cg]] = _unit(gl[:, 512 * cg:512 * cg + 512])
            for r in range(4):
                chunk = np.concatenate([w_ap[l, 64 * r:64 * r + 64], w_ap[l, 64 * (4 + r):64 * (4 + r) + 64]], axis=0)
                wu[l, U_AP[cg]][:, r, :] = chunk[:, 512 * cg:512 * cg + 512]
            wu[l, U_LP[cg]] = _unit(w_lp[l][:, 512 * cg:512 * cg + 512])
            wu[l, U_OUT[cg]] = _unit(w_out[l][:, 512 * cg:512 * cg + 512])
        for i in range(11):
            n0, n1 = 2 * i, 2 * i + 1
            wu[l, U_F1[i]] = _unit(np.concatenate([wf1[l][:, 128 * n0:128 * n0 + 128], wf1[l][:, 2816 + 128 * n0:2816 + 128 * n0 + 128],
                                                   wf1[l][:, 128 * n1:128 * n1 + 128], wf1[l][:, 2816 + 128 * n1:2816 + 128 * n1 + 128]], axis=1))
        for og in range(2):
            for kg in range(3):
                for kk in range(8):
                    kc = 8 * kg + kk
                    if kc < 22:
                        wu[l, U_F2[og][kg]][:, kk, :] = wf2[l][128 * kc:128 * kc + 128, 512 * og:512 * og + 512]
    return wu


def _fm(v):
    return np.ascontiguousarray(v.reshape(8, 128).T)


def _build_prm(inp):
    prm = np.zeros((NL, 128, 104), np.float32)
    for l in range(NL):
        prm[l, :, 0:8] = _fm(inp['ln1_g'][l]); prm[l, :, 8:16] = _fm(inp['ln1_b'][l])
        prm[l, :, 16:24] = _fm(inp['ln2_g'][l]); prm[l, :, 24:32] = _fm(inp['ln2_b'][l])
        prm[l, :, 32:40] = _fm(inp['conv_b'][l]); prm[l, :, 40:48] = _fm(inp['lru_ba'][l])
        prm[l, :, 48:56] = _fm(inp['lru_bx'][l]); prm[l, :, 56:64] = _fm(inp['lru_lambda'][l])
        cw = inp['conv_w'][l]
        prm[l, :, 64:96] = cw.reshape(4, 8, 128).transpose(2, 1, 0).reshape(128, 32)
        prm[l, :, 96:104] = np.broadcast_to(inp['attn_sinks'][l][None, :], (128, 8))
    return prm


def _build_consts():
    half = 32
    inv = (np.float32(10000.0) ** (-np.arange(half, dtype=np.float32) / np.float32(half))).astype(np.float32)
    rt = np.zeros((2, 2, 128, NCMAX), np.float32)
    p = np.arange(128)
    fi = p % 32
    sign = np.where((p % 64) < 32, -1.0, 1.0).astype(np.float32)
    for hh in range(2):
        if hh == 0:
            pos = np.concatenate([PAST + (np.arange(64) % 4), np.arange(16), 16 + np.arange(1024)])
        else:
            pos = np.concatenate([16 + 1024 + np.arange(1024), np.zeros(NCMAX - 1024)])
        ang = pos.astype(np.float32)[None, :] * inv[fi][:, None]
        ang = ang.astype(np.float32)
        rt[hh, 0] = np.cos(ang).astype(np.float32)
        rt[hh, 1] = np.sin(ang).astype(np.float32) * sign[:, None]
    cst = np.zeros((128, 2688), np.float32)
    s = np.arange(128)[:, None]; qq = np.arange(128)[None, :]
    md = (s <= qq).astype(np.float32); mp = (s > qq).astype(np.float32)
    mpmeta = np.zeros((128, 128), np.float32)
    mpmeta[0:16] = ((112 + np.arange(16))[:, None] > qq).astype(np.float32)
    cst[:, 0:512] = np.tile(md, (1, 4)); cst[:, 512:1024] = np.tile(mp, (1, 4)); cst[:, 1024:1536] = np.tile(mpmeta, (1, 4))
    col = np.arange(512)
    t_ = col % 4; b_ = (col // 4) % 16
    cst[:, 1536:2048] = (np.arange(128)[:, None] > t_[None, :]).astype(np.float32)
    sp_ = np.arange(64)
    mnm = ((sp_[:, None] // 4) == b_[None, :]) & ((sp_[:, None] % 4) <= t_[None, :])
    cst[0:64, 2048:2560] = mnm.astype(np.float32)
    cst[:, 2560:2688] = np.eye(128, dtype=np.float32)
    return rt, cst


def kernel(**inp):
    inp = {k: np.asarray(v) for k, v in inp.items()}
    n = 8
    nc = build_program()
    wu = _build_units(inp)
    prm = _build_prm(inp)
    rt, cst = _build_consts()
    in_maps = []
    for c in range(n):
        sl = slice(16 * c, 16 * c + 16)
        in_maps.append(dict(
            xp=np.ascontiguousarray(inp['x_prompt'][c]),
            xs=np.ascontiguousarray(inp['x_sample'][sl].reshape(64, D)),
            meta=np.ascontiguousarray(inp['meta_tokens']),
            ck=np.ascontiguousarray(inp['cache_win_k'][:, sl].reshape(NL, 16, 128, 128)),
            cv=np.ascontiguousarray(inp['cache_win_v'][:, sl].reshape(NL, 16, 128, 128)),
            sconv=np.ascontiguousarray(inp['state_conv'][:, sl].reshape(NL, 48, D)),
            slru=np.ascontiguousarray(inp['state_lru'][:, sl]),
            wu=wu, prm=prm, rtab=rt, cst=cst,
        ))
    res = run_bass_kernel_spmd(nc, in_maps, core_ids=list(range(n)))
    R = res.results
    y_prompt = np.stack([R[c]['yp'] for c in range(n)], axis=0)
    y_sample = np.concatenate([R[c]['ys'].reshape(16, 4, D) for c in range(n)], axis=0)
    wkp = np.stack([R[c]['wkp'].reshape(NL, 128, 2, 64) for c in range(n)], axis=1)
    wvp = np.stack([R[c]['wvp'].reshape(NL, 128, 2, 64) for c in range(n)], axis=1)
    cvp = np.stack([R[c]['cvp'] for c in range(n)], axis=1)
    lrp = np.stack([R[c]['lrp'].reshape(NL, D) for c in range(n)], axis=1)
    wks = np.concatenate([R[c]['wks'].reshape(NL, 16, 128, 2, 64) for c in range(n)], axis=1)
    wvs = np.concatenate([R[c]['wvs'].reshape(NL, 16, 128, 2, 64) for c in range(n)], axis=1)
    cvs = np.concatenate([R[c]['cvs'].reshape(NL, 16, 3, D) for c in range(n)], axis=1)
    lrs = np.concatenate([R[c]['lrs'] for c in range(n)], axis=1)
    f = lambda a: np.ascontiguousarray(a, dtype=np.float32)
    return (f(y_prompt), f(y_sample), f(wkp), f(wvp), f(cvp), f(lrp), f(wks), f(wvs), f(cvs), f(lrs))
```

```python
import numpy as np
import concourse.bass as bass
import concourse.mybir as mybir
from concourse.bass_utils import run_bass_kernel_spmd

F32 = mybir.dt.float32
BF16 = mybir.dt.bfloat16
AF = mybir.ActivationFunctionType
ALU = mybir.AluOpType

NL = 4
D = 1024
NSLOT = 5
NU = 35
ALPHA = float((2 * NL) ** 0.25)
LN_EPS = 1e-5
PAST = 8192

U_Q, U_QR, U_KV, U_LRU = 0, 1, 2, 3
U_XG = [4, 5, 6, 7]
U_GA = [8, 12]
U_AP = [9, 13]
U_GL = [10, 14]
U_LP = [11, 15]
U_OUT = [16, 17]
U_F1 = list(range(18, 29))
U_F2 = [[29, 30, 31], [32, 33, 34]]

HALVES = [
    dict(NC=1104, tts=[(0, 80), (80, 592), (592, 1104)], blocks=[80 + 128 * j for j in range(8)], samp=True, seq0=64),
    dict(NC=1024, tts=[(0, 512), (512, 1024)], blocks=[128 * j for j in range(8)], samp=False, seq0=0),
]
NCMAX = 1104


class Sched:
    def __init__(self, nc):
        self.nc = nc
        self.E = {'pe': nc.tensor, 'act': nc.scalar, 'dve': nc.vector, 'pool': nc.gpsimd, 'sp': nc.sync}
        self.comp = ('pe', 'act', 'dve', 'pool')
        self.csem = {e: nc.alloc_semaphore('c_' + e) for e in self.comp}
        self.ccnt = {e: 0 for e in self.comp}
        self.dsem = {}
        self.dcnt = {}
        self.waited = {}
        self.lastw = {}
        self.readers = {}

    def _wait(self, eng, tok):
        name, sem, val, src = tok
        if src == eng and (eng == 'pe' or SELFWAIT[0] == 0 or (SELFWAIT[0] == 2 and eng == 'act')
                           or (SELFWAIT[0] == 3 and eng == 'dve')):
            return
        key = (eng, name)
        if self.waited.get(key, 0) >= val:
            return
        self.waited[key] = val
        self.E[eng].wait_ge(sem, val)

    def _deps(self, eng, reads, writes):
        for k in list(reads) + list(writes):
            t = self.lastw.get(k)
            if t is not None:
                if isinstance(t, list):
                    for t_ in t:
                        self._wait(eng, t_)
                else:
                    self._wait(eng, t)
        for k in writes:
            for t in self.readers.get(k, {}).values():
                self._wait(eng, t)

    def _toks(self, k):
        out = []
        t = self.lastw.get(k)
        if t is not None:
            out.extend(t if isinstance(t, list) else [t])
        out.extend(self.readers.get(k, {}).values())
        return out

    def transfer(self, old_keys, new_keys):
        best = {}
        for k in list(old_keys) + list(new_keys):
            for t in self._toks(k):
                if t[0] not in best or best[t[0]][2] < t[2]:
                    best[t[0]] = t
        for nk in new_keys:
            self.lastw[nk] = list(best.values())
            self.readers[nk] = {}

    def _commit(self, tok, reads, writes):
        for k in writes:
            self.lastw[k] = tok
            self.readers[k] = {}
        for k in reads:
            self.readers.setdefault(k, {})[tok[0]] = tok

    @staticmethod
    def _excl(reads, writes):
        r = [k for k in reads if not (isinstance(k, tuple) and k[0] == 'ps')]
        w = list(writes) + [k for k in reads if isinstance(k, tuple) and k[0] == 'ps']
        return r, w

    def op(self, eng, fn, reads=(), writes=()):
        reads, writes = self._excl(reads, writes)
        self._deps(eng, reads, writes)
        ins = fn(self.E[eng])
        self.ccnt[eng] += 1
        ins.then_inc(self.csem[eng], 1)
        tok = ('c_' + eng, self.csem[eng], self.ccnt[eng], eng)
        self._commit(tok, reads, writes)

    def dma(self, q, out, in_, semkey, reads=(), writes=()):
        self._deps(q, reads, writes)
        if semkey not in self.dsem:
            self.dsem[semkey] = self.nc.alloc_semaphore('d_' + semkey)
            self.dcnt[semkey] = 0
        ins = self.E[q].dma_start(out=out, in_=in_)
        self.dcnt[semkey] += 16
        ins.then_inc(self.dsem[semkey], 16)
        tok = ('d_' + semkey, self.dsem[semkey], self.dcnt[semkey], 'dma')
        self._commit(tok, reads, writes)

    def barrier(self, engines=('pe', 'act', 'dve', 'pool', 'sp')):
        for e in engines:
            for f in self.comp:
                if f != e and self.ccnt[f] > 0:
                    self._wait(e, ('c_' + f, self.csem[f], self.ccnt[f], f))

    def finish(self):
        for k, sem in self.dsem.items():
            self._wait('sp', ('d_' + k, sem, self.dcnt[k], 'dma'))
        for f in self.comp:
            if self.ccnt[f] > 0:
                self._wait('sp', ('c_' + f, self.csem[f], self.ccnt[f], f))


class _StopBuild(Exception):
    pass


STOP = [None]
SELFWAIT = [1]


def _stage(name):
    if STOP[0] is not None and STOP[0] == name:
        raise _StopBuild()


def build_program():
    nc = bass.Bass("TRN2", target_bir_lowering=False)
    S = Sched(nc)
    try:
        _build_body(nc, S)
    except _StopBuild:
        pass
    S.finish()
    return nc


def _build_body(nc, S):

    def din(name, shape):
        return nc.dram_tensor(name, shape, F32, kind="ExternalInput").ap()

    def dout(name, shape):
        return nc.dram_tensor(name, shape, F32, kind="ExternalOutput").ap()

    xp = din("xp", [2048, D]); xs = din("xs", [64, D]); meta = din("meta", [16, D])
    ck = din("ck", [NL, 16, 128, 128]); cv = din("cv", [NL, 16, 128, 128])
    sconv = din("sconv", [NL, 48, D]); slru = din("slru", [NL, 16, D])
    wu = din("wu", [NL, NU, 128, 8, 512])
    prm = din("prm", [NL, 128, 104])
    rtab = din("rtab", [2, 2, 128, NCMAX])
    cst = din("cst", [128, 2688])
    yp = dout("yp", [2048, D]); ys = dout("ys", [64, D])
    wkp = dout("wkp", [NL, 128, 128]); wvp = dout("wvp", [NL, 128, 128])
    cvp = dout("cvp", [NL, 3, D]); lrp = dout("lrp", [NL, 8, 128])
    wks = dout("wks", [NL, 16, 128, 128]); wvs = dout("wvs", [NL, 16, 128, 128])
    cvs = dout("cvs", [NL, 48, D]); lrs = dout("lrs", [NL, 16, D])

    def sb(name, shape, dt):
        return nc.alloc_sbuf_tensor(name, shape, dt)

    xres = sb("xres", [128, 8, NCMAX], F32)
    xT = sb("xT", [128, 8, NCMAX], BF16)
    ring = [sb(f"ring{i}", [128, 8, 512], BF16) for i in range(NSLOT)]
    cosT = sb("cosT", [128, NCMAX], F32)
    sinT = sb("sinT", [128, NCMAX], F32)
    identF = sb("identF", [128, 128], F32)
    mdiag = sb("mdiag", [128, 4, 128], BF16)
    mprev = sb("mprev", [128, 4, 128], BF16)
    mpm = sb("mpm", [128, 4, 128], BF16)
    mc = sb("mc", [128, 512], BF16)
    mn = sb("mn", [128, 512], BF16)
    onesB = sb("onesB", [128, 128], BF16)
    halfF = sb("halfF", [128, 512], F32)
    lruw = sb("lruw", [128, 8, 256], BF16)
    pl = sb("pl", [128, 104], F32)
    es8 = sb("es8", [128, 8], F32)
    nsp8 = sb("nsp8", [128, 8], F32)
    nsp16 = sb("nsp16", [128, 8], F32)
    sptmp = sb("sptmp", [128, 8], F32)
    es_tile = sb("es_tile", [128, 4, 128], F32)
    kcar = sb("kcar", [128, NL, 128], BF16)
    vcar = sb("vcar", [128, NL, 2, 128], BF16)
    ccar = sb("ccar", [128, NL, 8, 3], F32)
    hcar = sb("hcar", [128, NL, 8], F32)
    cs_s = sb("cs_s", [128, 8, 48], F32)
    hs_s = sb("hs_s", [128, 8, 16], F32)
    stg = [sb(f"stg{i}", [128, 1024], F32) for i in range(2)]
    hb16 = sb("hb16", [128, 16], F32)
    hnsp8 = sb("hnsp8", [128, 8], F32)
    last = sb("lastperm", [128, 8], F32)
    A0 = (nc.lookup_mloc(last).addr + 32 + 63) // 64 * 64
    assert A0 + 80512 <= nc.SBUF_PARTITION_SIZE_BYTES, (A0,)

    def at(name, shape, dt, off):
        return nc.alloc_sbuf_tensor_at(name, shape, dt, offset=A0 + off)

    attnT = at("attnT", [128, 4, NCMAX], BF16, 0)
    rec = at("rec", [128, 8, NCMAX], BF16, 8832)
    merged = at("merged", [128, 8, NCMAX], BF16, 26496)
    qT = at("qT", [128, 4, NCMAX], BF16, 8832)
    kT = at("kT", [128, 128 + NCMAX], BF16, 17664)
    vaug = at("vaug", [128, 11, 2, 128], BF16, 20160)
    Pa = [at("Pa0", [128, 512], BF16, 25792), at("Pa1", [128, 512], BF16, 68032)]
    Pb = [at("Pb0", [128, 512], BF16, 26816), at("Pb1", [128, 512], BF16, 69056)]
    dsb = [at("dsb0", [128, 512], F32, 27840), at("dsb1", [128, 512], F32, 70080)]
    rden = [at("rden0", [128, 512], F32, 29888), at("rden1", [128, 512], F32, 72128)]
    kc_raw = at("kc_raw", [128, 16, 128], F32, 31936)
    kcT = at("kcT", [128, 16, 128], BF16, 40128)
    vc_aug = at("vc_aug", [128, 16, 2, 128], BF16, 44224)
    Pc = at("Pc", [128, 512], BF16, 52416)
    Pn = at("Pn", [128, 512], BF16, 53440)
    kf32 = at("kf32", [128, 192], F32, 54464)
    vtmp = at("vtmp", [128, 128], F32, 55232)
    ropeA = [at("ropeA0", [128, 512], F32, 55744), at("ropeA1", [128, 512], F32, 61888)]
    ropeB = [at("ropeB0", [128, 512], F32, 57792), at("ropeB1", [128, 512], F32, 63936)]
    ropeC = [at("ropeC0", [128, 512], F32, 59840), at("ropeC1", [128, 512], F32, 65984)]
    cstF = at("cstF", [128, 2688], F32, 61888)
    LB = 44160
    xrow = [at("xrow0", [128, 3 + 1040], F32, 26496), at("xrow1", [128, 3 + 1040], F32, 30688)]
    hrow = [at("hrow0", [128, 1 + 1040], F32, 34880), at("hrow1", [128, 1 + 1040], F32, 39072)]
    hs = at("hs", [128, 16, 4], F32, 43264)
    h0_s = at("h0_s", [128, 8, 16], F32, 43520)
    asets = []
    for i_ in range(4):
        o_ = LB + 5120 * i_
        asets.append(dict(xc=at(f"xc{i_}", [128, 512], F32, o_), xcb=at(f"xcb{i_}", [128, 512], BF16, o_ + 2048),
                          gg=at(f"gg{i_}", [128, 512], F32, o_ + 3072)))
    bsets = []
    for i_ in range(2):
        o_ = LB + 20480 + 6144 * i_
        bsets.append(dict(ii=at(f"ii{i_}", [128, 512], F32, o_),
                          aa=at(f"aa{i_}", [128, 512], F32, o_ + 2048), a2=at(f"a2{i_}", [128, 512], F32, o_ + 4096)))
    xe_s = at("xe_s", [128, 8, 16, 7], F32, 76928)
    cset = []
    for i_ in range(2):
        o_ = LB + 8192 * i_
        cset.append(dict(sga=at(f"sga{i_}", [128, 512], F32, o_), sgl=at(f"sgl{i_}", [128, 512], F32, o_ + 2048),
                         m1=at(f"m1{i_}", [128, 512], F32, o_ + 4096), m2=at(f"m2{i_}", [128, 512], F32, o_ + 6144)))
    zb = at("zb", [128, 8, 512], BF16, 0)
    sq = at("sq", [128, 8, 512], BF16, 8192)
    mean = at("mean", [128, 512], F32, 16384)
    msq = at("msq", [128, 512], F32, 18432)
    rstd = at("rstd", [128, 512], F32, 20480)
    t1 = [at("t1a", [128, 512], F32, 22528), at("t1b", [128, 512], F32, 24576 - 128)]
    hT = at("hT", [128, 22, NCMAX], BF16, 26496)
    silu_t = [at("silu0", [128, 512], F32, 75072), at("silu1", [128, 512], F32, 77120)]

    ps = [nc.alloc_psum_tensor(f"ps{i}", [128, 512], F32) for i in range(8)]
    mmctr = [0]
    mmpool = [[0, 1, 2, 3]]

    def set_banks(lst):
        mmpool[0] = list(lst)

    def mmbank():
        b = mmpool[0][mmctr[0] % len(mmpool[0])]
        mmctr[0] += 1
        return b

    class Rot:
        def __init__(self, items):
            self.items = items
            self.i = -1

        def nxt(self):
            self.i = (self.i + 1) % len(self.items)
            return self.i, self.items[self.i]

    def mm(out_ap, pairs, reads, writes):
        def fn(pe):
            n = len(pairs)
            ins = None
            for i, (l, r) in enumerate(pairs):
                ins = pe.matmul(out_ap, lhsT=l, rhs=r, start=(i == 0), stop=(i == n - 1))
            return ins
        S.op('pe', fn, reads, writes)

    def tp(out_ap, in_ap, n, reads, writes):
        S.op('pe', lambda pe: pe.transpose(out_ap, in_ap, identF[0:n, 0:n]), reads, writes)

    def act(out, in_, func, reads, writes, bias=None, scale=None):
        kw = {}
        if func == AF.Copy and (bias is not None or scale is not None):
            func = AF.Identity
        if bias is not None:
            kw['bias'] = bias
        if scale is not None:
            kw['scale'] = scale
        S.op('act', lambda e: e.activation(out=out, in_=in_, func=func, **kw), reads, writes)

    def tt(out, in0, in1, op, reads, writes, eng='dve'):
        S.op(eng, lambda e: e.tensor_tensor(out=out, in0=in0, in1=in1, op=op), reads, writes)

    def ts(out, in0, s1, s2, op0, op1, reads, writes, eng='dve'):
        if s2 is None:
            S.op(eng, lambda e: e.tensor_scalar(out=out, in0=in0, scalar1=s1, scalar2=None, op0=op0), reads, writes)
        else:
            S.op(eng, lambda e: e.tensor_scalar(out=out, in0=in0, scalar1=s1, scalar2=s2, op0=op0, op1=op1), reads, writes)

    def stt(out, in0, sc, in1, op0, op1, reads, writes, eng='dve'):
        S.op(eng, lambda e: e.scalar_tensor_tensor(out=out, in0=in0, scalar=sc, in1=in1, op0=op0, op1=op1), reads, writes)

    def cp(out, in_, reads, writes, eng='dve'):
        S.op(eng, lambda e: e.tensor_copy(out=out, in_=in_), reads, writes)

    def ms(ap, val, writes, eng='dve'):
        S.op(eng, lambda e: e.memset(ap, val), (), writes)

    rs = dict(issued=0, released=-1)
    total_units = 2 * NL * NU

    def ring_prefetch():
        while rs['issued'] < total_units and rs['issued'] - NSLOT <= rs['released']:
            g = rs['issued']
            l = (g // NU) % NL
            u = g % NU
            s = g % NSLOT
            for hf in range(2):
                S.dma('pool', ring[s][:, 4 * hf:4 * hf + 4, :], wu[l, u, :, 4 * hf:4 * hf + 4, :], f"ring{s}_{hf}", reads=(), writes=[('slot', s, hf)])
            rs['issued'] += 1

    def ring_release(g):
        rs['released'] = max(rs['released'], g)
        ring_prefetch()

    S.dma('sp', cstF[:, :], cst[:, :], 'cst', (), ['cstF'])
    cp(mdiag[:, :, :], cstF[:, 0:512].rearrange("p (r q) -> p r q", r=4), ['cstF'], ['masks'])
    cp(mprev[:, :, :], cstF[:, 512:1024].rearrange("p (r q) -> p r q", r=4), ['cstF'], ['masks'])
    cp(mpm[:, :, :], cstF[:, 1024:1536].rearrange("p (r q) -> p r q", r=4), ['cstF'], ['masks'])
    cp(mc[:, :], cstF[:, 1536:2048], ['cstF'], ['masks'])
    cp(mn[:, :], cstF[:, 2048:2560], ['cstF'], ['masks'])
    cp(identF[:, :], cstF[:, 2560:2688], ['cstF'], ['ident'])
    ms(onesB[:, :], 1.0, ['ones'])
    ms(halfF[:, :], 0.5, ['ones'])
    ring_prefetch()
    _stage('setup')

    stg_i = [0]

    def next_stg():
        i = stg_i[0] % 2
        stg_i[0] += 1
        return i

    def load_x_tile(src_ap, nrows, col0, ti):
        si = next_stg()
        S.dma('sp', stg[si][0:nrows, :], src_ap, f"stg{si}", (), [('stg', si)])
        for hb in range(2):
            bank = 6 + hb
            for kk in range(4):
                k = 4 * hb + kk
                tp(ps[bank][:, 128 * kk:128 * kk + nrows], stg[si][0:nrows, 128 * k:128 * k + 128], nrows,
                   [('stg', si), 'ident'], [('ps', bank)])
            src = ps[bank][:, :].rearrange("p (a b) -> p a b", a=4)[:, :, 0:nrows]
            act(xres[:, 4 * hb:4 * hb + 4, col0:col0 + nrows], src, AF.Copy, [('ps', bank)], [('xres', ti)])
            cp(xT[:, 4 * hb:4 * hb + 4, col0:col0 + nrows], src, [('ps', bank)], [('xT', ti)])

    def store_tok_tile(dst_ap, nrows, col0, ti):
        si = next_stg()
        for hb in range(2):
            bank = 6 + hb
            for kk in range(4):
                k = 4 * hb + kk
                tp(ps[bank][0:nrows, 128 * kk:128 * kk + 128], xres[:, k, col0:col0 + nrows], 128,
                   [('xres', ti), 'ident'], [('ps', bank)])
            act(stg[si][0:nrows, 512 * hb:512 * hb + 512], ps[bank][0:nrows, :], AF.Copy, [('ps', bank)], [('stg', si)])
        S.dma('sp', dst_ap, stg[si][0:nrows, :], f"stg{si}", [('stg', si)], ())

    def tile_of(c, tts):
        for i, (a, b) in enumerate(tts):
            if a <= c < b:
                return i
        raise ValueError

    t1bufs = [at(f"t1z{i}", [128, 512], F32, 2048 * i) for i in range(4)]
    t1keys = [('t1', i) for i in range(4)]
    t1r = Rot(list(zip(t1bufs, t1keys)))

    ZBK = [('zb', r) for r in range(8)]
    SQK = [('sq', r) for r in range(8)]

    def ln_row_stage(ti, c0, c1, r):
        w = c1 - c0
        cp(zb[:, r, 0:w], xres[:, r, c0:c1], [('xres', ti)], [('zb', r), ('t1', r // 2)])
        act(sq[:, r, 0:w], xres[:, r, c0:c1], AF.Square, [('xres', ti)], [('sq', r)])

    def layer_norm_tile(ti, c0, c1, gcol, bcol):
        w = c1 - c0
        mm(ps[4][:, 0:w], [(onesB[:, :], zb[:, r, 0:w]) for r in range(8)], ZBK + ['ones'], [('ps', 4)])
        mm(ps[5][:, 0:w], [(onesB[:, :], sq[:, r, 0:w]) for r in range(8)], SQK + ['ones'], [('ps', 5)])
        act(mean[:, 0:w], ps[4][:, 0:w], AF.Copy, [('ps', 4)], ['mean'], scale=1.0 / D)
        tt(msq[:, 0:w], mean[:, 0:w], mean[:, 0:w], ALU.mult, ['mean'], ['msq'])
        stt(rstd[:, 0:w], ps[5][:, 0:w], 1.0 / D, msq[:, 0:w], ALU.mult, ALU.subtract, [('ps', 5), 'msq'], ['rstd'])
        ts(rstd[:, 0:w], rstd[:, 0:w], LN_EPS, None, ALU.add, ALU.bypass, ['rstd'], ['rstd'])
        act(rstd[:, 0:w], rstd[:, 0:w], AF.Ln, ['rstd'], ['rstd'])
        act(rstd[:, 0:w], rstd[:, 0:w], AF.Exp, ['rstd'], ['rstd'], scale=-0.5)
        for r in range(8 + 2):
            if r < 8:
                _, (tb, tk) = t1r.nxt()
                tt(tb[:, 0:w], xres[:, r, c0:c1], mean[:, 0:w], ALU.subtract, [('xres', ti), 'mean'], [tk])
                tt(tb[:, 0:w], tb[:, 0:w], rstd[:, 0:w], ALU.mult, [tk, 'rstd'], [tk], eng='pool')
                act(xres[:, r, c0:c1], tb[:, 0:w], AF.Identity, [tk, 'pl'], [('xres', ti, r)],
                    bias=pl[:, bcol + r:bcol + r + 1], scale=pl[:, gcol + r:gcol + r + 1])
            if r - 2 >= 0:
                rr_ = r - 2
                cp(xT[:, rr_, c0:c1], xres[:, rr_, c0:c1], [('xres', ti, rr_)], [('xT', ti)])
        S.transfer([('xres', ti, r) for r in range(8)], [('xres', ti)])

    for hh, H in enumerate(HALVES):
        NC_ = H['NC']; tts = H['tts']; blocks = H['blocks']; samp = H['samp']; seq0 = H['seq0']
        nt = len(tts)
        xT_all = [('xT', i) for i in range(nt)]
        S.barrier()
        S.dma('sp', cosT[:, 0:NC_], rtab[hh, 0, :, 0:NC_], 'rtab', (), ['rtab'])
        S.dma('sp', sinT[:, 0:NC_], rtab[hh, 1, :, 0:NC_], 'rtab', (), ['rtab'])
        _stage('rtab')
        if samp:
            load_x_tile(xs[:, :], 64, 0, 0)
            _stage('x0')
            load_x_tile(meta[:, :], 16, 64, 0)
            _stage('x1')
        for j, c0 in enumerate(blocks):
            gj = j + (0 if hh == 0 else 8)
            load_x_tile(xp[128 * gj:128 * gj + 128, :], 128, c0, tile_of(c0, tts))

        _stage('xload')
        for l in range(NL):
            gbase = (hh * NL + l) * NU
            if l == 0:
                S.barrier()
            else:
                oldk = ([('hT', i) for i in range(nt)] + [('silu', 0), ('silu', 1)] + [('zb', r_) for r_ in range(8)] + [('sq', r_) for r_ in range(8)] + ['mean', 'msq', 'rstd', 't1a'] + [('t1', i) for i in range(4)]
                        + [('merged', i) for i in range(nt)])
                newk = ([('q', i) for i in range(nt)] + [('k', i) for i in range(-1, nt)] + ['vaug', 'kc_raw', 'kcT', 'vc_aug', 'Pc', 'Pn', 'kf32']
                        + [(nm, i) for nm in ('ropeA', 'ropeB', 'ropeC', 'Pa', 'Pb', 'dsb', 'rden') for i in range(2)]
                        + [('attnT', 0), ('attnT', 1)])
                S.transfer(oldk, newk)
            S.dma('sp', pl[:, :], prm[l], 'prm', (), ['pl'])
            act(es8[:, :], pl[:, 96:104], AF.Exp, ['pl'], ['es8'])
            for r in range(4):
                ts(es_tile[0:64, r, :], halfF[0:64, 0:128], es8[0:64, 4 + r:5 + r], 2.0, ALU.mult, ALU.mult, ['es8', 'ones'], ['es'])
                ts(es_tile[64:128, r, :], halfF[64:128, 0:128], es8[64:128, r:r + 1], 2.0, ALU.mult, ALU.mult, ['es8', 'ones'], ['es'])
            act(sptmp[:, :], pl[:, 56:64], AF.Exp, ['pl'], ['sp'], scale=-1.0)
            act(sptmp[:, :], sptmp[:, :], AF.Ln, ['sp'], ['sp'], bias=1.0)
            ts(nsp8[:, :], sptmp[:, :], -8.0, None, ALU.mult, ALU.bypass, ['sp'], ['nsp8'])
            ts(nsp16[:, :], sptmp[:, :], -16.0, None, ALU.mult, ALU.bypass, ['sp'], ['nsp8'])
            ts(hnsp8[:, :], sptmp[:, :], -4.0, None, ALU.mult, ALU.bypass, ['sp'], ['nsp8'])
            ts(hb16[:, :], pl[:, 40:56], 0.5, None, ALU.mult, ALU.bypass, ['pl'], ['nsp8'])

            _stage('params')
            if samp:
                S.dma('sp', wks[l, :, 0:124, :], ck[l, :, 4:128, :], 'cpy', (), ())
                S.dma('sp', wvs[l, :, 0:124, :], cv[l, :, 4:128, :], 'cpy', (), ())
                S.dma('sp', kc_raw[:, :, :], ck[l].rearrange("b s c -> s b c"), 'kc', (), ['kc_raw'])
            ms(vaug[:, :, 0, 64:128], 1.0, ['vaug'])
            ms(vaug[:, :, 1, 0:64], 1.0, ['vaug'])
            if hh == 1:
                cp(kT[:, 0:128], kcar[:, l, :], ['kcar'], [('k', -1)])
                cp(vaug[:, 0, 0, 0:64], vcar[:, l, 0, 0:64], ['vcar'], ['vaug'])
                cp(vaug[:, 0, 1, 64:128], vcar[:, l, 1, 64:128], ['vcar'], ['vaug'])

            gq = gbase + U_Q; gqr = gbase + U_QR; gkv = gbase + U_KV
            sq_ = ring[gq % NSLOT]; sqr_ = ring[gqr % NSLOT]; skv_ = ring[gkv % NSLOT]
            def slotkey(g):
                return (('slot', g % NSLOT, 0), ('slot', g % NSLOT, 1))

            set_banks([0, 1, 2, 3, 4, 5])
            rrot = Rot([0, 1])
            for ti, (c0, c1) in enumerate(tts):
                for r in range(4):
                    w = c1 - c0
                    b1 = mmbank(); b2 = mmbank()
                    ri, _ = rrot.nxt()
                    rA, rB = ropeA[ri], ropeB[ri]
                    mm(ps[b1][:, 0:w], [(sq_[:, k, 128 * r:128 * r + 128], xT[:, k, c0:c1]) for k in range(8)],
                       [*slotkey(gq), ('xT', ti)], [('ps', b1)])
                    mm(ps[b2][:, 0:w], [(sqr_[:, k, 128 * r:128 * r + 128], xT[:, k, c0:c1]) for k in range(8)],
                       [*slotkey(gqr), ('xT', ti)], [('ps', b2)])
                    tt(rA[:, 0:w], ps[b1][:, 0:w], cosT[:, c0:c1], ALU.mult, [('ps', b1), 'rtab'], [('ropeA', ri)])
                    tt(rB[:, 0:w], ps[b2][:, 0:w], sinT[:, c0:c1], ALU.mult, [('ps', b2), 'rtab'], [('ropeB', ri)])
                    tt(qT[:, r, c0:c1], rA[:, 0:w], rB[:, 0:w], ALU.add, [('ropeA', ri), ('ropeB', ri)], [('q', ti)], eng='pool')
            ring_release(gq); ring_release(gqr)
            for ti, (c0, c1) in enumerate(tts):
                w = c1 - c0
                b1 = mmbank(); b2 = mmbank()
                ri, _ = rrot.nxt()
                rA, rB, rC = ropeA[ri], ropeB[ri], ropeC[ri]
                mm(ps[b1][:, 0:w], [(skv_[:, k, 0:128], xT[:, k, c0:c1]) for k in range(8)], [*slotkey(gkv), ('xT', ti)], [('ps', b1)])
                mm(ps[b2][:, 0:w], [(skv_[:, k, 128:256], xT[:, k, c0:c1]) for k in range(8)], [*slotkey(gkv), ('xT', ti)], [('ps', b2)])
                tt(rA[:, 0:w], ps[b1][:, 0:w], cosT[:, c0:c1], ALU.mult, [('ps', b1), 'rtab'], [('ropeA', ri)])
                tt(rB[:, 0:w], ps[b2][:, 0:w], sinT[:, c0:c1], ALU.mult, [('ps', b2), 'rtab'], [('ropeB', ri)])
                tt(rC[:, 0:w], rA[:, 0:w], rB[:, 0:w], ALU.add, [('ropeA', ri), ('ropeB', ri)], [('ropeC', ri)], eng='pool')
                act(kT[:, 128 + c0:128 + c1], rC[:, 0:w], AF.Copy, [('ropeC', ri)], [('k', ti)])
                if samp and ti == 0:
                    act(kf32[:, 0:64], rC[:, 0:64], AF.Copy, [('ropeC', ri)], ['kf32'])
                if hh == 1 and ti == nt - 1:
                    act(kf32[:, 64:192], rC[:, w - 128:w], AF.Copy, [('ropeC', ri)], ['kf32'])
            vtiles = []
            if samp:
                vtiles.append((2, 0, 64, 0)); vtiles.append((1, 64, 16, 0))
            for j, c0 in enumerate(blocks):
                vtiles.append((3 + j, c0, 128, tile_of(c0, tts)))
            for (vi, c0, nrow, ti) in vtiles:
                mm(ps[7][0:nrow, 0:128], [(xT[:, k, c0:c0 + nrow], skv_[:, k, 256:384]) for k in range(8)],
                   [*slotkey(gkv), ('xT', ti)], [('ps', 7)])
                act(vaug[0:nrow, vi, 0, 0:64], ps[7][0:nrow, 0:64], AF.Copy, [('ps', 7)], ['vaug'])
                act(vaug[0:nrow, vi, 1, 64:128], ps[7][0:nrow, 64:128], AF.Copy, [('ps', 7)], ['vaug'])
                if vi == 2:
                    cp(stg[0][0:64, 0:128], ps[7][0:64, 0:128], [('ps', 7)], [('stg', 0)])
                    for b_ in range(16):
                        S.dma('sp', wvs[l, b_, 124:128, :], stg[0][4 * b_:4 * b_ + 4, 0:128], 'stg0', [('stg', 0)], ())
                if hh == 1 and vi == 10:
                    cp(stg[0][:, 0:128], ps[7][:, 0:128], [('ps', 7)], [('stg', 0)])
                    S.dma('sp', wvp[l], stg[0][:, 0:128], 'stg0', [('stg', 0)], ())
            ring_release(gkv)
            if samp:
                tp(ps[7][0:64, 0:128], kf32[:, 0:64], 128, ['kf32', 'ident'], [('ps', 7)])
                cp(stg[1][0:64, 0:128], ps[7][0:64, 0:128], [('ps', 7)], [('stg', 1)])
                for b_ in range(16):
                    S.dma('sp', wks[l, b_, 124:128, :], stg[1][4 * b_:4 * b_ + 4, 0:128], 'stg1', [('stg', 1)], ())
            if hh == 1:
                tp(ps[7][:, 0:128], kf32[:, 64:192], 128, ['kf32', 'ident'], [('ps', 7)])
                cp(stg[1][:, 0:128], ps[7][:, 0:128], [('ps', 7)], [('stg', 1)])
                S.dma('sp', wkp[l], stg[1][:, 0:128], 'stg1', [('stg', 1)], ())

            _stage('qkv')
            kall = [('k', i) for i in range(-1, nt)]
            qall = [('q', i) for i in range(nt)]

            items = []
            if samp:
                items.append((64, 16, 0, 0, None, vaug[:, 1], None))
            for j, c0 in enumerate(blocks):
                if hh == 0 and j == 0:
                    items.append((c0, 128, 16, 128 + 64, vaug[:, 1], vaug[:, 3], mpm))
                else:
                    items.append((c0, 128, 128, 128 + c0 - 128, vaug[:, 3 + j - 1] if j > 0 else vaug[:, 0], vaug[:, 3 + j], mprev))
            work = [(it, g) for it in items for g in range(2)]

            def v3(ap2):
                return ap2.rearrange("p (r q) -> p r q", r=4)

            def attn_s(wi):
                (cq0, Nq, Kp, kp0, vprev, vcur, mask_prev), g = work[wi]
                si_ = wi % 2
                bA, bB = (0, 1) if si_ == 0 else (3, 4)
                Pa_, Pb_ = Pa[si_], Pb[si_]
                kPa, kPb = ('Pa', si_), ('Pb', si_)
                pr = slice(64 * g, 64 * g + 64)
                rhs_q = qT[pr, :, cq0:cq0 + Nq]
                NN = 4 * Nq
                if Kp:
                    mm(v3(ps[bA][0:Kp, 0:NN]), [(kT[pr, kp0:kp0 + Kp], rhs_q)], kall + qall, [('ps', bA)])
                    act(Pa_[0:Kp, 0:NN], ps[bA][0:Kp, 0:NN], AF.Exp, [('ps', bA)], [kPa], scale=0.125)
                    tt(v3(Pa_[0:Kp, 0:NN]), v3(Pa_[0:Kp, 0:NN]), mask_prev[0:Kp, :, 0:Nq], ALU.mult, [kPa, 'masks'], [kPa], eng='pool')
                mm(v3(ps[bB][0:Nq, 0:NN]), [(kT[pr, 128 + cq0:128 + cq0 + Nq], rhs_q)], kall + qall, [('ps', bB)])
                act(Pb_[0:Nq, 0:NN], ps[bB][0:Nq, 0:NN], AF.Exp, [('ps', bB)], [kPb], scale=0.125)
                tt(v3(Pb_[0:Nq, 0:NN]), v3(Pb_[0:Nq, 0:NN]), mdiag[0:Nq, :, 0:Nq], ALU.mult, [kPb, 'masks'], [kPb], eng='pool')

            def attn_pv(wi):
                (cq0, Nq, Kp, kp0, vprev, vcur, mask_prev), g = work[wi]
                si_ = wi % 2
                bO = 2 if si_ == 0 else 5
                Pa_, Pb_, dsb_, rden_ = Pa[si_], Pb[si_], dsb[si_], rden[si_]
                kPa, kPb, kds, krd = ('Pa', si_), ('Pb', si_), ('dsb', si_), ('rden', si_)
                pr = slice(64 * g, 64 * g + 64)
                dn = slice(64 - 64 * g, 128 - 64 * g)
                NN = 4 * Nq
                pairs = []
                rd_ = [kPb, 'vaug']
                if Kp:
                    pairs.append((vprev[0:Kp, g, :], Pa_[0:Kp, 0:NN]))
                    rd_.append(kPa)
                pairs.append((vcur[0:Nq, g, :], Pb_[0:Nq, 0:NN]))
                mm(ps[bO][:, 0:NN], pairs, rd_, [('ps', bO)])
                tt(v3(dsb_[dn, 0:NN]), v3(ps[bO][dn, 0:NN]), es_tile[dn, :, 0:Nq], ALU.add, [('ps', bO), 'es'], [kds])
                act(rden_[dn, 0:NN], dsb_[dn, 0:NN], AF.Ln, [kds], [krd])
                act(rden_[dn, 0:NN], rden_[dn, 0:NN], AF.Exp, [krd], [krd], scale=-1.0)
                tt(attnT[pr, :, cq0:cq0 + Nq], v3(ps[bO][pr, 0:NN]), v3(rden_[dn, 0:NN]), ALU.mult,
                   [('ps', bO), krd], [('attnT', g)])

            for wi in range(len(work) + 1):
                if wi < len(work):
                    attn_s(wi)
                if wi - 1 >= 0:
                    attn_pv(wi - 1)

            _stage('attnp')
            if samp:
                for b in range(16):
                    tp(ps[7][:, 0:128], kc_raw[:, b, :], 128, ['kc_raw', 'ident'], [('ps', 7)])
                    act(kcT[:, b, :], ps[7][:, 0:128], AF.Copy, [('ps', 7)], ['kcT'])
                S.dma('sp', kc_raw[:, :, :], cv[l].rearrange("b s c -> s b c"), 'kc', ['kc_raw'], ['kc_raw'])
                ms(vc_aug[:, :, 0, 64:128], 1.0, ['vc_aug'])
                ms(vc_aug[:, :, 1, 0:64], 1.0, ['vc_aug'])
                act(vc_aug[:, :, 0, 0:64], kc_raw[:, :, 0:64], AF.Copy, ['kc_raw'], ['vc_aug'])
                act(vc_aug[:, :, 1, 64:128], kc_raw[:, :, 64:128], AF.Copy, ['kc_raw'], ['vc_aug'])

                def cols(psap, g, b):
                    return psap[:, 256 * g:256 * g + 256].rearrange("p (r b t) -> p r b t", r=4, b=16)[:, :, b, :]
                for g in range(2):
                    pr = slice(64 * g, 64 * g + 64)
                    for b in range(16):
                        S.op('pe', lambda pe, g=g, b=b, pr=pr: pe.matmul(cols(ps[4], g, b), lhsT=kcT[pr, b, :], rhs=qT[pr, :, 4 * b:4 * b + 4], start=True, stop=True),
                             ['kcT'] + qall, [('ps', 4)])
                    mm(ps[5][0:64, 256 * g:256 * g + 256].rearrange("p (r q) -> p r q", r=4),
                       [(kT[pr, 128:192], qT[pr, :, 0:64])], kall + qall, [('ps', 5)])
                act(Pc[:, :], ps[4][:, :], AF.Exp, [('ps', 4)], ['Pc'], scale=0.125)
                tt(Pc[:, :], Pc[:, :], mc[:, :], ALU.mult, ['Pc', 'masks'], ['Pc'])
                act(Pn[0:64, :], ps[5][0:64, :], AF.Exp, [('ps', 5)], ['Pn'], scale=0.125)
                tt(Pn[0:64, :], Pn[0:64, :], mn[0:64, :], ALU.mult, ['Pn', 'masks'], ['Pn'])
                for g in range(2):
                    for b in range(16):
                        def fn(pe, g=g, b=b):
                            pe.matmul(cols(ps[6], g, b), lhsT=vc_aug[:, b, g, :], rhs=cols(Pc, g, b), start=True, stop=False)
                            return pe.matmul(cols(ps[6], g, b), lhsT=vaug[0:64, 2, g, :], rhs=cols(Pn, g, b)[0:64], start=False, stop=True)
                        S.op('pe', fn, ['Pc', 'Pn', 'vc_aug', 'vaug'], [('ps', 6)])
                for g in range(2):
                    pr = slice(64 * g, 64 * g + 64)
                    dn = slice(64 - 64 * g, 128 - 64 * g)
                    cs = slice(256 * g, 256 * g + 256)
                    tt(dsb[0][dn, cs].rearrange("p (r q) -> p r q", r=4), ps[6][dn, cs].rearrange("p (r q) -> p r q", r=4),
                       es_tile[dn, :, 0:64], ALU.add, [('ps', 6), 'es'], [('dsb', 0)])
                    act(rden[0][dn, cs], dsb[0][dn, cs], AF.Ln, [('dsb', 0)], [('rden', 0)])
                    act(rden[0][dn, cs], rden[0][dn, cs], AF.Exp, [('rden', 0)], [('rden', 0)], scale=-1.0)
                    tt(attnT[pr, :, 0:64], ps[6][pr, cs].rearrange("p (r q) -> p r q", r=4),
                       rden[0][dn, cs].rearrange("p (r q) -> p r q", r=4), ALU.mult, [('ps', 6), ('rden', 0)], [('attnT', g)])
            if hh == 0:
                cp(kcar[:, l, :], kT[:, 128 + NC_ - 128:128 + NC_], kall, ['kcar'])
                cp(vcar[:, l, :, :], vaug[:, 10, :, :], ['vaug'], ['vcar'])

            _stage('attns')
            oldA = ([(nm, i) for nm in ('Pa', 'Pb', 'dsb', 'rden', 'ropeA', 'ropeB', 'ropeC') for i in range(2)]
                    + ['kc_raw', 'kcT', 'vc_aug', 'Pc', 'Pn', 'kf32', 'vaug']
                    + [('q', i) for i in range(nt)] + [('k', i) for i in range(-1, nt)])
            newB = ([('xrow', i) for i in range(2)] + [('hrow', i) for i in range(2)] + ['hs'] + [('h0_s', n) for n in range(8)]
                    + [(nm, i) for nm in ('xc', 'xcb', 'gg') for i in range(4)] + [(nm, i) for nm in ('ii', 'aa', 'a2') for i in range(2)]
                    + [('xe_s', n) for n in range(8)] + [('rec', i) for i in range(nt)])
            S.transfer(oldA, newB)
            glru = gbase + U_LRU
            for hf in range(2):
                S.dma('pool', lruw[:, 4 * hf:4 * hf + 4, :], wu[l, U_LRU, :, 4 * hf:4 * hf + 4, 0:256], f'lruw{hf}', (), [('lruw', hf)])
            ring_release(glru)
            L = NC_ - seq0
            set_banks([0, 1, 2, 3, 4, 5, 6])
            if samp:
                S.dma('sp', stg[0][0:48, :], sconv[l], 'stg0', (), [('stg', 0)])
                S.dma('sp', stg[1][0:16, :], slru[l], 'stg1', (), [('stg', 1)])
                for n in range(8):
                    tp(ps[7][:, 0:48], stg[0][0:48, 128 * n:128 * n + 128], 48, [('stg', 0), 'ident'], [('ps', 7)])
                    act(xe_s[:, n, :, 0:3], ps[7][:, 0:48].rearrange("p (b j) -> p b j", b=16), AF.Copy, [('ps', 7)], [('xe_s', n)])
                    tp(ps[7][:, 64:80], stg[1][0:16, 128 * n:128 * n + 128], 16, [('stg', 1), 'ident'], [('ps', 7)])
                    act(h0_s[:, n, :], ps[7][:, 64:80], AF.Copy, [('ps', 7)], [('h0_s', n)])
            steps = [(n, ti) for n in range(8) for ti in range(nt)]
            set_banks([0, 1, 2, 3, 4, 5, 6, 7])

            gelu_pending = {}

            def emit_gelu(si):
                if si in gelu_pending:
                    gg_, bgk_, w_, kgg_ = gelu_pending.pop(si)
                    act(gg_[:, 0:w_], ps[bgk_][:, 0:w_], AF.Gelu_apprx_tanh, [('ps', bgk_)], [kgg_])

            def part_a(si):
                n, ti = steps[si]
                c0, c1 = tts[ti]
                w = c1 - c0
                AS = asets[si % 4]
                xc, xcb, gg = AS['xc'], AS['xcb'], AS['gg']
                kxc, kxcb, kgg = ('xc', si % 4), ('xcb', si % 4), ('gg', si % 4)
                xr_, kxr = xrow[n % 2], ('xrow', n % 2)
                hr_, khr = hrow[n % 2], ('hrow', n % 2)
                gx = gbase + U_XG[n // 2]
                sx_ = ring[gx % NSLOT]
                cxr = (n % 2) * 256
                cgt = cxr + 128
                if ti == 0:
                    if samp:
                        ms(xr_[:, 0:3], 0.0, [kxr])
                        ms(hr_[:, 0:1], 0.0, [khr])
                    else:
                        cp(xr_[:, 0:3], ccar[:, l, n, :], [('ccar', n)], [kxr])
                        cp(hr_[:, 0:1], hcar[:, l, n:n + 1], [('hcar', n)], [khr])
                bxk = mmbank(); bgk = mmbank()
                mm(ps[bxk][:, 0:w], [(sx_[:, k, cxr:cxr + 128], xT[:, k, c0:c1]) for k in range(8)], [*slotkey(gx), ('xT', ti)], [('ps', bxk)])
                mm(ps[bgk][:, 0:w], [(sx_[:, k, cgt:cgt + 128], xT[:, k, c0:c1]) for k in range(8)], [*slotkey(gx), ('xT', ti)], [('ps', bgk)])
                sa = max(c0, seq0)
                a = sa - seq0
                Ls = c1 - sa
                o = sa - c0
                if samp and ti == 0:
                    act(xe_s[:, n, :, 3:7], ps[bxk][:, 0:64].rearrange("p (b t) -> p b t", b=16), AF.Copy, [('ps', bxk)], [('xe_s', n)])
                cp(xr_[:, 3 + a:3 + a + Ls], ps[bxk][:, o:w], [('ps', bxk)], [kxr])
                gelu_pending[si] = (gg, bgk, w, kgg)
                if samp and ti == 0:
                    xcs = xc[:, 0:64].rearrange("p (b t) -> p b t", b=16)
                    ts(xcs, xe_s[:, n, :, 0:4], pl[:, 64 + 4 * n:65 + 4 * n], pl[:, 32 + n:33 + n], ALU.mult, ALU.add, [('xe_s', n), 'pl'], [kxc])
                    for jj in range(1, 4):
                        stt(xcs, xe_s[:, n, :, jj:jj + 4], pl[:, 64 + 4 * n + jj:65 + 4 * n + jj], xcs, ALU.mult, ALU.add, [('xe_s', n), 'pl', kxc], [kxc])
                    cp(cs_s[:, n, :].rearrange("p (b j) -> p b j", b=16), xe_s[:, n, :, 4:7], [('xe_s', n)], ['cs_s'], eng='pool')
                ts(xc[:, o:w], xr_[:, a:a + Ls], pl[:, 64 + 4 * n:65 + 4 * n], pl[:, 32 + n:33 + n], ALU.mult, ALU.add, [kxr, 'pl'], [kxc])
                for jj in range(1, 4):
                    stt(xc[:, o:w], xr_[:, a + jj:a + jj + Ls], pl[:, 64 + 4 * n + jj:65 + 4 * n + jj], xc[:, o:w], ALU.mult, ALU.add, [kxr, 'pl', kxc], [kxc])
                cp(xcb[:, 0:w], xc[:, 0:w], [kxc], [kxcb])
                if ti == nt - 1:
                    cp(ccar[:, l, n, :], xr_[:, L:L + 3], [kxr], [('ccar', n)], eng='pool')

            def bcommon(si):
                n, ti = steps[si]
                c0, c1 = tts[ti]
                w = c1 - c0
                AS = asets[si % 4]
                BS = bsets[si % 2]
                sa = max(c0, seq0)
                return dict(n=n, ti=ti, c0=c0, c1=c1, w=w, xc=AS['xc'], xcb=AS['xcb'], gg=AS['gg'],
                            kxc=('xc', si % 4), kxcb=('xcb', si % 4), kgg=('gg', si % 4),
                            ii=BS['ii'], aa=BS['aa'], a2=BS['a2'], kii=('ii', si % 2), kaa=('aa', si % 2), ka2=('a2', si % 2),
                            hr_=hrow[n % 2], khr=('hrow', n % 2), sa=sa, a=sa - seq0, Ls=c1 - sa, o=sa - c0)

            def part_b1(si):
                d = bcommon(si)
                n, w, xc, xcb, ii, aa, a2 = d['n'], d['w'], d['xc'], d['xcb'], d['ii'], d['aa'], d['a2']
                kxc, kxcb, kii, kaa, ka2 = d['kxc'], d['kxcb'], d['kii'], d['kaa'], d['ka2']
                brk = mmbank(); bik = mmbank()
                mm(ps[brk][:, 0:w], [(lruw[:, n, 0:128], xcb[:, 0:w])], [('lruw', n // 4), kxcb], [('ps', brk)])
                mm(ps[bik][:, 0:w], [(lruw[:, n, 128:256], xcb[:, 0:w])], [('lruw', n // 4), kxcb], [('ps', bik)])
                act(aa[:, 0:w], ps[brk][:, 0:w], AF.Tanh, [('ps', brk), 'nsp8'], [kaa], bias=hb16[:, n:n + 1], scale=0.5)
                act(ii[:, 0:w], ps[bik][:, 0:w], AF.Tanh, [('ps', bik), 'nsp8'], [kii], bias=hb16[:, 8 + n:9 + n], scale=0.5)
                emit_gelu(si + LOOK)

            def part_b1b(si):
                d = bcommon(si)
                n, w, xc, xcb, ii, aa, a2 = d['n'], d['w'], d['xc'], d['xcb'], d['ii'], d['aa'], d['a2']
                kxc, kxcb, kii, kaa, ka2 = d['kxc'], d['kxcb'], d['kii'], d['kaa'], d['ka2']
                act(a2[:, 0:w], aa[:, 0:w], AF.Exp, [kaa, 'nsp8'], [ka2], scale=nsp8[:, n:n + 1], bias=nsp8[:, n:n + 1])
                act(aa[:, 0:w], aa[:, 0:w], AF.Exp, [kaa, 'nsp8'], [kaa], scale=hnsp8[:, n:n + 1], bias=hnsp8[:, n:n + 1])
                act(a2[:, 0:w], a2[:, 0:w], AF.Relu, [ka2], [ka2], scale=-0.25, bias=0.25)
                act(a2[:, 0:w], a2[:, 0:w], AF.Ln, [ka2], [ka2], bias=1e-30)
                act(a2[:, 0:w], a2[:, 0:w], AF.Exp, [ka2], [ka2], scale=0.5)
                stt(ii[:, 0:w], ii[:, 0:w], 1.0, xc[:, 0:w], ALU.add, ALU.mult, [kii, kxc], [kii])
                tt(ii[:, 0:w], ii[:, 0:w], a2[:, 0:w], ALU.mult, [kii, ka2], [kii], eng='pool')

            def part_b2(si):
                d = bcommon(si)
                n, ti, c1, w, gg, ii, aa = d['n'], d['ti'], d['c1'], d['w'], d['gg'], d['ii'], d['aa']
                kgg, kii, kaa, hr_, khr = d['kgg'], d['kii'], d['kaa'], d['hr_'], d['khr']
                sa, a, Ls, o = d['sa'], d['a'], d['Ls'], d['o']
                bb = ii
                kbb = kii
                if samp and ti == 0:
                    a3 = aa[:, 0:64].rearrange("p (b t) -> p b t", b=16)
                    b3 = bb[:, 0:64].rearrange("p (b t) -> p b t", b=16)
                    tt(hs[:, :, 0], a3[:, :, 0], h0_s[:, n, :], ALU.mult, [kaa, ('h0_s', n)], ['hs'])
                    tt(hs[:, :, 0], hs[:, :, 0], b3[:, :, 0], ALU.add, ['hs', kbb], ['hs'])
                    for t_ in range(1, 4):
                        tt(hs[:, :, t_], a3[:, :, t_], hs[:, :, t_ - 1], ALU.mult, [kaa, 'hs'], ['hs'])
                        tt(hs[:, :, t_], hs[:, :, t_], b3[:, :, t_], ALU.add, ['hs', kbb], ['hs'])
                    cp(hs_s[:, n, :], hs[:, :, 3], ['hs'], ['hs_s'], eng='pool')
                    tt(rec[:, n, 0:64], hs[:, :, :].rearrange("p b t -> p (b t)"), gg[:, 0:64], ALU.mult, ['hs', kgg], [('rec', ti)])
                S.op('dve', lambda e: e.tensor_tensor_scan(out=hr_[:, 1 + a:1 + a + Ls], data0=aa[:, o:w], data1=bb[:, o:w],
                                                           initial=hr_[:, a:a + 1], op0=ALU.mult, op1=ALU.add),
                     [kaa, kbb, khr], [khr])
                tt(rec[:, n, sa:c1], hr_[:, 1 + a:1 + a + Ls], gg[:, o:w], ALU.mult, [khr, kgg], [('rec', ti)], eng='pool')
                if ti == nt - 1:
                    cp(hcar[:, l, n:n + 1], hr_[:, L:L + 1], [khr], [('hcar', n)], eng='pool')
                    if n % 2 == 1:
                        ring_release(gbase + U_XG[n // 2])

            LOOK = 2
            NS = len(steps)
            assert NS % 2 == 0 and LOOK == 2
            for it in range(NS + 4):
                if it % 2 == 0:
                    for s_ in (it - 4, it - 3):
                        if 0 <= s_ < NS:
                            part_b2(s_)
                if it < NS:
                    part_a(it)
                    if it < LOOK:
                        emit_gelu(it)
                if 0 <= it - 2 < NS:
                    part_b1(it - 2)
                if it % 2 == 1:
                    for s_ in (it - 3, it - 2):
                        if 0 <= s_ < NS:
                            part_b1b(s_)
            assert not gelu_pending
            def emit_b_outputs():
                ccar_all = [('ccar', n) for n in range(8)]
                hcar_all = [('hcar', n) for n in range(8)]
                if hh == 1:
                    for hb in range(2):
                        for kk in range(4):
                            n = 4 * hb + kk
                            tp(ps[6 + hb][0:3, 128 * kk:128 * kk + 128], ccar[:, l, n, :], 128, ccar_all + ['ident'], [('ps', 6 + hb)])
                        cp(stg[0][0:3, 512 * hb:512 * hb + 512], ps[6 + hb][0:3, :], [('ps', 6 + hb)], [('stg', 0)])
                    S.dma('sp', cvp[l], stg[0][0:3, :], 'stg0', [('stg', 0)], ())
                    tp(ps[7][0:8, 0:128], hcar[:, l, :], 128, hcar_all + ['ident'], [('ps', 7)])
                    cp(stg[1][0:8, 0:128], ps[7][0:8, 0:128], [('ps', 7)], [('stg', 1)])
                    S.dma('sp', lrp[l], stg[1][0:8, 0:128], 'stg1', [('stg', 1)], ())
                else:
                    for hb in range(2):
                        for kk in range(4):
                            n = 4 * hb + kk
                            tp(ps[6 + hb][0:48, 128 * kk:128 * kk + 128], cs_s[:, n, :], 128, ['cs_s', 'ident'], [('ps', 6 + hb)])
                        cp(stg[0][0:48, 512 * hb:512 * hb + 512], ps[6 + hb][0:48, :], [('ps', 6 + hb)], [('stg', 0)])
                    S.dma('sp', cvs[l], stg[0][0:48, :], 'stg0', [('stg', 0)], ())
                    for hb in range(2):
                        for kk in range(4):
                            n = 4 * hb + kk
                            tp(ps[6 + hb][0:16, 128 * kk:128 * kk + 128], hs_s[:, n, :], 128, ['hs_s', 'ident'], [('ps', 6 + hb)])
                        cp(stg[1][0:16, 512 * hb:512 * hb + 512], ps[6 + hb][0:16, :], [('ps', 6 + hb)], [('stg', 1)])
                    S.dma('sp', lrs[l], stg[1][0:16, :], 'stg1', [('stg', 1)], ())

            _stage('lru')
            keysB = ([('xrow', i) for i in range(2)] + [('hrow', i) for i in range(2)] + ['hs'] + [('h0_s', n) for n in range(8)]
                     + [(nm, i) for nm in ('xc', 'xcb', 'gg') for i in range(4)] + [(nm, i) for nm in ('ii', 'aa', 'a2') for i in range(2)]
                     + [('xe_s', n) for n in range(8)])
            keysC = [(nm, i) for nm in ('sga', 'sgl', 'm1', 'm2') for i in range(2)]
            keysM = [('merged', i) for i in range(nt)]
            S.transfer(keysB, keysC + keysM)
            set_banks([0, 1, 2, 3, 4, 5, 6, 7])
            crot = Rot(cset)
            for cg in range(2):
                gga = gbase + U_GA[cg]; gap = gbase + U_AP[cg]; ggl = gbase + U_GL[cg]; glp = gbase + U_LP[cg]
                s_ga = ring[gga % NSLOT]; s_ap = ring[gap % NSLOT]; s_gl = ring[ggl % NSLOT]; s_lp = ring[glp % NSLOT]
                for cc in range(4):
                    c = 4 * cg + cc
                    wc = slice(128 * cc, 128 * cc + 128)
                    for ti, (c0, c1) in enumerate(tts):
                        w = c1 - c0
                        ci, CS = crot.nxt()
                        sga, sgl, m1, m2 = CS['sga'], CS['sgl'], CS['m1'], CS['m2']
                        b1 = mmbank()
                        mm(ps[b1][:, 0:w], [(s_ga[:, k, wc], xT[:, k, c0:c1]) for k in range(8)], [*slotkey(gga), ('xT', ti)], [('ps', b1)])
                        act(sga[:, 0:w], ps[b1][:, 0:w], AF.Sigmoid, [('ps', b1)], [('sga', ci)])
                        b2 = mmbank()
                        mm(ps[b2][:, 0:w], [(s_ap[:, r, wc], attnT[:, r, c0:c1]) for r in range(4)], [*slotkey(gap), ('attnT', 0), ('attnT', 1)], [('ps', b2)])
                        tt(m1[:, 0:w], ps[b2][:, 0:w], sga[:, 0:w], ALU.mult, [('ps', b2), ('sga', ci)], [('m1', ci)])
                        b3 = mmbank()
                        mm(ps[b3][:, 0:w], [(s_gl[:, k, wc], xT[:, k, c0:c1]) for k in range(8)], [*slotkey(ggl), ('xT', ti)], [('ps', b3)])
                        act(sgl[:, 0:w], ps[b3][:, 0:w], AF.Sigmoid, [('ps', b3)], [('sgl', ci)])
                        b4 = mmbank()
                        mm(ps[b4][:, 0:w], [(s_lp[:, k, wc], rec[:, k, c0:c1]) for k in range(8)], [*slotkey(glp), ('rec', ti)], [('ps', b4)])
                        tt(m2[:, 0:w], ps[b4][:, 0:w], sgl[:, 0:w], ALU.mult, [('ps', b4), ('sgl', ci)], [('m2', ci)])
                        tt(merged[:, c, c0:c1], m1[:, 0:w], m2[:, 0:w], ALU.add, [('m1', ci), ('m2', ci)], [('merged', ti)], eng='pool')
                ring_release(gga); ring_release(gap); ring_release(ggl); ring_release(glp)
                if cg == 0:
                    emit_b_outputs()

            _stage('merge')
            keysLN = [('zb', r_) for r_ in range(8)] + [('sq', r_) for r_ in range(8)] + ['mean', 'msq', 'rstd', 't1a'] + [('t1', i) for i in range(4)]
            S.transfer([('attnT', 0), ('attnT', 1)] + [('rec', i) for i in range(nt)], keysLN)
            set_banks([0, 1, 2, 3])
            gos = [gbase + U_OUT[0], gbase + U_OUT[1]]
            for ti, (c0, c1) in enumerate(tts):
                w = c1 - c0
                for o_ in range(8):
                    go = gos[o_ // 4]
                    s_o = ring[go % NSLOT]
                    wc = slice(128 * (o_ % 4), 128 * (o_ % 4) + 128)
                    b1 = mmbank()
                    mm(ps[b1][:, 0:w], [(s_o[:, k, wc], merged[:, k, c0:c1]) for k in range(8)], [*slotkey(go), ('merged', ti)], [('ps', b1)])
                    stt(xres[:, o_, c0:c1], xres[:, o_, c0:c1], ALPHA, ps[b1][:, 0:w], ALU.mult, ALU.add, [('ps', b1), ('xres', ti)], [('xres', ti)])
                    ln_row_stage(ti, c0, c1, o_)
                layer_norm_tile(ti, c0, c1, 0, 8)
            ring_release(gos[0]); ring_release(gos[1])

            _stage('ln1')
            S.transfer(keysM + keysC + keysB, [('hT', i) for i in range(nt)] + [('silu', 0), ('silu', 1)])
            set_banks([0, 1, 2, 3, 4, 5, 6, 7])
            srot = Rot(silu_t)
            for i_ in range(11):
                gf = gbase + U_F1[i_]
                s_f = ring[gf % NSLOT]
                for ti, (c0, c1) in enumerate(tts):
                    for e_ in range(2):
                        n = 2 * i_ + e_
                        w = c1 - c0
                        b1 = mmbank(); b2 = mmbank()
                        mm(ps[b1][:, 0:w], [(s_f[:, k, 256 * e_:256 * e_ + 128], xT[:, k, c0:c1]) for k in range(8)], [*slotkey(gf), ('xT', ti)], [('ps', b1)])
                        mm(ps[b2][:, 0:w], [(s_f[:, k, 256 * e_ + 128:256 * e_ + 256], xT[:, k, c0:c1]) for k in range(8)], [*slotkey(gf), ('xT', ti)], [('ps', b2)])
                        sli, sl_ = srot.nxt()
                        act(sl_[:, 0:w], ps[b1][:, 0:w], AF.Silu, [('ps', b1)], [('silu', sli)])
                        tt(hT[:, n, c0:c1], ps[b2][:, 0:w], sl_[:, 0:w], ALU.mult, [('ps', b2), ('silu', sli)], [('hT', ti)])
                ring_release(gf)
            set_banks([0, 1, 2, 3])
            for og in range(2):
                gs = [gbase + u for u in U_F2[og]]
                ss = [ring[g % NSLOT] for g in gs]
                for ti, (c0, c1) in enumerate(tts):
                    w = c1 - c0
                    if og == 1:
                        for r_ in range(4):
                            ln_row_stage(ti, c0, c1, r_)
                    for oo in range(4):
                        o_ = 4 * og + oo
                        wc = slice(128 * oo, 128 * oo + 128)
                        b1 = mmbank()
                        pairs = [(ss[k // 8][:, k % 8, wc], hT[:, k, c0:c1]) for k in range(22)]
                        mm(ps[b1][:, 0:w], pairs, [k_ for g in gs for k_ in slotkey(g)] + [('hT', ti)], [('ps', b1)])
                        stt(xres[:, o_, c0:c1], xres[:, o_, c0:c1], ALPHA, ps[b1][:, 0:w], ALU.mult, ALU.add, [('ps', b1), ('xres', ti)], [('xres', ti)])
                        if og == 1:
                            ln_row_stage(ti, c0, c1, o_)
                    if og == 1:
                        layer_norm_tile(ti, c0, c1, 16, 24)
                for g in gs:
                    ring_release(g)

            _stage('layer')
        if samp:
            store_tok_tile(ys[:, :], 64, 0, 0)
        for j, c0 in enumerate(blocks):
            gj = j + (0 if hh == 0 else 8)
            store_tok_tile(yp[128 * gj:128 * gj + 128, :], 128, c0, tile_of(c0, tts))


def _unit(wcols):
    return np.ascontiguousarray(wcols.reshape(8, 128, 512).transpose(1, 0, 2))


def _rot_cols(w64):
    return np.concatenate([w64[:, 32:64], w64[:, 0:32]], axis=1)


def _build_units(inp):
    w_in = inp['w_in']; w_ap = inp['w_attn_proj']; w_lp = inp['w_lru_proj']; w_out = inp['w_out']
    wa = inp['lru_wa']; wx = inp['lru_wx']; wf1 = inp['w_ffn_in']; wf2 = inp['w_ffn_out']
    wu = np.zeros((NL, NU, 128, 8, 512), np.float32)
    z128 = np.zeros((1024, 128), np.float32)
    for l in range(NL):
        W = w_in[l]
        q = W[:, 0:512]; k = W[:, 512:640]; v = W[:, 640:768]
        xr = W[:, 768:1792]; gt = W[:, 1792:2816]; ga = W[:, 2816:3840]; gl = W[:, 3840:4864]
        qb = [np.concatenate([q[:, 64 * r:64 * r + 64], q[:, 64 * (4 + r):64 * (4 + r) + 64]], axis=1) for r in range(4)]
        qrb = [np.concatenate([_rot_cols(q[:, 64 * r:64 * r + 64]), _rot_cols(q[:, 64 * (4 + r):64 * (4 + r) + 64])], axis=1) for r in range(4)]
        wu[l, U_Q] = _unit(np.concatenate(qb, axis=1))
        wu[l, U_QR] = _unit(np.concatenate(qrb, axis=1))
        kr = np.concatenate([_rot_cols(k[:, 0:64]), _rot_cols(k[:, 64:128])], axis=1)
        wu[l, U_KV] = _unit(np.concatenate([k, kr, v, z128], axis=1))
        for n in range(8):
            wu[l, U_LRU][:, n, 0:128] = wa[l, n]
            wu[l, U_LRU][:, n, 128:256] = wx[l, n]
        for i in range(4):
            wu[l, U_XG[i]] = _unit(np.concatenate([xr[:, 256 * i:256 * i + 128], gt[:, 256 * i:256 * i + 128],
                                                   xr[:, 256 * i + 128:256 * i + 256], gt[:, 256 * i + 128:256 * i + 256]], axis=1))
        for cg in range(2):
            wu[l, U_GA[cg]] = _unit(ga[:, 512 * cg:512 * cg + 512])
            wu[l, U_GL[cg]] = _unit(gl[:, 512 * cg:512 * cg + 512])
            for r in range(4):
                chunk = np.concatenate([w_ap[l, 64 * r:64 * r + 64], w_ap[l, 64 * (4 + r):64 * (4 + r) + 64]], axis=0)
                wu[l, U_AP[cg]][:, r, :] = chunk[:, 512 * cg:512 * cg + 512]
            wu[l, U_LP[cg]] = _unit(w_lp[l][:, 512 * cg:512 * cg + 512])
            wu[l, U_OUT[cg]] = _unit(w_out[l][:, 512 * cg:512 * cg + 512])
        for i in range(11):
            n0, n1 = 2 * i, 2 * i + 1
            wu[l, U_F1[i]] = _unit(np.concatenate([wf1[l][:, 128 * n0:128 * n0 + 128], wf1[l][:, 2816 + 128 * n0:2816 + 128 * n0 + 128],
                                                   wf1[l][:, 128 * n1:128 * n1 + 128], wf1[l][:, 2816 + 128 * n1:2816 + 128 * n1 + 128]], axis=1))
        for og in range(2):
            for kg in range(3):
                for kk in range(8):
                    kc = 8 * kg + kk
                    if kc < 22:
                        wu[l, U_F2[og][kg]][:, kk, :] = wf2[l][128 * kc:128 * kc + 128, 512 * og:512 * og + 512]
    return wu


def _fm(v):
    return np.ascontiguousarray(v.reshape(8, 128).T)


def _build_prm(inp):
    prm = np.zeros((NL, 128, 104), np.float32)
    for l in range(NL):
        prm[l, :, 0:8] = _fm(inp['ln1_g'][l]); prm[l, :, 8:16] = _fm(inp['ln1_b'][l])
        prm[l, :, 16:24] = _fm(inp['ln2_g'][l]); prm[l, :, 24:32] = _fm(inp['ln2_b'][l])
        prm[l, :, 32:40] = _fm(inp['conv_b'][l]); prm[l, :, 40:48] = _fm(inp['lru_ba'][l])
        prm[l, :, 48:56] = _fm(inp['lru_bx'][l]); prm[l, :, 56:64] = _fm(inp['lru_lambda'][l])
        cw = inp['conv_w'][l]
        prm[l, :, 64:96] = cw.reshape(4, 8, 128).transpose(2, 1, 0).reshape(128, 32)
        prm[l, :, 96:104] = np.broadcast_to(inp['attn_sinks'][l][None, :], (128, 8))
    return prm


def _build_consts():
    half = 32
    inv = (np.float32(10000.0) ** (-np.arange(half, dtype=np.float32) / np.float32(half))).astype(np.float32)
    rt = np.zeros((2, 2, 128, NCMAX), np.float32)
    p = np.arange(128)
    fi = p % 32
    sign = np.where((p % 64) < 32, -1.0, 1.0).astype(np.float32)
    for hh in range(2):
        if hh == 0:
            pos = np.concatenate([PAST + (np.arange(64) % 4), np.arange(16), 16 + np.arange(1024)])
        else:
            pos = np.concatenate([16 + 1024 + np.arange(1024), np.zeros(NCMAX - 1024)])
        ang = pos.astype(np.float32)[None, :] * inv[fi][:, None]
        ang = ang.astype(np.float32)
        rt[hh, 0] = np.cos(ang).astype(np.float32)
        rt[hh, 1] = np.sin(ang).astype(np.float32) * sign[:, None]
    cst = np.zeros((128, 2688), np.float32)
    s = np.arange(128)[:, None]; qq = np.arange(128)[None, :]
    md = (s <= qq).astype(np.float32); mp = (s > qq).astype(np.float32)
    mpmeta = np.zeros((128, 128), np.float32)
    mpmeta[0:16] = ((112 + np.arange(16))[:, None] > qq).astype(np.float32)
    cst[:, 0:512] = np.tile(md, (1, 4)); cst[:, 512:1024] = np.tile(mp, (1, 4)); cst[:, 1024:1536] = np.tile(mpmeta, (1, 4))
    col = np.arange(512)
    t_ = col % 4; b_ = (col // 4) % 16
    cst[:, 1536:2048] = (np.arange(128)[:, None] > t_[None, :]).astype(np.float32)
    sp_ = np.arange(64)
    mnm = ((sp_[:, None] // 4) == b_[None, :]) & ((sp_[:, None] % 4) <= t_[None, :])
    cst[0:64, 2048:2560] = mnm.astype(np.float32)
    cst[:, 2560:2688] = np.eye(128, dtype=np.float32)
    return rt, cst


def kernel(**inp):
    inp = {k: np.asarray(v) for k, v in inp.items()}
    n = 8
    nc = build_program()
    wu = _build_units(inp)
    prm = _build_prm(inp)
    rt, cst = _build_consts()
    in_maps = []
    for c in range(n):
        sl = slice(16 * c, 16 * c + 16)
        in_maps.append(dict(
            xp=np.ascontiguousarray(inp['x_prompt'][c]),
            xs=np.ascontiguousarray(inp['x_sample'][sl].reshape(64, D)),
            meta=np.ascontiguousarray(inp['meta_tokens']),
            ck=np.ascontiguousarray(inp['cache_win_k'][:, sl].reshape(NL, 16, 128, 128)),
            cv=np.ascontiguousarray(inp['cache_win_v'][:, sl].reshape(NL, 16, 128, 128)),
            sconv=np.ascontiguousarray(inp['state_conv'][:, sl].reshape(NL, 48, D)),
            slru=np.ascontiguousarray(inp['state_lru'][:, sl]),
            wu=wu, prm=prm, rtab=rt, cst=cst,
        ))
    res = run_bass_kernel_spmd(nc, in_maps, core_ids=list(range(n)))
    R = res.results
    y_prompt = np.stack([R[c]['yp'] for c in range(n)], axis=0)
    y_sample = np.concatenate([R[c]['ys'].reshape(16, 4, D) for c in range(n)], axis=0)
    wkp = np.stack([R[c]['wkp'].reshape(NL, 128, 2, 64) for c in range(n)], axis=1)
    wvp = np.stack([R[c]['wvp'].reshape(NL, 128, 2, 64) for c in range(n)], axis=1)
    cvp = np.stack([R[c]['cvp'] for c in range(n)], axis=1)
    lrp = np.stack([R[c]['lrp'].reshape(NL, D) for c in range(n)], axis=1)
    wks = np.concatenate([R[c]['wks'].reshape(NL, 16, 128, 2, 64) for c in range(n)], axis=1)
    wvs = np.concatenate([R[c]['wvs'].reshape(NL, 16, 128, 2, 64) for c in range(n)], axis=1)
    cvs = np.concatenate([R[c]['cvs'].reshape(NL, 16, 3, D) for c in range(n)], axis=1)
    lrs = np.concatenate([R[c]['lrs'] for c in range(n)], axis=1)
    f = lambda a: np.ascontiguousarray(a, dtype=np.float32)
    return (f(y_prompt), f(y_sample), f(wkp), f(wvp), f(cvp), f(lrp), f(wks), f(wvs), f(cvs), f(lrs))
```

```python
import numpy as np
import concourse.bass as bass
import concourse.mybir as mybir
from concourse.bass_utils import run_bass_kernel_spmd

F32 = mybir.dt.float32
BF16 = mybir.dt.bfloat16
AF = mybir.ActivationFunctionType
ALU = mybir.AluOpType

NL = 4
D = 1024
NSLOT = 5
NU = 35
ALPHA = float((2 * NL) ** 0.25)
LN_EPS = 1e-5
PAST = 8192

U_Q, U_QR, U_KV, U_LRU = 0, 1, 2, 3
U_XG = [4, 5, 6, 7]
U_GA = [8, 12]
U_AP = [9, 13]
U_GL = [10, 14]
U_LP = [11, 15]
U_OUT = [16, 17]
U_F1 = list(range(18, 29))
U_F2 = [[29, 30, 31], [32, 33, 34]]

HALVES = [
    dict(NC=1104, tts=[(0, 80), (80, 592), (592, 1104)], blocks=[80 + 128 * j for j in range(8)], samp=True, seq0=64),
    dict(NC=1024, tts=[(0, 512), (512, 1024)], blocks=[128 * j for j in range(8)], samp=False, seq0=0),
]
NCMAX = 1104


class Sched:
    def __init__(self, nc):
        self.nc = nc
        self.E = {'pe': nc.tensor, 'act': nc.scalar, 'dve': nc.vector, 'pool': nc.gpsimd, 'sp': nc.sync}
        self.comp = ('pe', 'act', 'dve', 'pool')
        self.csem = {e: nc.alloc_semaphore('c_' + e) for e in self.comp}
        self.ccnt = {e: 0 for e in self.comp}
        self.dsem = {}
        self.dcnt = {}
        self.waited = {}
        self.lastw = {}
        self.readers = {}

    def _wait(self, eng, tok):
        name, sem, val, src = tok
        if src == eng and (eng == 'pe' or SELFWAIT[0] == 0 or (SELFWAIT[0] == 2 and eng == 'act')
                           or (SELFWAIT[0] == 3 and eng == 'dve')):
            return
        key = (eng, name)
        if self.waited.get(key, 0) >= val:
            return
        self.waited[key] = val
        self.E[eng].wait_ge(sem, val)

    def _deps(self, eng, reads, writes):
        for k in list(reads) + list(writes):
            t = self.lastw.get(k)
            if t is not None:
                if isinstance(t, list):
                    for t_ in t:
                        self._wait(eng, t_)
                else:
                    self._wait(eng, t)
        for k in writes:
            for t in self.readers.get(k, {}).values():
                self._wait(eng, t)

    def _toks(self, k):
        out = []
        t = self.lastw.get(k)
        if t is not None:
            out.extend(t if isinstance(t, list) else [t])
        out.extend(self.readers.get(k, {}).values())
        return out

    def transfer(self, old_keys, new_keys):
        best = {}
        for k in list(old_keys) + list(new_keys):
            for t in self._toks(k):
                if t[0] not in best or best[t[0]][2] < t[2]:
                    best[t[0]] = t
        for nk in new_keys:
            self.lastw[nk] = list(best.values())
            self.readers[nk] = {}

    def _commit(self, tok, reads, writes):
        for k in writes:
            self.lastw[k] = tok
            self.readers[k] = {}
        for k in reads:
            self.readers.setdefault(k, {})[tok[0]] = tok

    @staticmethod
    def _excl(reads, writes):
        r = [k for k in reads if not (isinstance(k, tuple) and k[0] == 'ps')]
        w = list(writes) + [k for k in reads if isinstance(k, tuple) and k[0] == 'ps']
        return r, w

    def op(self, eng, fn, reads=(), writes=()):
        reads, writes = self._excl(reads, writes)
        self._deps(eng, reads, writes)
        ins = fn(self.E[eng])
        self.ccnt[eng] += 1
        ins.then_inc(self.csem[eng], 1)
        tok = ('c_' + eng, self.csem[eng], self.ccnt[eng], eng)
        self._commit(tok, reads, writes)

    def dma(self, q, out, in_, semkey, reads=(), writes=()):
        self._deps(q, reads, writes)
        if semkey not in self.dsem:
            self.dsem[semkey] = self.nc.alloc_semaphore('d_' + semkey)
            self.dcnt[semkey] = 0
        ins = self.E[q].dma_start(out=out, in_=in_)
        self.dcnt[semkey] += 16
        ins.then_inc(self.dsem[semkey], 16)
        tok = ('d_' + semkey, self.dsem[semkey], self.dcnt[semkey], 'dma')
        self._commit(tok, reads, writes)

    def barrier(self, engines=('pe', 'act', 'dve', 'pool', 'sp')):
        for e in engines:
            for f in self.comp:
                if f != e and self.ccnt[f] > 0:
                    self._wait(e, ('c_' + f, self.csem[f], self.ccnt[f], f))

    def finish(self):
        for k, sem in self.dsem.items():
            self._wait('sp', ('d_' + k, sem, self.dcnt[k], 'dma'))
        for f in self.comp:
            if self.ccnt[f] > 0:
                self._wait('sp', ('c_' + f, self.csem[f], self.ccnt[f], f))


class _StopBuild(Exception):
    pass


STOP = [None]
SELFWAIT = [1]


def _stage(name):
    if STOP[0] is not None and STOP[0] == name:
        raise _StopBuild()


def build_program():
    nc = bass.Bass("TRN2", target_bir_lowering=False)
    S = Sched(nc)
    try:
        _build_body(nc, S)
    except _StopBuild:
        pass
    S.finish()
    return nc


def _build_body(nc, S):

    def din(name, shape):
        return nc.dram_tensor(name, shape, F32, kind="ExternalInput").ap()

    def dout(name, shape):
        return nc.dram_tensor(name, shape, F32, kind="ExternalOutput").ap()

    xp = din("xp", [2048, D]); xs = din("xs", [64, D]); meta = din("meta", [16, D])
    ck = din("ck", [NL, 16, 128, 128]); cv = din("cv", [NL, 16, 128, 128])
    sconv = din("sconv", [NL, 48, D]); slru = din("slru", [NL, 16, D])
    wu = din("wu", [NL, NU, 128, 8, 512])
    prm = din("prm", [NL, 128, 104])
    rtab = din("rtab", [2, 2, 128, NCMAX])
    cst = din("cst", [128, 2688])
    yp = dout("yp", [2048, D]); ys = dout("ys", [64, D])
    wkp = dout("wkp", [NL, 128, 128]); wvp = dout("wvp", [NL, 128, 128])
    cvp = dout("cvp", [NL, 3, D]); lrp = dout("lrp", [NL, 8, 128])
    wks = dout("wks", [NL, 16, 128, 128]); wvs = dout("wvs", [NL, 16, 128, 128])
    cvs = dout("cvs", [NL, 48, D]); lrs = dout("lrs", [NL, 16, D])

    def sb(name, shape, dt):
        return nc.alloc_sbuf_tensor(name, shape, dt)

    xres = sb("xres", [128, 8, NCMAX], F32)
    xT = sb("xT", [128, 8, NCMAX], BF16)
    ring = [sb(f"ring{i}", [128, 8, 512], BF16) for i in range(NSLOT)]
    cosT = sb("cosT", [128, NCMAX], F32)
    sinT = sb("sinT", [128, NCMAX], F32)
    identF = sb("identF", [128, 128], F32)
    mdiag = sb("mdiag", [128, 4, 128], BF16)
    mprev = sb("mprev", [128, 4, 128], BF16)
    mpm = sb("mpm", [128, 4, 128], BF16)
    mc = sb("mc", [128, 512], BF16)
    mn = sb("mn", [128, 512], BF16)
    onesB = sb("onesB", [128, 128], BF16)
    halfF = sb("halfF", [128, 512], F32)
    lruw = sb("lruw", [128, 8, 256], BF16)
    pl = sb("pl", [128, 104], F32)
    es8 = sb("es8", [128, 8], F32)
    nsp8 = sb("nsp8", [128, 8], F32)
    nsp16 = sb("nsp16", [128, 8], F32)
    sptmp = sb("sptmp", [128, 8], F32)
    es_tile = sb("es_tile", [128, 4, 128], F32)
    kcar = sb("kcar", [128, NL, 128], BF16)
    vcar = sb("vcar", [128, NL, 2, 128], BF16)
    ccar = sb("ccar", [128, NL, 8, 3], F32)
    hcar = sb("hcar", [128, NL, 8], F32)
    cs_s = sb("cs_s", [128, 8, 48], F32)
    hs_s = sb("hs_s", [128, 8, 16], F32)
    stg = [sb(f"stg{i}", [128, 1024], F32) for i in range(2)]
    hb16 = sb("hb16", [128, 16], F32)
    hnsp8 = sb("hnsp8", [128, 8], F32)
    last = sb("lastperm", [128, 8], F32)
    A0 = (nc.lookup_mloc(last).addr + 32 + 63) // 64 * 64
    assert A0 + 80512 <= nc.SBUF_PARTITION_SIZE_BYTES, (A0,)

    def at(name, shape, dt, off):
        return nc.alloc_sbuf_tensor_at(name, shape, dt, offset=A0 + off)

    attnT = at("attnT", [128, 4, NCMAX], BF16, 0)
    rec = at("rec", [128, 8, NCMAX], BF16, 8832)
    merged = at("merged", [128, 8, NCMAX], BF16, 26496)
    qT = at("qT", [128, 4, NCMAX], BF16, 8832)
    kT = at("kT", [128, 128 + NCMAX], BF16, 17664)
    vaug = at("vaug", [128, 11, 2, 128], BF16, 20160)
    Pa = [at("Pa0", [128, 512], BF16, 25792), at("Pa1", [128, 512], BF16, 68032)]
    Pb = [at("Pb0", [128, 512], BF16, 26816), at("Pb1", [128, 512], BF16, 69056)]
    dsb = [at("dsb0", [128, 512], F32, 27840), at("dsb1", [128, 512], F32, 70080)]
    rden = [at("rden0", [128, 512], F32, 29888), at("rden1", [128, 512], F32, 72128)]
    kc_raw = at("kc_raw", [128, 16, 128], F32, 31936)
    kcT = at("kcT", [128, 16, 128], BF16, 40128)
    vc_aug = at("vc_aug", [128, 16, 2, 128], BF16, 44224)
    Pc = at("Pc", [128, 512], BF16, 52416)
    Pn = at("Pn", [128, 512], BF16, 53440)
    kf32 = at("kf32", [128, 192], F32, 54464)
    vtmp = at("vtmp", [128, 128], F32, 55232)
    ropeA = [at("ropeA0", [128, 512], F32, 55744), at("ropeA1", [128, 512], F32, 61888)]
    ropeB = [at("ropeB0", [128, 512], F32, 57792), at("ropeB1", [128, 512], F32, 63936)]
    ropeC = [at("ropeC0", [128, 512], F32, 59840), at("ropeC1", [128, 512], F32, 65984)]
    cstF = at("cstF", [128, 2688], F32, 61888)
    identB = at("identB", [128, 128], BF16, 74240)
    LB = 44160
    xrow = [at("xrow0", [128, 3 + 1040], F32, 26496), at("xrow1", [128, 3 + 1040], F32, 30688)]
    hrow = [at("hrow0", [128, 1 + 1040], F32, 34880), at("hrow1", [128, 1 + 1040], F32, 39072)]
    hs = at("hs", [128, 16, 4], F32, 43264)
    h0_s = at("h0_s", [128, 8, 16], F32, 43520)
    asets = []
    for i_ in range(4):
        o_ = LB + 5120 * i_
        asets.append(dict(xc=at(f"xc{i_}", [128, 512], F32, o_), xcb=at(f"xcb{i_}", [128, 512], BF16, o_ + 2048),
                          gg=at(f"gg{i_}", [128, 512], F32, o_ + 3072)))
    bsets = []
    for i_ in range(2):
        o_ = LB + 20480 + 6144 * i_
        bsets.append(dict(ii=at(f"ii{i_}", [128, 512], F32, o_),
                          aa=at(f"aa{i_}", [128, 512], F32, o_ + 2048), a2=at(f"a2{i_}", [128, 512], F32, o_ + 4096)))
    xe_s = at("xe_s", [128, 8, 16, 7], F32, 76928)
    cset = []
    for i_ in range(2):
        o_ = LB + 8192 * i_
        cset.append(dict(sga=at(f"sga{i_}", [128, 512], F32, o_), sgl=at(f"sgl{i_}", [128, 512], F32, o_ + 2048),
                         m1=at(f"m1{i_}", [128, 512], F32, o_ + 4096), m2=at(f"m2{i_}", [128, 512], F32, o_ + 6144)))
    zb = at("zb", [128, 8, 512], BF16, 0)
    sq = at("sq", [128, 8, 512], BF16, 8192)
    mean = at("mean", [128, 512], F32, 16384)
    msq = at("msq", [128, 512], F32, 18432)
    rstd = at("rstd", [128, 512], F32, 20480)
    t1 = [at("t1a", [128, 512], F32, 22528), at("t1b", [128, 512], F32, 24576 - 128)]
    hT = at("hT", [128, 22, NCMAX], BF16, 26496)
    silu_t = [at("silu0", [128, 512], F32, 75072), at("silu1", [128, 512], F32, 77120)]

    ps = [nc.alloc_psum_tensor(f"ps{i}", [128, 512], F32) for i in range(8)]
    mmctr = [0]
    mmpool = [[0, 1, 2, 3]]

    def set_banks(lst):
        mmpool[0] = list(lst)

    def mmbank():
        b = mmpool[0][mmctr[0] % len(mmpool[0])]
        mmctr[0] += 1
        return b

    class Rot:
        def __init__(self, items):
            self.items = items
            self.i = -1

        def nxt(self):
            self.i = (self.i + 1) % len(self.items)
            return self.i, self.items[self.i]

    def mm(out_ap, pairs, reads, writes):
        def fn(pe):
            n = len(pairs)
            ins = None
            for i, (l, r) in enumerate(pairs):
                ins = pe.matmul(out_ap, lhsT=l, rhs=r, start=(i == 0), stop=(i == n - 1))
            return ins
        S.op('pe', fn, reads, writes)

    def tp(out_ap, in_ap, n, reads, writes):
        S.op('pe', lambda pe: pe.transpose(out_ap, in_ap, identF[0:n, 0:n]), reads, writes)

    def act(out, in_, func, reads, writes, bias=None, scale=None):
        kw = {}
        if func == AF.Copy and (bias is not None or scale is not None):
            func = AF.Identity
        if bias is not None:
            kw['bias'] = bias
        if scale is not None:
            kw['scale'] = scale
        S.op('act', lambda e: e.activation(out=out, in_=in_, func=func, **kw), reads, writes)

    def tt(out, in0, in1, op, reads, writes, eng='dve'):
        S.op(eng, lambda e: e.tensor_tensor(out=out, in0=in0, in1=in1, op=op), reads, writes)

    def ts(out, in0, s1, s2, op0, op1, reads, writes, eng='dve'):
        if s2 is None:
            S.op(eng, lambda e: e.tensor_scalar(out=out, in0=in0, scalar1=s1, scalar2=None, op0=op0), reads, writes)
        else:
            S.op(eng, lambda e: e.tensor_scalar(out=out, in0=in0, scalar1=s1, scalar2=s2, op0=op0, op1=op1), reads, writes)

    def stt(out, in0, sc, in1, op0, op1, reads, writes, eng='dve'):
        S.op(eng, lambda e: e.scalar_tensor_tensor(out=out, in0=in0, scalar=sc, in1=in1, op0=op0, op1=op1), reads, writes)

    def cp(out, in_, reads, writes, eng='dve'):
        S.op(eng, lambda e: e.tensor_copy(out=out, in_=in_), reads, writes)

    def ms(ap, val, writes, eng='dve'):
        S.op(eng, lambda e: e.memset(ap, val), (), writes)

    rs = dict(issued=0, released=-1)
    total_units = 2 * NL * NU

    def ring_prefetch():
        while rs['issued'] < total_units and rs['issued'] - NSLOT <= rs['released']:
            g = rs['issued']
            l = (g // NU) % NL
            u = g % NU
            s = g % NSLOT
            for hf in range(2):
                S.dma('pool', ring[s][:, 4 * hf:4 * hf + 4, :], wu[l, u, :, 4 * hf:4 * hf + 4, :], f"ring{s}_{hf}", reads=(), writes=[('slot', s, hf)])
            rs['issued'] += 1

    def ring_release(g):
        rs['released'] = max(rs['released'], g)
        ring_prefetch()

    S.dma('sp', cstF[:, :], cst[:, :], 'cst', (), ['cstF'])
    cp(mdiag[:, :, :], cstF[:, 0:512].rearrange("p (r q) -> p r q", r=4), ['cstF'], ['masks'])
    cp(mprev[:, :, :], cstF[:, 512:1024].rearrange("p (r q) -> p r q", r=4), ['cstF'], ['masks'])
    cp(mpm[:, :, :], cstF[:, 1024:1536].rearrange("p (r q) -> p r q", r=4), ['cstF'], ['masks'])
    cp(mc[:, :], cstF[:, 1536:2048], ['cstF'], ['masks'])
    cp(mn[:, :], cstF[:, 2048:2560], ['cstF'], ['masks'])
    cp(identF[:, :], cstF[:, 2560:2688], ['cstF'], ['ident'])
    ms(onesB[:, :], 1.0, ['ones'])
    ms(halfF[:, :], 0.5, ['ones'])
    ring_prefetch()
    _stage('setup')

    stg_i = [0]

    def next_stg():
        i = stg_i[0] % 2
        stg_i[0] += 1
        return i

    def load_x_tile(src_ap, nrows, col0, ti):
        si = next_stg()
        S.dma('sp', stg[si][0:nrows, :], src_ap, f"stg{si}", (), [('stg', si)])
        for hb in range(2):
            bank = 6 + hb
            for kk in range(4):
                k = 4 * hb + kk
                tp(ps[bank][:, 128 * kk:128 * kk + nrows], stg[si][0:nrows, 128 * k:128 * k + 128], nrows,
                   [('stg', si), 'ident'], [('ps', bank)])
            src = ps[bank][:, :].rearrange("p (a b) -> p a b", a=4)[:, :, 0:nrows]
            act(xres[:, 4 * hb:4 * hb + 4, col0:col0 + nrows], src, AF.Copy, [('ps', bank)], [('xres', ti)])
            cp(xT[:, 4 * hb:4 * hb + 4, col0:col0 + nrows], src, [('ps', bank)], [('xT', ti)])

    def store_tok_tile(dst_ap, nrows, col0, ti):
        si = next_stg()
        for hb in range(2):
            bank = 6 + hb
            for kk in range(4):
                k = 4 * hb + kk
                tp(ps[bank][0:nrows, 128 * kk:128 * kk + 128], xres[:, k, col0:col0 + nrows], 128,
                   [('xres', ti), 'ident'], [('ps', bank)])
            act(stg[si][0:nrows, 512 * hb:512 * hb + 512], ps[bank][0:nrows, :], AF.Copy, [('ps', bank)], [('stg', si)])
        S.dma('sp', dst_ap, stg[si][0:nrows, :], f"stg{si}", [('stg', si)], ())

    def tile_of(c, tts):
        for i, (a, b) in enumerate(tts):
            if a <= c < b:
                return i
        raise ValueError

    t1bufs = [at(f"t1z{i}", [128, 512], F32, 2048 * i) for i in range(4)]
    t1keys = [('t1', i) for i in range(4)]
    t1r = Rot(list(zip(t1bufs, t1keys)))

    ZBK = [('zb', r) for r in range(8)]
    SQK = [('sq', r) for r in range(8)]

    def ln_row_stage(ti, c0, c1, r):
        w = c1 - c0
        cp(zb[:, r, 0:w], xres[:, r, c0:c1], [('xres', ti)], [('zb', r), ('t1', r // 2)])
        act(sq[:, r, 0:w], xres[:, r, c0:c1], AF.Square, [('xres', ti)], [('sq', r)])

    def layer_norm_tile(ti, c0, c1, gcol, bcol):
        w = c1 - c0
        mm(ps[4][:, 0:w], [(onesB[:, :], zb[:, r, 0:w]) for r in range(8)], ZBK + ['ones'], [('ps', 4)])
        mm(ps[5][:, 0:w], [(onesB[:, :], sq[:, r, 0:w]) for r in range(8)], SQK + ['ones'], [('ps', 5)])
        act(mean[:, 0:w], ps[4][:, 0:w], AF.Copy, [('ps', 4)], ['mean'], scale=1.0 / D)
        tt(msq[:, 0:w], mean[:, 0:w], mean[:, 0:w], ALU.mult, ['mean'], ['msq'])
        stt(rstd[:, 0:w], ps[5][:, 0:w], 1.0 / D, msq[:, 0:w], ALU.mult, ALU.subtract, [('ps', 5), 'msq'], ['rstd'])
        ts(rstd[:, 0:w], rstd[:, 0:w], LN_EPS, None, ALU.add, ALU.bypass, ['rstd'], ['rstd'])
        act(rstd[:, 0:w], rstd[:, 0:w], AF.Ln, ['rstd'], ['rstd'])
        act(rstd[:, 0:w], rstd[:, 0:w], AF.Exp, ['rstd'], ['rstd'], scale=-0.5)
        for r in range(8 + 2):
            if r < 8:
                _, (tb, tk) = t1r.nxt()
                tt(tb[:, 0:w], xres[:, r, c0:c1], mean[:, 0:w], ALU.subtract, [('xres', ti), 'mean'], [tk])
                tt(tb[:, 0:w], tb[:, 0:w], rstd[:, 0:w], ALU.mult, [tk, 'rstd'], [tk], eng='pool')
                act(xres[:, r, c0:c1], tb[:, 0:w], AF.Identity, [tk, 'pl'], [('xres', ti, r)],
                    bias=pl[:, bcol + r:bcol + r + 1], scale=pl[:, gcol + r:gcol + r + 1])
            if r - 2 >= 0:
                rr_ = r - 2
                cp(xT[:, rr_, c0:c1], xres[:, rr_, c0:c1], [('xres', ti, rr_)], [('xT', ti)])
        S.transfer([('xres', ti, r) for r in range(8)], [('xres', ti)])

    for hh, H in enumerate(HALVES):
        NC_ = H['NC']; tts = H['tts']; blocks = H['blocks']; samp = H['samp']; seq0 = H['seq0']
        nt = len(tts)
        xT_all = [('xT', i) for i in range(nt)]
        S.barrier()
        S.dma('sp', cosT[:, 0:NC_], rtab[hh, 0, :, 0:NC_], 'rtab', (), ['rtab'])
        S.dma('sp', sinT[:, 0:NC_], rtab[hh, 1, :, 0:NC_], 'rtab', (), ['rtab'])
        _stage('rtab')
        if samp:
            load_x_tile(xs[:, :], 64, 0, 0)
            _stage('x0')
            load_x_tile(meta[:, :], 16, 64, 0)
            _stage('x1')
        for j, c0 in enumerate(blocks):
            gj = j + (0 if hh == 0 else 8)
            load_x_tile(xp[128 * gj:128 * gj + 128, :], 128, c0, tile_of(c0, tts))

        _stage('xload')
        for l in range(NL):
            gbase = (hh * NL + l) * NU
            if l == 0:
                S.barrier()
            else:
                oldk = ([('hT', i) for i in range(nt)] + [('silu', 0), ('silu', 1)] + [('zb', r_) for r_ in range(8)] + [('sq', r_) for r_ in range(8)] + ['mean', 'msq', 'rstd', 't1a'] + [('t1', i) for i in range(4)]
                        + [('merged', i) for i in range(nt)])
                newk = ([('q', i) for i in range(nt)] + [('k', i) for i in range(-1, nt)] + ['vaug', 'kc_raw', 'kcT', 'vc_aug', 'Pc', 'Pn', 'kf32', 'identB']
                        + [(nm, i) for nm in ('ropeA', 'ropeB', 'ropeC', 'Pa', 'Pb', 'dsb', 'rden') for i in range(2)]
                        + [('attnT', 0), ('attnT', 1)])
                S.transfer(oldk, newk)
            S.dma('sp', pl[:, :], prm[l], 'prm', (), ['pl'])
            act(es8[:, :], pl[:, 96:104], AF.Exp, ['pl'], ['es8'])
            for r in range(4):
                ts(es_tile[0:64, r, :], halfF[0:64, 0:128], es8[0:64, 4 + r:5 + r], 2.0, ALU.mult, ALU.mult, ['es8', 'ones'], ['es'])
                ts(es_tile[64:128, r, :], halfF[64:128, 0:128], es8[64:128, r:r + 1], 2.0, ALU.mult, ALU.mult, ['es8', 'ones'], ['es'])
            act(sptmp[:, :], pl[:, 56:64], AF.Exp, ['pl'], ['sp'], scale=-1.0)
            act(sptmp[:, :], sptmp[:, :], AF.Ln, ['sp'], ['sp'], bias=1.0)
            ts(nsp8[:, :], sptmp[:, :], -8.0, None, ALU.mult, ALU.bypass, ['sp'], ['nsp8'])
            ts(nsp16[:, :], sptmp[:, :], -16.0, None, ALU.mult, ALU.bypass, ['sp'], ['nsp8'])
            ts(hnsp8[:, :], sptmp[:, :], -4.0, None, ALU.mult, ALU.bypass, ['sp'], ['nsp8'])
            ts(hb16[:, :], pl[:, 40:56], 0.5, None, ALU.mult, ALU.bypass, ['pl'], ['nsp8'])

            _stage('params')
            if samp:
                S.dma('sp', wks[l, :, 0:124, :], ck[l, :, 4:128, :], 'cpy', (), ())
                S.dma('sp', wvs[l, :, 0:124, :], cv[l, :, 4:128, :], 'cpy', (), ())
                S.dma('sp', kc_raw[:, :, :], ck[l].rearrange("b s c -> s b c"), 'kc', (), ['kc_raw'])
            cp(identB[:, :], identF[:, :], ['ident'], ['identB'])
            ms(vaug[:, :, 0, 64:128], 1.0, ['vaug'])
            ms(vaug[:, :, 1, 0:64], 1.0, ['vaug'])
            if hh == 1:
                cp(kT[:, 0:128], kcar[:, l, :], ['kcar'], [('k', -1)])
                cp(vaug[:, 0, 0, 0:64], vcar[:, l, 0, 0:64], ['vcar'], ['vaug'])
                cp(vaug[:, 0, 1, 64:128], vcar[:, l, 1, 64:128], ['vcar'], ['vaug'])

            gq = gbase + U_Q; gqr = gbase + U_QR; gkv = gbase + U_KV
            sq_ = ring[gq % NSLOT]; sqr_ = ring[gqr % NSLOT]; skv_ = ring[gkv % NSLOT]
            def slotkey(g):
                return (('slot', g % NSLOT, 0), ('slot', g % NSLOT, 1))

            set_banks([0, 1, 2, 3, 4, 5])
            rrot = Rot([0, 1])
            for ti, (c0, c1) in enumerate(tts):
                for r in range(4):
                    w = c1 - c0
                    b1 = mmbank(); b2 = mmbank()
                    ri, _ = rrot.nxt()
                    rA, rB = ropeA[ri], ropeB[ri]
                    mm(ps[b1][:, 0:w], [(sq_[:, k, 128 * r:128 * r + 128], xT[:, k, c0:c1]) for k in range(8)],
                       [*slotkey(gq), ('xT', ti)], [('ps', b1)])
                    mm(ps[b2][:, 0:w], [(sqr_[:, k, 128 * r:128 * r + 128], xT[:, k, c0:c1]) for k in range(8)],
                       [*slotkey(gqr), ('xT', ti)], [('ps', b2)])
                    tt(rA[:, 0:w], ps[b1][:, 0:w], cosT[:, c0:c1], ALU.mult, [('ps', b1), 'rtab'], [('ropeA', ri)])
                    tt(rB[:, 0:w], ps[b2][:, 0:w], sinT[:, c0:c1], ALU.mult, [('ps', b2), 'rtab'], [('ropeB', ri)])
                    tt(qT[:, r, c0:c1], rA[:, 0:w], rB[:, 0:w], ALU.add, [('ropeA', ri), ('ropeB', ri)], [('q', ti)], eng='pool')
            ring_release(gq); ring_release(gqr)
            for ti, (c0, c1) in enumerate(tts):
                w = c1 - c0
                b1 = mmbank(); b2 = mmbank()
                ri, _ = rrot.nxt()
                rA, rB, rC = ropeA[ri], ropeB[ri], ropeC[ri]
                mm(ps[b1][:, 0:w], [(skv_[:, k, 0:128], xT[:, k, c0:c1]) for k in range(8)], [*slotkey(gkv), ('xT', ti)], [('ps', b1)])
                mm(ps[b2][:, 0:w], [(skv_[:, k, 128:256], xT[:, k, c0:c1]) for k in range(8)], [*slotkey(gkv), ('xT', ti)], [('ps', b2)])
                tt(rA[:, 0:w], ps[b1][:, 0:w], cosT[:, c0:c1], ALU.mult, [('ps', b1), 'rtab'], [('ropeA', ri)])
                tt(rB[:, 0:w], ps[b2][:, 0:w], sinT[:, c0:c1], ALU.mult, [('ps', b2), 'rtab'], [('ropeB', ri)])
                tt(rC[:, 0:w], rA[:, 0:w], rB[:, 0:w], ALU.add, [('ropeA', ri), ('ropeB', ri)], [('ropeC', ri)], eng='pool')
                act(kT[:, 128 + c0:128 + c1], rC[:, 0:w], AF.Copy, [('ropeC', ri)], [('k', ti)])
                if samp and ti == 0:
                    act(kf32[:, 0:64], rC[:, 0:64], AF.Copy, [('ropeC', ri)], ['kf32'])
                if hh == 1 and ti == nt - 1:
                    act(kf32[:, 64:192], rC[:, w - 128:w], AF.Copy, [('ropeC', ri)], ['kf32'])
            vtiles = []
            if samp:
                vtiles.append((2, 0, 64, 0)); vtiles.append((1, 64, 16, 0))
            for j, c0 in enumerate(blocks):
                vtiles.append((3 + j, c0, 128, tile_of(c0, tts)))
            for (vi, c0, nrow, ti) in vtiles:
                mm(ps[7][0:nrow, 0:128], [(xT[:, k, c0:c0 + nrow], skv_[:, k, 256:384]) for k in range(8)],
                   [*slotkey(gkv), ('xT', ti)], [('ps', 7)])
                act(vaug[0:nrow, vi, 0, 0:64], ps[7][0:nrow, 0:64], AF.Copy, [('ps', 7)], ['vaug'])
                act(vaug[0:nrow, vi, 1, 64:128], ps[7][0:nrow, 64:128], AF.Copy, [('ps', 7)], ['vaug'])
                if vi == 2:
                    cp(stg[0][0:64, 0:128], ps[7][0:64, 0:128], [('ps', 7)], [('stg', 0)])
                    for b_ in range(16):
                        S.dma('sp', wvs[l, b_, 124:128, :], stg[0][4 * b_:4 * b_ + 4, 0:128], 'stg0', [('stg', 0)], ())
                if hh == 1 and vi == 10:
                    cp(stg[0][:, 0:128], ps[7][:, 0:128], [('ps', 7)], [('stg', 0)])
                    S.dma('sp', wvp[l], stg[0][:, 0:128], 'stg0', [('stg', 0)], ())
            ring_release(gkv)
            if samp:
                tp(ps[7][0:64, 0:128], kf32[:, 0:64], 128, ['kf32', 'ident'], [('ps', 7)])
                cp(stg[1][0:64, 0:128], ps[7][0:64, 0:128], [('ps', 7)], [('stg', 1)])
                for b_ in range(16):
                    S.dma('sp', wks[l, b_, 124:128, :], stg[1][4 * b_:4 * b_ + 4, 0:128], 'stg1', [('stg', 1)], ())
            if hh == 1:
                tp(ps[7][:, 0:128], kf32[:, 64:192], 128, ['kf32', 'ident'], [('ps', 7)])
                cp(stg[1][:, 0:128], ps[7][:, 0:128], [('ps', 7)], [('stg', 1)])
                S.dma('sp', wkp[l], stg[1][:, 0:128], 'stg1', [('stg', 1)], ())

            _stage('qkv')
            kall = [('k', i) for i in range(-1, nt)]
            qall = [('q', i) for i in range(nt)]

            items = []
            if samp:
                items.append((64, 16, 0, 0, None, vaug[:, 1], None))
            for j, c0 in enumerate(blocks):
                if hh == 0 and j == 0:
                    items.append((c0, 128, 16, 128 + 64, vaug[:, 1], vaug[:, 3], mpm))
                else:
                    items.append((c0, 128, 128, 128 + c0 - 128, vaug[:, 3 + j - 1] if j > 0 else vaug[:, 0], vaug[:, 3 + j], mprev))
            work = [(it, g) for it in items for g in range(2)]

            def v3(ap2):
                return ap2.rearrange("p (r q) -> p r q", r=4)

            def attn_s(wi):
                (cq0, Nq, Kp, kp0, vprev, vcur, mask_prev), g = work[wi]
                si_ = wi % 2
                bA, bB = (0, 1) if si_ == 0 else (3, 4)
                Pa_, Pb_ = Pa[si_], Pb[si_]
                kPa, kPb = ('Pa', si_), ('Pb', si_)
                pr = slice(64 * g, 64 * g + 64)
                rhs_q = qT[pr, :, cq0:cq0 + Nq]
                NN = 4 * Nq
                if Kp:
                    mm(v3(ps[bA][0:Kp, 0:NN]), [(kT[pr, kp0:kp0 + Kp], rhs_q), (identB[0:Kp, 0:Kp], mask_prev[0:Kp, :, 0:Nq])],
                       kall + qall + ['masks', 'identB'], [('ps', bA)])
                    act(Pa_[0:Kp, 0:NN], ps[bA][0:Kp, 0:NN], AF.Exp, [('ps', bA)], [kPa], scale=0.125)
                mm(v3(ps[bB][0:Nq, 0:NN]), [(kT[pr, 128 + cq0:128 + cq0 + Nq], rhs_q), (identB[0:Nq, 0:Nq], mdiag[0:Nq, :, 0:Nq])],
                   kall + qall + ['masks', 'identB'], [('ps', bB)])
                act(Pb_[0:Nq, 0:NN], ps[bB][0:Nq, 0:NN], AF.Exp, [('ps', bB)], [kPb], scale=0.125)

            def attn_pv(wi):
                (cq0, Nq, Kp, kp0, vprev, vcur, mask_prev), g = work[wi]
                si_ = wi % 2
                bO = 2 if si_ == 0 else 5
                Pa_, Pb_, dsb_, rden_ = Pa[si_], Pb[si_], dsb[si_], rden[si_]
                kPa, kPb, kds, krd = ('Pa', si_), ('Pb', si_), ('dsb', si_), ('rden', si_)
                pr = slice(64 * g, 64 * g + 64)
                dn = slice(64 - 64 * g, 128 - 64 * g)
                NN = 4 * Nq
                pairs = []
                rd_ = [kPb, 'vaug']
                if Kp:
                    pairs.append((vprev[0:Kp, g, :], Pa_[0:Kp, 0:NN]))
                    rd_.append(kPa)
                pairs.append((vcur[0:Nq, g, :], Pb_[0:Nq, 0:NN]))
                mm(ps[bO][:, 0:NN], pairs, rd_, [('ps', bO)])
                tt(v3(dsb_[dn, 0:NN]), v3(ps[bO][dn, 0:NN]), es_tile[dn, :, 0:Nq], ALU.add, [('ps', bO), 'es'], [kds])
                act(rden_[dn, 0:NN], dsb_[dn, 0:NN], AF.Ln, [kds], [krd])
                act(rden_[dn, 0:NN], rden_[dn, 0:NN], AF.Exp, [krd], [krd], scale=-1.0)
                tt(attnT[pr, :, cq0:cq0 + Nq], v3(ps[bO][pr, 0:NN]), v3(rden_[dn, 0:NN]), ALU.mult,
                   [('ps', bO), krd], [('attnT', g)])

            for wi in range(len(work) + 1):
                if wi < len(work):
                    attn_s(wi)
                if wi - 1 >= 0:
                    attn_pv(wi - 1)

            _stage('attnp')
            if samp:
                for b in range(16):
                    tp(ps[7][:, 0:128], kc_raw[:, b, :], 128, ['kc_raw', 'ident'], [('ps', 7)])
                    act(kcT[:, b, :], ps[7][:, 0:128], AF.Copy, [('ps', 7)], ['kcT'])
                S.dma('sp', kc_raw[:, :, :], cv[l].rearrange("b s c -> s b c"), 'kc', ['kc_raw'], ['kc_raw'])
                ms(vc_aug[:, :, 0, 64:128], 1.0, ['vc_aug'])
                ms(vc_aug[:, :, 1, 0:64], 1.0, ['vc_aug'])
                act(vc_aug[:, :, 0, 0:64], kc_raw[:, :, 0:64], AF.Copy, ['kc_raw'], ['vc_aug'])
                act(vc_aug[:, :, 1, 64:128], kc_raw[:, :, 64:128], AF.Copy, ['kc_raw'], ['vc_aug'])

                def cols(psap, g, b):
                    return psap[:, 256 * g:256 * g + 256].rearrange("p (r b t) -> p r b t", r=4, b=16)[:, :, b, :]
                for g in range(2):
                    pr = slice(64 * g, 64 * g + 64)
                    for b in range(16):
                        S.op('pe', lambda pe, g=g, b=b, pr=pr: pe.matmul(cols(ps[4], g, b), lhsT=kcT[pr, b, :], rhs=qT[pr, :, 4 * b:4 * b + 4], start=True, stop=True),
                             ['kcT'] + qall, [('ps', 4)])
                    mm(ps[5][0:64, 256 * g:256 * g + 256].rearrange("p (r q) -> p r q", r=4),
                       [(kT[pr, 128:192], qT[pr, :, 0:64])], kall + qall, [('ps', 5)])
                act(Pc[:, :], ps[4][:, :], AF.Exp, [('ps', 4)], ['Pc'], scale=0.125)
                tt(Pc[:, :], Pc[:, :], mc[:, :], ALU.mult, ['Pc', 'masks'], ['Pc'])
                act(Pn[0:64, :], ps[5][0:64, :], AF.Exp, [('ps', 5)], ['Pn'], scale=0.125)
                tt(Pn[0:64, :], Pn[0:64, :], mn[0:64, :], ALU.mult, ['Pn', 'masks'], ['Pn'])
                for g in range(2):
                    for b in range(16):
                        def fn(pe, g=g, b=b):
                            pe.matmul(cols(ps[6], g, b), lhsT=vc_aug[:, b, g, :], rhs=cols(Pc, g, b), start=True, stop=False)
                            return pe.matmul(cols(ps[6], g, b), lhsT=vaug[0:64, 2, g, :], rhs=cols(Pn, g, b)[0:64], start=False, stop=True)
                        S.op('pe', fn, ['Pc', 'Pn', 'vc_aug', 'vaug'], [('ps', 6)])
                for g in range(2):
                    pr = slice(64 * g, 64 * g + 64)
                    dn = slice(64 - 64 * g, 128 - 64 * g)
                    cs = slice(256 * g, 256 * g + 256)
                    tt(dsb[0][dn, cs].rearrange("p (r q) -> p r q", r=4), ps[6][dn, cs].rearrange("p (r q) -> p r q", r=4),
                       es_tile[dn, :, 0:64], ALU.add, [('ps', 6), 'es'], [('dsb', 0)])
                    act(rden[0][dn, cs], dsb[0][dn, cs], AF.Ln, [('dsb', 0)], [('rden', 0)])
                    act(rden[0][dn, cs], rden[0][dn, cs], AF.Exp, [('rden', 0)], [('rden', 0)], scale=-1.0)
                    tt(attnT[pr, :, 0:64], ps[6][pr, cs].rearrange("p (r q) -> p r q", r=4),
                       rden[0][dn, cs].rearrange("p (r q) -> p r q", r=4), ALU.mult, [('ps', 6), ('rden', 0)], [('attnT', g)])
            if hh == 0:
                cp(kcar[:, l, :], kT[:, 128 + NC_ - 128:128 + NC_], kall, ['kcar'])
                cp(vcar[:, l, :, :], vaug[:, 10, :, :], ['vaug'], ['vcar'])

            _stage('attns')
            oldA = ([(nm, i) for nm in ('Pa', 'Pb', 'dsb', 'rden', 'ropeA', 'ropeB', 'ropeC') for i in range(2)]
                    + ['kc_raw', 'kcT', 'vc_aug', 'Pc', 'Pn', 'kf32', 'vaug', 'identB']
                    + [('q', i) for i in range(nt)] + [('k', i) for i in range(-1, nt)])
            newB = ([('xrow', i) for i in range(2)] + [('hrow', i) for i in range(2)] + ['hs'] + [('h0_s', n) for n in range(8)]
                    + [(nm, i) for nm in ('xc', 'xcb', 'gg') for i in range(4)] + [(nm, i) for nm in ('ii', 'aa', 'a2') for i in range(2)]
                    + [('xe_s', n) for n in range(8)] + [('rec', i) for i in range(nt)])
            S.transfer(oldA, newB)
            glru = gbase + U_LRU
            for hf in range(2):
                S.dma('pool', lruw[:, 4 * hf:4 * hf + 4, :], wu[l, U_LRU, :, 4 * hf:4 * hf + 4, 0:256], f'lruw{hf}', (), [('lruw', hf)])
            ring_release(glru)
            L = NC_ - seq0
            set_banks([0, 1, 2, 3, 4, 5, 6])
            if samp:
                S.dma('sp', stg[0][0:48, :], sconv[l], 'stg0', (), [('stg', 0)])
                S.dma('sp', stg[1][0:16, :], slru[l], 'stg1', (), [('stg', 1)])
                for n in range(8):
                    tp(ps[7][:, 0:48], stg[0][0:48, 128 * n:128 * n + 128], 48, [('stg', 0), 'ident'], [('ps', 7)])
                    act(xe_s[:, n, :, 0:3], ps[7][:, 0:48].rearrange("p (b j) -> p b j", b=16), AF.Copy, [('ps', 7)], [('xe_s', n)])
                    tp(ps[7][:, 64:80], stg[1][0:16, 128 * n:128 * n + 128], 16, [('stg', 1), 'ident'], [('ps', 7)])
                    act(h0_s[:, n, :], ps[7][:, 64:80], AF.Copy, [('ps', 7)], [('h0_s', n)])
            steps = [(n, ti) for n in range(8) for ti in range(nt)]
            set_banks([0, 1, 2, 3, 4, 5, 6, 7])

            gelu_pending = {}

            def emit_gelu(si):
                if si in gelu_pending:
                    gg_, bgk_, w_, kgg_ = gelu_pending.pop(si)
                    act(gg_[:, 0:w_], ps[bgk_][:, 0:w_], AF.Gelu_apprx_tanh, [('ps', bgk_)], [kgg_])

            def part_a(si):
                n, ti = steps[si]
                c0, c1 = tts[ti]
                w = c1 - c0
                AS = asets[si % 4]
                xc, xcb, gg = AS['xc'], AS['xcb'], AS['gg']
                kxc, kxcb, kgg = ('xc', si % 4), ('xcb', si % 4), ('gg', si % 4)
                xr_, kxr = xrow[n % 2], ('xrow', n % 2)
                hr_, khr = hrow[n % 2], ('hrow', n % 2)
                gx = gbase + U_XG[n // 2]
                sx_ = ring[gx % NSLOT]
                cxr = (n % 2) * 256
                cgt = cxr + 128
                if ti == 0:
                    if samp:
                        ms(xr_[:, 0:3], 0.0, [kxr])
                        ms(hr_[:, 0:1], 0.0, [khr])
                    else:
                        cp(xr_[:, 0:3], ccar[:, l, n, :], [('ccar', n)], [kxr])
                        cp(hr_[:, 0:1], hcar[:, l, n:n + 1], [('hcar', n)], [khr])
                bxk = mmbank(); bgk = mmbank()
                mm(ps[bxk][:, 0:w], [(sx_[:, k, cxr:cxr + 128], xT[:, k, c0:c1]) for k in range(8)], [*slotkey(gx), ('xT', ti)], [('ps', bxk)])
                mm(ps[bgk][:, 0:w], [(sx_[:, k, cgt:cgt + 128], xT[:, k, c0:c1]) for k in range(8)], [*slotkey(gx), ('xT', ti)], [('ps', bgk)])
                sa = max(c0, seq0)
                a = sa - seq0
                Ls = c1 - sa
                o = sa - c0
                if samp and ti == 0:
                    act(xe_s[:, n, :, 3:7], ps[bxk][:, 0:64].rearrange("p (b t) -> p b t", b=16), AF.Copy, [('ps', bxk)], [('xe_s', n)])
                cp(xr_[:, 3 + a:3 + a + Ls], ps[bxk][:, o:w], [('ps', bxk)], [kxr])
                gelu_pending[si] = (gg, bgk, w, kgg)
                if samp and ti == 0:
                    xcs = xc[:, 0:64].rearrange("p (b t) -> p b t", b=16)
                    ts(xcs, xe_s[:, n, :, 0:4], pl[:, 64 + 4 * n:65 + 4 * n], pl[:, 32 + n:33 + n], ALU.mult, ALU.add, [('xe_s', n), 'pl'], [kxc])
                    for jj in range(1, 4):
                        stt(xcs, xe_s[:, n, :, jj:jj + 4], pl[:, 64 + 4 * n + jj:65 + 4 * n + jj], xcs, ALU.mult, ALU.add, [('xe_s', n), 'pl', kxc], [kxc])
                    cp(cs_s[:, n, :].rearrange("p (b j) -> p b j", b=16), xe_s[:, n, :, 4:7], [('xe_s', n)], ['cs_s'], eng='pool')
                ts(xc[:, o:w], xr_[:, a:a + Ls], pl[:, 64 + 4 * n:65 + 4 * n], pl[:, 32 + n:33 + n], ALU.mult, ALU.add, [kxr, 'pl'], [kxc])
                for jj in range(1, 4):
                    stt(xc[:, o:w], xr_[:, a + jj:a + jj + Ls], pl[:, 64 + 4 * n + jj:65 + 4 * n + jj], xc[:, o:w], ALU.mult, ALU.add, [kxr, 'pl', kxc], [kxc])
                cp(xcb[:, 0:w], xc[:, 0:w], [kxc], [kxcb])
                if ti == nt - 1:
                    cp(ccar[:, l, n, :], xr_[:, L:L + 3], [kxr], [('ccar', n)], eng='pool')

            def bcommon(si):
                n, ti = steps[si]
                c0, c1 = tts[ti]
                w = c1 - c0
                AS = asets[si % 4]
                BS = bsets[si % 2]
                sa = max(c0, seq0)
                return dict(n=n, ti=ti, c0=c0, c1=c1, w=w, xc=AS['xc'], xcb=AS['xcb'], gg=AS['gg'],
                            kxc=('xc', si % 4), kxcb=('xcb', si % 4), kgg=('gg', si % 4),
                            ii=BS['ii'], aa=BS['aa'], a2=BS['a2'], kii=('ii', si % 2), kaa=('aa', si % 2), ka2=('a2', si % 2),
                            hr_=hrow[n % 2], khr=('hrow', n % 2), sa=sa, a=sa - seq0, Ls=c1 - sa, o=sa - c0)

            def part_b1(si):
                d = bcommon(si)
                n, w, xc, xcb, ii, aa, a2 = d['n'], d['w'], d['xc'], d['xcb'], d['ii'], d['aa'], d['a2']
                kxc, kxcb, kii, kaa, ka2 = d['kxc'], d['kxcb'], d['kii'], d['kaa'], d['ka2']
                brk = mmbank(); bik = mmbank()
                mm(ps[brk][:, 0:w], [(lruw[:, n, 0:128], xcb[:, 0:w])], [('lruw', n // 4), kxcb], [('ps', brk)])
                mm(ps[bik][:, 0:w], [(lruw[:, n, 128:256], xcb[:, 0:w])], [('lruw', n // 4), kxcb], [('ps', bik)])
                act(aa[:, 0:w], ps[brk][:, 0:w], AF.Tanh, [('ps', brk), 'nsp8'], [kaa], bias=hb16[:, n:n + 1], scale=0.5)
                act(ii[:, 0:w], ps[bik][:, 0:w], AF.Tanh, [('ps', bik), 'nsp8'], [kii], bias=hb16[:, 8 + n:9 + n], scale=0.5)
                emit_gelu(si + LOOK)

            def part_b1b(si):
                d = bcommon(si)
                n, w, xc, xcb, ii, aa, a2 = d['n'], d['w'], d['xc'], d['xcb'], d['ii'], d['aa'], d['a2']
                kxc, kxcb, kii, kaa, ka2 = d['kxc'], d['kxcb'], d['kii'], d['kaa'], d['ka2']
                act(a2[:, 0:w], aa[:, 0:w], AF.Exp, [kaa, 'nsp8'], [ka2], scale=nsp8[:, n:n + 1], bias=nsp8[:, n:n + 1])
                act(aa[:, 0:w], aa[:, 0:w], AF.Exp, [kaa, 'nsp8'], [kaa], scale=hnsp8[:, n:n + 1], bias=hnsp8[:, n:n + 1])
                act(a2[:, 0:w], a2[:, 0:w], AF.Relu, [ka2], [ka2], scale=-0.25, bias=0.25)
                act(a2[:, 0:w], a2[:, 0:w], AF.Ln, [ka2], [ka2], bias=1e-30)
                act(a2[:, 0:w], a2[:, 0:w], AF.Exp, [ka2], [ka2], scale=0.5)
                stt(ii[:, 0:w], ii[:, 0:w], 1.0, xc[:, 0:w], ALU.add, ALU.mult, [kii, kxc], [kii])
                tt(ii[:, 0:w], ii[:, 0:w], a2[:, 0:w], ALU.mult, [kii, ka2], [kii], eng='pool')

            def part_b2(si):
                d = bcommon(si)
                n, ti, c1, w, gg, ii, aa = d['n'], d['ti'], d['c1'], d['w'], d['gg'], d['ii'], d['aa']
                kgg, kii, kaa, hr_, khr = d['kgg'], d['kii'], d['kaa'], d['hr_'], d['khr']
                sa, a, Ls, o = d['sa'], d['a'], d['Ls'], d['o']
                bb = ii
                kbb = kii
                if samp and ti == 0:
                    a3 = aa[:, 0:64].rearrange("p (b t) -> p b t", b=16)
                    b3 = bb[:, 0:64].rearrange("p (b t) -> p b t", b=16)
                    tt(hs[:, :, 0], a3[:, :, 0], h0_s[:, n, :], ALU.mult, [kaa, ('h0_s', n)], ['hs'])
                    tt(hs[:, :, 0], hs[:, :, 0], b3[:, :, 0], ALU.add, ['hs', kbb], ['hs'])
                    for t_ in range(1, 4):
                        tt(hs[:, :, t_], a3[:, :, t_], hs[:, :, t_ - 1], ALU.mult, [kaa, 'hs'], ['hs'])
                        tt(hs[:, :, t_], hs[:, :, t_], b3[:, :, t_], ALU.add, ['hs', kbb], ['hs'])
                    cp(hs_s[:, n, :], hs[:, :, 3], ['hs'], ['hs_s'], eng='pool')
                    tt(rec[:, n, 0:64], hs[:, :, :].rearrange("p b t -> p (b t)"), gg[:, 0:64], ALU.mult, ['hs', kgg], [('rec', ti)])
                S.op('dve', lambda e: e.tensor_tensor_scan(out=hr_[:, 1 + a:1 + a + Ls], data0=aa[:, o:w], data1=bb[:, o:w],
                                                           initial=hr_[:, a:a + 1], op0=ALU.mult, op1=ALU.add),
                     [kaa, kbb, khr], [khr])
                tt(rec[:, n, sa:c1], hr_[:, 1 + a:1 + a + Ls], gg[:, o:w], ALU.mult, [khr, kgg], [('rec', ti)], eng='pool')
                if ti == nt - 1:
                    cp(hcar[:, l, n:n + 1], hr_[:, L:L + 1], [khr], [('hcar', n)], eng='pool')
                    if n % 2 == 1:
                        ring_release(gbase + U_XG[n // 2])

            LOOK = 2
            NS = len(steps)
            assert NS % 2 == 0 and LOOK == 2
            for it in range(NS + 4):
                if it % 2 == 0:
                    for s_ in (it - 4, it - 3):
                        if 0 <= s_ < NS:
                            part_b2(s_)
                if it < NS:
                    part_a(it)
                    if it < LOOK:
                        emit_gelu(it)
                if 0 <= it - 2 < NS:
                    part_b1(it - 2)
                if it % 2 == 1:
                    for s_ in (it - 3, it - 2):
                        if 0 <= s_ < NS:
                            part_b1b(s_)
            assert not gelu_pending
            def emit_b_outputs():
                ccar_all = [('ccar', n) for n in range(8)]
                hcar_all = [('hcar', n) for n in range(8)]
                if hh == 1:
                    for hb in range(2):
                        for kk in range(4):
                            n = 4 * hb + kk
                            tp(ps[6 + hb][0:3, 128 * kk:128 * kk + 128], ccar[:, l, n, :], 128, ccar_all + ['ident'], [('ps', 6 + hb)])
                        cp(stg[0][0:3, 512 * hb:512 * hb + 512], ps[6 + hb][0:3, :], [('ps', 6 + hb)], [('stg', 0)])
                    S.dma('sp', cvp[l], stg[0][0:3, :], 'stg0', [('stg', 0)], ())
                    tp(ps[7][0:8, 0:128], hcar[:, l, :], 128, hcar_all + ['ident'], [('ps', 7)])
                    cp(stg[1][0:8, 0:128], ps[7][0:8, 0:128], [('ps', 7)], [('stg', 1)])
                    S.dma('sp', lrp[l], stg[1][0:8, 0:128], 'stg1', [('stg', 1)], ())
                else:
                    for hb in range(2):
                        for kk in range(4):
                            n = 4 * hb + kk
                            tp(ps[6 + hb][0:48, 128 * kk:128 * kk + 128], cs_s[:, n, :], 128, ['cs_s', 'ident'], [('ps', 6 + hb)])
                        cp(stg[0][0:48, 512 * hb:512 * hb + 512], ps[6 + hb][0:48, :], [('ps', 6 + hb)], [('stg', 0)])
                    S.dma('sp', cvs[l], stg[0][0:48, :], 'stg0', [('stg', 0)], ())
                    for hb in range(2):
                        for kk in range(4):
                            n = 4 * hb + kk
                            tp(ps[6 + hb][0:16, 128 * kk:128 * kk + 128], hs_s[:, n, :], 128, ['hs_s', 'ident'], [('ps', 6 + hb)])
                        cp(stg[1][0:16, 512 * hb:512 * hb + 512], ps[6 + hb][0:16, :], [('ps', 6 + hb)], [('stg', 1)])
                    S.dma('sp', lrs[l], stg[1][0:16, :], 'stg1', [('stg', 1)], ())

            _stage('lru')
            keysB = ([('xrow', i) for i in range(2)] + [('hrow', i) for i in range(2)] + ['hs'] + [('h0_s', n) for n in range(8)]
                     + [(nm, i) for nm in ('xc', 'xcb', 'gg') for i in range(4)] + [(nm, i) for nm in ('ii', 'aa', 'a2') for i in range(2)]
                     + [('xe_s', n) for n in range(8)])
            keysC = [(nm, i) for nm in ('sga', 'sgl', 'm1', 'm2') for i in range(2)]
            keysM = [('merged', i) for i in range(nt)]
            S.transfer(keysB, keysC + keysM)
            set_banks([0, 1, 2, 3, 4, 5, 6, 7])
            crot = Rot(cset)
            for cg in range(2):
                gga = gbase + U_GA[cg]; gap = gbase + U_AP[cg]; ggl = gbase + U_GL[cg]; glp = gbase + U_LP[cg]
                s_ga = ring[gga % NSLOT]; s_ap = ring[gap % NSLOT]; s_gl = ring[ggl % NSLOT]; s_lp = ring[glp % NSLOT]
                for cc in range(4):
                    c = 4 * cg + cc
                    wc = slice(128 * cc, 128 * cc + 128)
                    for ti, (c0, c1) in enumerate(tts):
                        w = c1 - c0
                        ci, CS = crot.nxt()
                        sga, sgl, m1, m2 = CS['sga'], CS['sgl'], CS['m1'], CS['m2']
                        b1 = mmbank()
                        mm(ps[b1][:, 0:w], [(s_ga[:, k, wc], xT[:, k, c0:c1]) for k in range(8)], [*slotkey(gga), ('xT', ti)], [('ps', b1)])
                        act(sga[:, 0:w], ps[b1][:, 0:w], AF.Sigmoid, [('ps', b1)], [('sga', ci)])
                        b2 = mmbank()
                        mm(ps[b2][:, 0:w], [(s_ap[:, r, wc], attnT[:, r, c0:c1]) for r in range(4)], [*slotkey(gap), ('attnT', 0), ('attnT', 1)], [('ps', b2)])
                        tt(m1[:, 0:w], ps[b2][:, 0:w], sga[:, 0:w], ALU.mult, [('ps', b2), ('sga', ci)], [('m1', ci)])
                        b3 = mmbank()
                        mm(ps[b3][:, 0:w], [(s_gl[:, k, wc], xT[:, k, c0:c1]) for k in range(8)], [*slotkey(ggl), ('xT', ti)], [('ps', b3)])
                        act(sgl[:, 0:w], ps[b3][:, 0:w], AF.Sigmoid, [('ps', b3)], [('sgl', ci)])
                        b4 = mmbank()
                        mm(ps[b4][:, 0:w], [(s_lp[:, k, wc], rec[:, k, c0:c1]) for k in range(8)], [*slotkey(glp), ('rec', ti)], [('ps', b4)])
                        tt(m2[:, 0:w], ps[b4][:, 0:w], sgl[:, 0:w], ALU.mult, [('ps', b4), ('sgl', ci)], [('m2', ci)])
                        tt(merged[:, c, c0:c1], m1[:, 0:w], m2[:, 0:w], ALU.add, [('m1', ci), ('m2', ci)], [('merged', ti)], eng='pool')
                ring_release(gga); ring_release(gap); ring_release(ggl); ring_release(glp)
                if cg == 0:
                    emit_b_outputs()

            _stage('merge')
            keysLN = [('zb', r_) for r_ in range(8)] + [('sq', r_) for r_ in range(8)] + ['mean', 'msq', 'rstd', 't1a'] + [('t1', i) for i in range(4)]
            S.transfer([('attnT', 0), ('attnT', 1)] + [('rec', i) for i in range(nt)], keysLN)
            set_banks([0, 1, 2, 3])
            gos = [gbase + U_OUT[0], gbase + U_OUT[1]]
            for ti, (c0, c1) in enumerate(tts):
                w = c1 - c0
                for o_ in range(8):
                    go = gos[o_ // 4]
                    s_o = ring[go % NSLOT]
                    wc = slice(128 * (o_ % 4), 128 * (o_ % 4) + 128)
                    b1 = mmbank()
                    mm(ps[b1][:, 0:w], [(s_o[:, k, wc], merged[:, k, c0:c1]) for k in range(8)], [*slotkey(go), ('merged', ti)], [('ps', b1)])
                    stt(xres[:, o_, c0:c1], xres[:, o_, c0:c1], ALPHA, ps[b1][:, 0:w], ALU.mult, ALU.add, [('ps', b1), ('xres', ti)], [('xres', ti)])
                    ln_row_stage(ti, c0, c1, o_)
                layer_norm_tile(ti, c0, c1, 0, 8)
            ring_release(gos[0]); ring_release(gos[1])

            _stage('ln1')
            S.transfer(keysM + keysC + keysB, [('hT', i) for i in range(nt)] + [('silu', 0), ('silu', 1)])
            set_banks([0, 1, 2, 3, 4, 5, 6, 7])
            srot = Rot(silu_t)
            for i_ in range(11):
                gf = gbase + U_F1[i_]
                s_f = ring[gf % NSLOT]
                for ti, (c0, c1) in enumerate(tts):
                    for e_ in range(2):
                        n = 2 * i_ + e_
                        w = c1 - c0
                        b1 = mmbank(); b2 = mmbank()
                        mm(ps[b1][:, 0:w], [(s_f[:, k, 256 * e_:256 * e_ + 128], xT[:, k, c0:c1]) for k in range(8)], [*slotkey(gf), ('xT', ti)], [('ps', b1)])
                        mm(ps[b2][:, 0:w], [(s_f[:, k, 256 * e_ + 128:256 * e_ + 256], xT[:, k, c0:c1]) for k in range(8)], [*slotkey(gf), ('xT', ti)], [('ps', b2)])
                        sli, sl_ = srot.nxt()
                        act(sl_[:, 0:w], ps[b1][:, 0:w], AF.Silu, [('ps', b1)], [('silu', sli)])
                        tt(hT[:, n, c0:c1], ps[b2][:, 0:w], sl_[:, 0:w], ALU.mult, [('ps', b2), ('silu', sli)], [('hT', ti)])
                ring_release(gf)
            set_banks([0, 1, 2, 3])
            for og in range(2):
                gs = [gbase + u for u in U_F2[og]]
                ss = [ring[g % NSLOT] for g in gs]
                for ti, (c0, c1) in enumerate(tts):
                    w = c1 - c0
                    if og == 1:
                        for r_ in range(4):
                            ln_row_stage(ti, c0, c1, r_)
                    for oo in range(4):
                        o_ = 4 * og + oo
                        wc = slice(128 * oo, 128 * oo + 128)
                        b1 = mmbank()
                        pairs = [(ss[k // 8][:, k % 8, wc], hT[:, k, c0:c1]) for k in range(22)]
                        mm(ps[b1][:, 0:w], pairs, [k_ for g in gs for k_ in slotkey(g)] + [('hT', ti)], [('ps', b1)])
                        stt(xres[:, o_, c0:c1], xres[:, o_, c0:c1], ALPHA, ps[b1][:, 0:w], ALU.mult, ALU.add, [('ps', b1), ('xres', ti)], [('xres', ti)])
                        if og == 1:
                            ln_row_stage(ti, c0, c1, o_)
                    if og == 1:
                        layer_norm_tile(ti, c0, c1, 16, 24)
                for g in gs:
                    ring_release(g)

            _stage('layer')
        if samp:
            store_tok_tile(ys[:, :], 64, 0, 0)
        for j, c0 in enumerate(blocks):
            gj = j + (0 if hh == 0 else 8)
            store_tok_tile(yp[128 * gj:128 * gj + 128, :], 128, c0, tile_of(c0, tts))


def _unit(wcols):
    return np.ascontiguousarray(wcols.reshape(8, 128, 512).transpose(1, 0, 2))


def _rot_cols(w64):
    return np.concatenate([w64[:, 32:64], w64[:, 0:32]], axis=1)


def _build_units(inp):
    w_in = inp['w_in']; w_ap = inp['w_attn_proj']; w_lp = inp['w_lru_proj']; w_out = inp['w_out']
    wa = inp['lru_wa']; wx = inp['lru_wx']; wf1 = inp['w_ffn_in']; wf2 = inp['w_ffn_out']
    wu = np.zeros((NL, NU, 128, 8, 512), np.float32)
    z128 = np.zeros((1024, 128), np.float32)
    for l in range(NL):
        W = w_in[l]
        q = W[:, 0:512]; k = W[:, 512:640]; v = W[:, 640:768]
        xr = W[:, 768:1792]; gt = W[:, 1792:2816]; ga = W[:, 2816:3840]; gl = W[:, 3840:4864]
        qb = [np.concatenate([q[:, 64 * r:64 * r + 64], q[:, 64 * (4 + r):64 * (4 + r) + 64]], axis=1) for r in range(4)]
        qrb = [np.concatenate([_rot_cols(q[:, 64 * r:64 * r + 64]), _rot_cols(q[:, 64 * (4 + r):64 * (4 + r) + 64])], axis=1) for r in range(4)]
        wu[l, U_Q] = _unit(np.concatenate(qb, axis=1))
        wu[l, U_QR] = _unit(np.concatenate(qrb, axis=1))
        kr = np.concatenate([_rot_cols(k[:, 0:64]), _rot_cols(k[:, 64:128])], axis=1)
        wu[l, U_KV] = _unit(np.concatenate([k, kr, v, z128], axis=1))
        for n in range(8):
            wu[l, U_LRU][:, n, 0:128] = wa[l, n]
            wu[l, U_LRU][:, n, 128:256] = wx[l, n]
        for i in range(4):
            wu[l, U_XG[i]] = _unit(np.concatenate([xr[:, 256 * i:256 * i + 128], gt[:, 256 * i:256 * i + 128],
                                                   xr[:, 256 * i + 128:256 * i + 256], gt[:, 256 * i + 128:256 * i + 256]], axis=1))
        for cg in range(2):
            wu[l, U_GA[cg]] = _unit(ga[:, 512 * cg:512 * cg + 512])
            wu[l, U_GL[cg]] = _unit(gl[:, 512 * cg:512 * cg + 512])
            for r in range(4):
                chunk = np.concatenate([w_ap[l, 64 * r:64 * r + 64], w_ap[l, 64 * (4 + r):64 * (4 + r) + 64]], axis=0)
                wu[l, U_AP[cg]][:, r, :] = chunk[:, 512 * cg:512 * cg + 512]
            wu[l, U_LP[cg]] = _unit(w_lp[l][:, 512 * cg:512 * cg + 512])
            wu[l, U_OUT[cg]] = _unit(w_out[l][:, 512 * cg:512 * cg + 512])
        for i in range(11):
            n0, n1 = 2 * i, 2 * i + 1
            wu[l, U_F1[i]] = _unit(np.concatenate([wf1[l][:, 128 * n0:128 * n0 + 128], wf1[l][:, 2816 + 128 * n0:2816 + 128 * n0 + 128],
                                                   wf1[l][:, 128 * n1:128 * n1 + 128], wf1[l][:, 2816 + 128 * n1:2816 + 128 * n1 + 128]], axis=1))
        for og in range(2):
            for kg in range(3):
                for kk in range(8):
                    kc = 8 * kg + kk
                    if kc < 22:
                        wu[l, U_F2[og][kg]][:, kk, :] = wf2[l][128 * kc:128 * kc + 128, 512 * og:512 * og + 512]
    return wu


def _fm(v):
    return np.ascontiguousarray(v.reshape(8, 128).T)


def _build_prm(inp):
    prm = np.zeros((NL, 128, 104), np.float32)
    for l in range(NL):
        prm[l, :, 0:8] = _fm(inp['ln1_g'][l]); prm[l, :, 8:16] = _fm(inp['ln1_b'][l])
        prm[l, :, 16:24] = _fm(inp['ln2_g'][l]); prm[l, :, 24:32] = _fm(inp['ln2_b'][l])
        prm[l, :, 32:40] = _fm(inp['conv_b'][l]); prm[l, :, 40:48] = _fm(inp['lru_ba'][l])
        prm[l, :, 48:56] = _fm(inp['lru_bx'][l]); prm[l, :, 56:64] = _fm(inp['lru_lambda'][l])
        cw = inp['conv_w'][l]
        prm[l, :, 64:96] = cw.reshape(4, 8, 128).transpose(2, 1, 0).reshape(128, 32)
        prm[l, :, 96:104] = np.broadcast_to(inp['attn_sinks'][l][None, :], (128, 8))
    return prm


def _build_consts():
    half = 32
    inv = (np.float32(10000.0) ** (-np.arange(half, dtype=np.float32) / np.float32(half))).astype(np.float32)
    rt = np.zeros((2, 2, 128, NCMAX), np.float32)
    p = np.arange(128)
    fi = p % 32
    sign = np.where((p % 64) < 32, -1.0, 1.0).astype(np.float32)
    for hh in range(2):
        if hh == 0:
            pos = np.concatenate([PAST + (np.arange(64) % 4), np.arange(16), 16 + np.arange(1024)])
        else:
            pos = np.concatenate([16 + 1024 + np.arange(1024), np.zeros(NCMAX - 1024)])
        ang = pos.astype(np.float32)[None, :] * inv[fi][:, None]
        ang = ang.astype(np.float32)
        rt[hh, 0] = np.cos(ang).astype(np.float32)
        rt[hh, 1] = np.sin(ang).astype(np.float32) * sign[:, None]
    cst = np.zeros((128, 2688), np.float32)
    s = np.arange(128)[:, None]; qq = np.arange(128)[None, :]
    md = (s <= qq).astype(np.float32); mp = (s > qq).astype(np.float32)
    mpmeta = np.zeros((128, 128), np.float32)
    mpmeta[0:16] = ((112 + np.arange(16))[:, None] > qq).astype(np.float32)
    NEG = np.float32(-30000.0)
    cst[:, 0:512] = np.tile((1 - md) * NEG, (1, 4)); cst[:, 512:1024] = np.tile((1 - mp) * NEG, (1, 4)); cst[:, 1024:1536] = np.tile((1 - mpmeta) * NEG, (1, 4))
    col = np.arange(512)
    t_ = col % 4; b_ = (col // 4) % 16
    cst[:, 1536:2048] = (np.arange(128)[:, None] > t_[None, :]).astype(np.float32)
    sp_ = np.arange(64)
    mnm = ((sp_[:, None] // 4) == b_[None, :]) & ((sp_[:, None] % 4) <= t_[None, :])
    cst[0:64, 2048:2560] = mnm.astype(np.float32)
    cst[:, 2560:2688] = np.eye(128, dtype=np.float32)
    return rt, cst


def kernel(**inp):
    inp = {k: np.asarray(v) for k, v in inp.items()}
    n = 8
    nc = build_program()
    wu = _build_units(inp)
    prm = _build_prm(inp)
    rt, cst = _build_consts()
    in_maps = []
    for c in range(n):
        sl = slice(16 * c, 16 * c + 16)
        in_maps.append(dict(
            xp=np.ascontiguousarray(inp['x_prompt'][c]),
            xs=np.ascontiguousarray(inp['x_sample'][sl].reshape(64, D)),
            meta=np.ascontiguousarray(inp['meta_tokens']),
            ck=np.ascontiguousarray(inp['cache_win_k'][:, sl].reshape(NL, 16, 128, 128)),
            cv=np.ascontiguousarray(inp['cache_win_v'][:, sl].reshape(NL, 16, 128, 128)),
            sconv=np.ascontiguousarray(inp['state_conv'][:, sl].reshape(NL, 48, D)),
            slru=np.ascontiguousarray(inp['state_lru'][:, sl]),
            wu=wu, prm=prm, rtab=rt, cst=cst,
        ))
    res = run_bass_kernel_spmd(nc, in_maps, core_ids=list(range(n)))
    R = res.results
    y_prompt = np.stack([R[c]['yp'] for c in range(n)], axis=0)
    y_sample = np.concatenate([R[c]['ys'].reshape(16, 4, D) for c in range(n)], axis=0)
    wkp = np.stack([R[c]['wkp'].reshape(NL, 128, 2, 64) for c in range(n)], axis=1)
    wvp = np.stack([R[c]['wvp'].reshape(NL, 128, 2, 64) for c in range(n)], axis=1)
    cvp = np.stack([R[c]['cvp'] for c in range(n)], axis=1)
    lrp = np.stack([R[c]['lrp'].reshape(NL, D) for c in range(n)], axis=1)
    wks = np.concatenate([R[c]['wks'].reshape(NL, 16, 128, 2, 64) for c in range(n)], axis=1)
    wvs = np.concatenate([R[c]['wvs'].reshape(NL, 16, 128, 2, 64) for c in range(n)], axis=1)
    cvs = np.concatenate([R[c]['cvs'].reshape(NL, 16, 3, D) for c in range(n)], axis=1)
    lrs = np.concatenate([R[c]['lrs'] for c in range(n)], axis=1)
    f = lambda a: np.ascontiguousarray(a, dtype=np.float32)
    return (f(y_prompt), f(y_sample), f(wkp), f(wvp), f(cvp), f(lrp), f(wks), f(wvs), f(cvs), f(lrs))
```
